# Optimizing a Trainium2 kernel written in Bass

```python
import jax, jax.numpy as jnp
from jax import lax
import numpy as np

D_MODEL = 1024
BATCH = 16
SEQ = 2048
DEPTH = 4
DEC_BATCH = 2
DEC_SEQ = 16384
PAST_LEN = 128

HEAD_DIM = 64
N_GROUPS = 4
MIX_W = D_MODEL
GROUP_W = MIX_W // N_GROUPS
GROUP_HEADS = GROUP_W // HEAD_DIM
A_HEADS = GROUP_HEADS
A_CONFIGS = ((128, 1), (512, 4), (2048, 16))
A_BLOCK = 64
B_HEADS = GROUP_HEADS
B_KV_HEADS = 2
B_HALF_WINDOW = 128
B_BLOCK = 128
C_HEADS = GROUP_HEADS
C_Q_RANK = 256
C_KV_RANK = 128
C_NOPE = 64
C_ROPE = 32
C_V = HEAD_DIM
C_QBLOCK = 128
ROPE_THETA = 10000.0
D_HEADS = GROUP_HEADS
GRID_W = 64
NA_KH = 8
NA_KW = 16
D_FF = 4 * D_MODEL
EPS = 1e-5
NEG_INF = -1e30
N_ALIBI_HEADS = A_HEADS + B_HEADS
IN_SECTIONS = (GROUP_W, GROUP_W, GROUP_W,
               GROUP_W, B_KV_HEADS * HEAD_DIM, B_KV_HEADS * HEAD_DIM,
               C_Q_RANK, C_KV_RANK, C_ROPE,
               GROUP_W, GROUP_W, GROUP_W)
IN_W = sum(IN_SECTIONS)

kernel_name = 'hybrid_parallel_group_encoder'


def rmsnorm(x, g):
    xf = x.astype(jnp.float32)
    y = xf * lax.rsqrt(jnp.mean(xf * xf, axis=-1, keepdims=True) + EPS)
    return (y * g.astype(jnp.float32)).astype(x.dtype)


def split_points():
    return np.cumsum(np.array(IN_SECTIONS))[:-1].tolist()


def alibi_slopes():
    return jnp.exp2(-8.0 * jnp.arange(1, N_ALIBI_HEADS + 1, dtype=jnp.float32) / N_ALIBI_HEADS)


def rope_tables(S):
    inv_freq = 1.0 / (ROPE_THETA ** (jnp.arange(0, C_ROPE, 2, dtype=jnp.float32) / C_ROPE))
    ang = jnp.arange(S, dtype=jnp.float32)[:, None] * inv_freq[None, :]
    return jnp.cos(ang), jnp.sin(ang)


def rotate(x, cos, sin):
    x1, x2 = jnp.split(x.astype(jnp.float32), 2, axis=-1)
    return jnp.concatenate([x1 * cos - x2 * sin, x1 * sin + x2 * cos], axis=-1).astype(x.dtype)


def banded_attention(q, k, v, half_window, block, slopes, dist_scale, sink=None):
    N, L, H, dh = q.shape
    Hkv = k.shape[2]
    G = H // Hkv
    nb = -(-L // block)
    Lp = nb * block
    pad = Lp - L
    qb = jnp.pad(q, ((0, 0), (0, pad), (0, 0), (0, 0))).reshape(N, nb, block, Hkv, G, dh)

    def windows(a):
        ap = jnp.pad(a, ((0, 0), (block, pad + block), (0, 0), (0, 0))).reshape(N, nb + 2, block, Hkv, dh)
        return jnp.concatenate([ap[:, :-2], ap[:, 1:-1], ap[:, 2:]], axis=2)

    kw, vw = windows(k), windows(v)
    qi = jnp.arange(block)
    kj = jnp.arange(3 * block)
    rel = kj[None, :] - block - qi[:, None]
    kpos = (jnp.arange(nb)[:, None] - 1) * block + kj[None, :]
    mask = (jnp.abs(rel) <= half_window)[None] & ((kpos >= 0) & (kpos < L))[:, None, :]
    s = jnp.einsum('nbqkgd,nbskd->nbkgqs', qb, kw, preferred_element_type=jnp.float32) * (dh ** -0.5)
    dist = jnp.abs(rel).astype(jnp.float32) * dist_scale
    s = s - slopes.astype(jnp.float32).reshape(Hkv, G)[:, :, None, None] * dist[None, None]
    s = jnp.where(mask[None, :, None, None], s, NEG_INF)
    m = jnp.max(s, axis=-1, keepdims=True)
    if sink is not None:
        sink_b = sink.astype(jnp.float32).reshape(Hkv, G)[:, :, None, None]
        m = jnp.maximum(m, sink_b)
    p = jnp.exp(s - m)
    den = jnp.sum(p, axis=-1, keepdims=True)
    if sink is not None:
        den = den + jnp.exp(sink_b - m)
    p = p / den
    o = jnp.einsum('nbkgqs,nbskd->nbqkgd', p.astype(v.dtype), vw, preferred_element_type=jnp.float32)
    o = o.reshape(N, Lp, H, dh)[:, :L].astype(q.dtype)
    lse = (m + jnp.log(den))[..., 0].transpose(0, 1, 4, 2, 3).reshape(N, Lp, H)[:, :L]
    return o, lse


def dilated_attention(q, k, v, slopes):
    B, S, H, dh = q.shape
    outs, lses = [], []
    for window, dil in A_CONFIGS:
        L = S // dil

        def to_strided(a):
            return a.reshape(B, L, dil, H, dh).transpose(0, 2, 1, 3, 4).reshape(B * dil, L, H, dh)

        o, lse = banded_attention(to_strided(q), to_strided(k), to_strided(v),
                                  window // (2 * dil), A_BLOCK, slopes, float(dil))
        outs.append(o.reshape(B, dil, L, H, dh).transpose(0, 2, 1, 3, 4).reshape(B, S, H, dh))
        lses.append(lse.reshape(B, dil, L, H).transpose(0, 2, 1, 3).reshape(B, S, H))
    w = jax.nn.softmax(jnp.stack(lses, axis=0), axis=0)
    o = jnp.sum(w[..., None] * jnp.stack(outs, axis=0).astype(jnp.float32), axis=0)
    return o.astype(q.dtype)


def latent_attention(c_q, c_kv, k_rope, q_norm, w_q_up, kv_norm, w_kv_up, cos, sin):
    B, S, _ = c_q.shape
    q = jnp.einsum('bsr,re->bse', rmsnorm(c_q, q_norm), w_q_up).reshape(B, S, C_HEADS, C_NOPE + C_ROPE)
    q_nope = q[..., :C_NOPE]
    q_pe = rotate(q[..., C_NOPE:], cos[:, None], sin[:, None])
    kv = jnp.einsum('bsr,re->bse', rmsnorm(c_kv, kv_norm), w_kv_up).reshape(B, S, C_HEADS, C_NOPE + C_V)
    k_nope, v = kv[..., :C_NOPE], kv[..., C_NOPE:]
    k_pe = rotate(k_rope, cos, sin)
    scale = (C_NOPE + C_ROPE) ** -0.5
    nq = S // C_QBLOCK
    qn_blocks = q_nope.reshape(B, nq, C_QBLOCK, C_HEADS, C_NOPE).transpose(1, 0, 2, 3, 4)
    qp_blocks = q_pe.reshape(B, nq, C_QBLOCK, C_HEADS, C_ROPE).transpose(1, 0, 2, 3, 4)

    def attend(blk):
        qn_b, qp_b = blk
        s = (jnp.einsum('bqhd,bshd->bhqs', qn_b, k_nope, preferred_element_type=jnp.float32)
             + jnp.einsum('bqhr,bsr->bhqs', qp_b, k_pe, preferred_element_type=jnp.float32)) * scale
        p = jax.nn.softmax(s, axis=-1)
        return jnp.einsum('bhqs,bshd->bqhd', p.astype(v.dtype), v,
                          preferred_element_type=jnp.float32).astype(v.dtype)

    o = lax.map(attend, (qn_blocks, qp_blocks))
    return o.transpose(1, 0, 2, 3, 4).reshape(B, S, C_HEADS, C_V)


def neighborhood_attention(q, k, v, rpb):
    B, S, H, dh = q.shape
    rows = S // GRID_W
    kh = min(NA_KH, rows)
    r = jnp.arange(rows)
    row_start = jnp.clip(r - kh // 2, 0, rows - kh)
    row_idx = row_start[:, None] + jnp.arange(kh)[None, :]
    c = jnp.arange(GRID_W)
    col_start = jnp.clip(c - NA_KW // 2, 0, GRID_W - NA_KW)
    col_mask = (c[None, :] >= col_start[:, None]) & (c[None, :] < col_start[:, None] + NA_KW)
    qg = q.reshape(B, rows, GRID_W, H, dh)
    kg = k.reshape(B, rows, GRID_W, H, dh)[:, row_idx]
    vg = v.reshape(B, rows, GRID_W, H, dh)[:, row_idx]
    dr = row_idx - r[:, None] + (NA_KH - 1)
    dc = jnp.clip(c[None, :] - c[:, None], -(NA_KW - 1), NA_KW - 1) + (NA_KW - 1)
    bias = rpb.astype(jnp.float32)[:, dr][:, :, :, dc]
    bias = bias.transpose(0, 1, 3, 2, 4)
    bias = jnp.where(col_mask[None, None, :, None, :], bias, NEG_INF)
    s = jnp.einsum('brchd,briwhd->bhrciw', qg, kg, preferred_element_type=jnp.float32) * (dh ** -0.5)
    s = s + bias[None]
    p = jax.nn.softmax(s.reshape(B, H, rows, GRID_W, kh * GRID_W), axis=-1)
    p = p.reshape(B, H, rows, GRID_W, kh, GRID_W)
    o = jnp.einsum('bhrciw,briwhd->brchd', p.astype(v.dtype), vg, preferred_element_type=jnp.float32)
    return o.reshape(B, S, H, dh).astype(q.dtype)


def encoder_layer(x, lp, cos, sin, slopes_a, slopes_b):
    (norm_attn, w_in, mla_q_norm, w_q_up, mla_kv_norm, w_kv_up, sink_logits, na_rpb,
     group_norm, w_out, norm_mlp, w_mlp_up, w_mlp_down) = lp
    B, S, _ = x.shape
    h = rmsnorm(x, norm_attn)
    proj = jnp.einsum('bsd,de->bse', h, w_in)
    a_q, a_k, a_v, b_q, b_k, b_v, c_q, c_kv, c_kr, d_q, d_k, d_v = jnp.split(proj, split_points(), axis=-1)

    def heads(t):
        return t.reshape(B, S, -1, HEAD_DIM)

    o_a = dilated_attention(heads(a_q), heads(a_k), heads(a_v), slopes_a)
    o_b, _ = banded_attention(heads(b_q), heads(b_k), heads(b_v), B_HALF_WINDOW, B_BLOCK,
                              slopes_b, 1.0, sink_logits)
    o_c = latent_attention(c_q, c_kv, c_kr, mla_q_norm, w_q_up, mla_kv_norm, w_kv_up, cos, sin)
    o_d = neighborhood_attention(heads(d_q), heads(d_k), heads(d_v), na_rpb)
    o = jnp.concatenate([o_a.reshape(B, S, GROUP_W), o_b.reshape(B, S, GROUP_W),
                         o_c.reshape(B, S, GROUP_W), o_d.reshape(B, S, GROUP_W)], axis=-1)
    o = rmsnorm(o.reshape(B, S, N_GROUPS, GROUP_W), group_norm).reshape(B, S, MIX_W)
    x = x + jnp.einsum('bse,ed->bsd', o, w_out)
    h = rmsnorm(x, norm_mlp)
    u = jnp.square(jax.nn.relu(jnp.einsum('bsd,df->bsf', h, w_mlp_up)))
    return x + jnp.einsum('bsf,fd->bsd', u, w_mlp_down)


def encoder_trunk(x, params, norm_final):
    S = x.shape[1]
    cos, sin = rope_tables(S)
    slopes = alibi_slopes()
    slopes_a, slopes_b = slopes[1::2], slopes[0::2]
    for layer in range(DEPTH):
        x = encoder_layer(x, tuple(p[layer] for p in params), cos, sin, slopes_a, slopes_b)
    return rmsnorm(x, norm_final)


def setup_inputs(seed: int = 0) -> dict:
    key = jax.random.key(seed)
    ks = jax.random.split(key, 16)

    def normal(k, shape, scale):
        return scale * jax.random.normal(k, shape, jnp.float32)

    def gain(k, shape):
        return 1.0 + 0.05 * jax.random.normal(k, shape, jnp.float32)

    return {
        'x_prompt': normal(ks[0], (BATCH, SEQ, D_MODEL), 1.0),
        'x_sample': normal(ks[1], (DEC_BATCH, DEC_SEQ, D_MODEL), 1.0),
        'norm_attn': gain(ks[2], (DEPTH, D_MODEL)),
        'w_in': normal(ks[3], (DEPTH, D_MODEL, IN_W), D_MODEL ** -0.5),
        'mla_q_norm': gain(ks[4], (DEPTH, C_Q_RANK)),
        'w_q_up': normal(ks[5], (DEPTH, C_Q_RANK, C_HEADS * (C_NOPE + C_ROPE)), C_Q_RANK ** -0.5),
        'mla_kv_norm': gain(ks[6], (DEPTH, C_KV_RANK)),
        'w_kv_up': normal(ks[7], (DEPTH, C_KV_RANK, C_HEADS * (C_NOPE + C_V)), C_KV_RANK ** -0.5),
        'sink_logits': normal(ks[8], (DEPTH, B_HEADS), 0.5),
        'na_rpb': normal(ks[9], (DEPTH, D_HEADS, 2 * NA_KH - 1, 2 * NA_KW - 1), 0.1),
        'group_norm': gain(ks[10], (DEPTH, N_GROUPS, GROUP_W)),
        'w_out': normal(ks[11], (DEPTH, MIX_W, D_MODEL), MIX_W ** -0.5),
        'norm_mlp': gain(ks[12], (DEPTH, D_MODEL)),
        'w_mlp_up': normal(ks[13], (DEPTH, D_MODEL, D_FF), D_MODEL ** -0.5),
        'w_mlp_down': normal(ks[14], (DEPTH, D_FF, D_MODEL), D_FF ** -0.5),
        'norm_final': gain(ks[15], (D_MODEL,)),
    }


def reference(x_prompt, x_sample, norm_attn, w_in, mla_q_norm, w_q_up, mla_kv_norm, w_kv_up,
              sink_logits, na_rpb, group_norm, w_out, norm_mlp, w_mlp_up, w_mlp_down, norm_final):
    params = (norm_attn, w_in, mla_q_norm, w_q_up, mla_kv_norm, w_kv_up, sink_logits, na_rpb,
              group_norm, w_out, norm_mlp, w_mlp_up, w_mlp_down)
    y_prompt = encoder_trunk(x_prompt, params, norm_final)
    y_sample = encoder_trunk(x_sample, params, norm_final)
    return (y_prompt, y_sample)
```

```python
import contextlib
import numpy as np
import concourse.bass as bass
import concourse.mybir as mybir
from concourse.bass_utils import run_bass_kernel_spmd

F32 = mybir.dt.float32
BF16 = mybir.dt.bfloat16
AF = mybir.ActivationFunctionType
ALU = mybir.AluOpType

D_MODEL = 1024
DEPTH = 4
D_FF = 4096
EPS = 1e-5
N_CORES = 8
FM_COLS = 1408
TM0 = 1408
KR0 = 2432
WIN_COLS = 2624
NPCOL = 40
PC_NATT, PC_NMLP, PC_GN, PC_QN, PC_KVN, PC_SINK = 0, 8, 16, 24, 26, 27
MA_LEN, MB_LEN, MD_ARR = 2944, 1152, 1408
MD_LEN = MD_ARR + 12 * 512
MASK_MAX = MD_LEN
ROW_AQ, ROW_AK, ROW_BQ, ROW_BK, ROW_DQ, ROW_DK = 0, 256, 512, 768, 896, 1152
NVH = 14


class _Stop(Exception):
    pass


class Res:
    __slots__ = ("w", "r")

    def __init__(self):
        self.w = {}
        self.r = {}


class Sched:
    def __init__(self, nc, stack, n_dma_sems=12):
        self.nc = nc
        self.eng = {"pe": nc.tensor, "act": nc.scalar, "dve": nc.vector, "pool": nc.gpsimd, "sp": nc.sync}
        self.names = list(self.eng.keys())
        self.sems = {}
        for n in self.names:
            self.sems[n] = stack.enter_context(nc.semaphore("s_" + n))
        self.cnt = {n: 0 for n in self.names}
        self.dq = {}
        for q in ("sp", "pool"):
            lst = []
            for i in range(n_dma_sems):
                key = "d_%s_%d" % (q, i)
                self.sems[key] = stack.enter_context(nc.semaphore(key))
                self.cnt[key] = 0
                lst.append(key)
            self.dq[q] = [lst, 0]
        self.waited = {n: {} for n in self.names}
        self.pending = {n: [] for n in self.names}
        self.nops = 0

    def _needs(self, X, reads, writes):
        need = {}
        for r in reads:
            for k, v in r.w.items():
                if need.get(k, 0) < v:
                    need[k] = v
        for w in writes:
            for k, v in w.w.items():
                if need.get(k, 0) < v:
                    need[k] = v
            for k, v in w.r.items():
                if need.get(k, 0) < v:
                    need[k] = v
        return need

    def _emit_waits(self, X, need):
        wd = self.waited[X]
        e = self.eng[X]
        for k, v in need.items():
            if k == X and X == "pe":
                continue
            if wd.get(k, 0) < v:
                e.wait_ge(self.sems[k], v)
                wd[k] = v

    def _commit(self, ev_key, ev_val, reads, writes):
        for r in reads:
            if r.r.get(ev_key, 0) < ev_val:
                r.r[ev_key] = ev_val
        for w in writes:
            w.w = {ev_key: ev_val}
            w.r = {}

    def op(self, X, fn, reads=(), writes=(), signal=True):
        need = self._needs(X, reads, writes)
        self._emit_waits(X, need)
        inst = fn()
        self.nops += 1
        if signal:
            self.cnt[X] += 1
            inst.then_inc(self.sems[X], 1)
            c = self.cnt[X]
            for (rr, ww) in self.pending[X]:
                self._commit(X, c, rr, ww)
            self.pending[X] = []
            self._commit(X, c, reads, writes)
        else:
            self.pending[X].append((reads, writes))
        return inst

    def dma(self, q, out, in_, reads=(), writes=(), **kw):
        assert not self.pending[q]
        need = self._needs(q, reads, writes)
        lst, idx = self.dq[q]
        key = lst[idx % len(lst)]
        self.dq[q][1] = idx + 1
        if self.cnt[key] > 0:
            need[key] = max(need.get(key, 0), self.cnt[key])
        self._emit_waits(q, need)
        inst = self.eng[q].dma_start(out=out, in_=in_, **kw)
        self.nops += 1
        self.cnt[key] += 16
        inst.then_inc(self.sems[key], 16)
        self._commit(key, self.cnt[key], reads, writes)
        return inst

    def barrier(self):
        for n in self.names:
            assert not self.pending[n], n
        allev = {k: v for k, v in self.cnt.items() if v > 0}
        for n in self.names:
            need = {k: v for k, v in allev.items() if k != n}
            wd = self.waited[n]
            for k, v in need.items():
                if wd.get(k, 0) < v:
                    self.eng[n].wait_ge(self.sems[k], v)
                    wd[k] = v

    def final_wait(self):
        self.barrier()


def _alibi_slopes():
    s = np.exp2(-8.0 * np.arange(1, 9, dtype=np.float64) / 8.0)
    return s[1::2], s[0::2]


def _mask_a():
    sa, _ = _alibi_slopes()
    d = np.arange(-4096, 4097)
    ad = np.abs(d)
    m = ((ad <= 64).astype(np.float64) + ((d % 4 == 0) & (ad <= 256)) + ((d % 16 == 0) & (ad <= 1024)))
    out = np.zeros((4, 128, MA_LEN), np.float32)
    i = np.arange(128)[:, None]
    n = np.arange(MA_LEN)[None, :]
    dd = i + 1408 - n
    for h in range(4):
        f = m * np.exp(-sa[h] * ad)
        out[h] = f[dd + 4096]
    return out


def _mask_b():
    _, sb = _alibi_slopes()
    out = np.zeros((4, 128, MB_LEN), np.float32)
    i = np.arange(128)[:, None]
    n = np.arange(MB_LEN)[None, :]
    dd = i + 512 - n
    ad = np.abs(dd)
    for h in range(4):
        out[h] = np.where(ad <= 128, np.exp(-sb[h] * ad), 0.0)
    return out


def _d_tables():
    i = np.arange(128)
    rk_l = (i // 64)[:, None, None]
    ck = (i % 64)[:, None, None]
    n = np.arange(22)[None, :, None]
    cq = np.arange(64)[None, None, :]
    delta = rk_l + 10 - n + 0 * cq
    dr_idx = np.clip(delta, -7, 7) + 7
    dc_idx = np.clip(ck - cq, -15, 15) + 15 + 0 * n
    cs = np.clip(cq - 8, 0, 64 - 16)
    colmask = ((ck >= cs) & (ck < cs + 16)) & (n >= 0)
    cw = (colmask & (delta >= -4) & (delta <= 3)).astype(np.float32).reshape(128, MD_ARR)
    cf = (colmask & (delta >= -7) & (delta <= 7)).astype(np.float32).reshape(128, MD_ARR)
    vb = np.zeros((12, 128, 512), np.float32)
    rkl = (i // 64)[:, None]
    rql = (np.arange(512) // 64)[None, :]
    for kt in range(6):
        rk = 2 * kt + rkl
        rs = np.maximum(rql - 4, 0)
        vb[kt] = ((rk >= rs) & (rk < rs + 8)).astype(np.float32)
        rk2 = -4 + 2 * kt + rkl
        rs2 = np.minimum(rql - 4, 0)
        vb[6 + kt] = ((rk2 >= rs2) & (rk2 < rs2 + 8)).astype(np.float32)
    return dr_idx.reshape(128, MD_ARR), dc_idx.reshape(128, MD_ARR), cw, cf, vb


def _rope_tables(smax):
    inv_freq = (1.0 / (np.float32(10000.0) ** (np.arange(0, 32, 2, dtype=np.float32) / np.float32(32)))).astype(np.float32)
    ang = (np.arange(smax, dtype=np.float32)[:, None] * inv_freq[None, :]).astype(np.float32)
    c = np.cos(ang.astype(np.float64)).astype(np.float32).T
    s = np.sin(ang.astype(np.float64)).astype(np.float32).T
    out = np.zeros((2, 32, smax), np.float32)
    out[0, :16] = c
    out[0, 16:] = c
    out[1, :16] = -s
    out[1, 16:] = s
    return out


def host_prep(inp, depth):
    L = depth
    w_in = np.asarray(inp["w_in"], np.float32)[:L]
    sec = np.cumsum([0, 256, 256, 256, 256, 128, 128, 256, 128, 32, 256, 256, 256])
    a_q, a_k, a_v, b_q, b_k, b_v, c_q, c_kv, c_kr, d_q, d_k, d_v = [w_in[:, :, sec[i]:sec[i + 1]] for i in range(12)]
    z64 = np.zeros((L, 1024, 64), np.float32)
    c_kr_sw = np.concatenate([c_kr[:, :, 16:32], c_kr[:, :, 0:16]], axis=-1)
    w_in_p = np.concatenate([a_q, a_k, b_q, b_k, d_q, d_k, a_v, b_v, d_v, c_q, c_kv, z64, c_kr, z64, c_kr_sw], axis=-1)
    assert w_in_p.shape[-1] == WIN_COLS
    wq = np.asarray(inp["w_q_up"], np.float32)[:L].reshape(L, 256, 4, 96)
    z = np.zeros((L, 256, 4, 64), np.float32)
    pe = wq[..., 64:96]
    pe_sw = np.concatenate([pe[..., 16:32], pe[..., 0:16]], axis=-1)
    wq_p = np.concatenate([wq, z, pe_sw], axis=-1).reshape(L, 256, 768)
    wkv = np.asarray(inp["w_kv_up"], np.float32)[:L].reshape(L, 128, 4, 128)
    wkv_p = np.concatenate([wkv[..., :64].reshape(L, 128, 256), wkv[..., 64:].reshape(L, 128, 256)], axis=-1)
    pcol = np.zeros((L, 128, NPCOL), np.float32)
    pcol[:, :, PC_NATT:PC_NATT + 8] = np.asarray(inp["norm_attn"], np.float32)[:L].reshape(L, 8, 128).transpose(0, 2, 1)
    pcol[:, :, PC_NMLP:PC_NMLP + 8] = np.asarray(inp["norm_mlp"], np.float32)[:L].reshape(L, 8, 128).transpose(0, 2, 1)
    pcol[:, :, PC_GN:PC_GN + 8] = np.asarray(inp["group_norm"], np.float32)[:L].reshape(L, 8, 128).transpose(0, 2, 1)
    pcol[:, :, PC_QN:PC_QN + 2] = np.asarray(inp["mla_q_norm"], np.float32)[:L].reshape(L, 2, 128).transpose(0, 2, 1)
    pcol[:, :, PC_KVN] = np.asarray(inp["mla_kv_norm"], np.float32)[:L]
    pcol[:, :, PC_SINK:PC_SINK + 4] = np.asarray(inp["sink_logits"], np.float32)[:L][:, None, :]
    nfin = np.broadcast_to(np.asarray(inp["norm_final"], np.float32)[None, :], (128, 1024)).copy()
    dr_idx, dc_idx, cw, cf, vb = _d_tables()
    rpb = np.asarray(inp["na_rpb"], np.float32)[:L]
    biasf = rpb[:, :, dr_idx, dc_idx]
    return {
        "w_in_p": np.ascontiguousarray(w_in_p),
        "w_q_p": np.ascontiguousarray(wq_p),
        "w_kv_p": np.ascontiguousarray(wkv_p),
        "w_out": np.ascontiguousarray(np.asarray(inp["w_out"], np.float32)[:L]),
        "w_up": np.ascontiguousarray(np.asarray(inp["w_mlp_up"], np.float32)[:L]),
        "w_down": np.ascontiguousarray(np.asarray(inp["w_mlp_down"], np.float32)[:L]),
        "pcol": pcol,
        "nfin": nfin,
        "ident": np.eye(128, dtype=np.float32),
        "mask_a": _mask_a(),
        "mask_b": _mask_b(),
        "biasf": np.ascontiguousarray(biasf),
        "dconst": np.ascontiguousarray(np.concatenate([cw, cf, vb.transpose(1, 0, 2).reshape(128, 12 * 512)], axis=1)),
    }


class Prog:
    def __init__(self, seqs, depth):
        self.seqs = list(seqs)
        self.L = depth
        self.T = sum(seqs)
        self.smax = max(seqs)
        self.starts = [int(x) for x in np.cumsum([0] + self.seqs[:-1])]

    def sb(self, st, name, shape, dt, n=1):
        out = []
        for i in range(n):
            t = st.enter_context(self.nc.sbuf_tensor("%s_%d_%d" % (name, self.uid, i), shape, dt))
            out.append((t, Res()))
        self.uid += 1
        return out

    def build(self):
        nc = bass.Bass("TRN2", target_bir_lowering=False)
        self.nc = nc
        self.uid = 0
        L, T = self.L, self.T

        def din(name, shape):
            return nc.dram_tensor(name, shape, F32, kind="ExternalInput").ap()

        def dscr(name, shape, dt):
            return nc.dram_tensor(name, shape, dt, kind="Internal").ap()

        self.x_in = din("x", [T, 1024])
        self.w_in_p = din("w_in_p", [L, 1024, WIN_COLS])
        self.w_q_p = din("w_q_p", [L, 256, 768])
        self.w_kv_p = din("w_kv_p", [L, 128, 512])
        self.w_out = din("w_out", [L, 1024, 1024])
        self.w_up = din("w_up", [L, 1024, 4096])
        self.w_down = din("w_down", [L, 4096, 1024])
        self.pcol_d = din("pcol", [L, 128, NPCOL])
        self.nfin_d = din("nfin", [128, 1024])
        self.ident_d = din("ident", [128, 128])
        self.mask_a_d = din("mask_a", [4, 128, MA_LEN])
        self.mask_b_d = din("mask_b", [4, 128, MB_LEN])
        self.biasf_d = din("biasf", [L, 4, 128, MD_ARR])
        self.dconst_d = din("dconst", [128, 2 * MD_ARR + 12 * 512])
        self.rope_d = din("rope", [2, 32, self.smax])
        self.y_out = nc.dram_tensor("y", [T, 1024], F32, kind="ExternalOutput").ap()

        self.xres = dscr("xres", [T, 1024], F32)
        self.win_b = dscr("win_b", [L, 128, 8, WIN_COLS], BF16)
        self.wq_b = dscr("wq_b", [L, 128, 2, 768], BF16)
        self.wkv_b = dscr("wkv_b", [L, 128, 512], BF16)
        self.wout_b = dscr("wout_b", [L, 128, 8, 1024], BF16)
        self.wup_b = dscr("wup_b", [L, 8, 128, 8, 512], BF16)
        self.wdown_b = dscr("wdown_b", [L, 128, 32, 1024], BF16)
        self.maska_b = dscr("maska_b", [4, 128, MA_LEN], BF16)
        self.maskb_b = dscr("maskb_b", [4, 128, MB_LEN], BF16)
        self.maskd_b = dscr("maskd_b", [4, 128, MD_LEN], BF16)
        self.qkT = dscr("qkT", [FM_COLS, T], BF16)
        self.qTc = dscr("qTc", [4, 96, T], BF16)
        self.kTc = dscr("kTc", [4, 96, T], BF16)
        self.vsc = dscr("vsc", [T, NVH, 65], BF16)
        self.onT = dscr("onT", [1024, T], BF16)
        nt = T // 512
        self.r_x = [Res() for _ in range(nt)]
        self.r_qk = [Res() for _ in range(nt)]
        self.r_on = [Res() for _ in range(nt)]
        self.r_qc = [Res() for _ in range(nt)]
        self.r_kc = [Res() for _ in range(nt)]
        self.r_v = [Res() for _ in range(nt)]
        self.r_w = Res()
        self.r_maskd = Res()

        with contextlib.ExitStack() as top:
            self.S = Sched(nc, top)
            self.ps = [(top.enter_context(nc.psum_tensor("ps%d" % i, [128, 512], F32)), Res()) for i in range(6)]
            self.pb = [(top.enter_context(nc.psum_tensor("pb%d" % i, [128, 1024], BF16)), Res()) for i in range(2)]
            self.ps_i = 0
            self.pb_i = 0
            cst = self.sb(top, "identf", [128, 128], F32)[0]
            self.identf, self.r_identf = cst
            cst = self.sb(top, "identb", [128, 128], BF16)[0]
            self.identb, self.r_identb = cst
            cst = self.sb(top, "neghalf", [128, 8], F32)[0]
            self.neghalf, self.r_neghalf = cst
            S = self.S
            S.dma("sp", self.identf[:], self.ident_d[:, :], writes=[self.r_identf])
            S.op("dve", lambda: nc.vector.tensor_copy(out=self.identb[:], in_=self.identf[:]),
                 reads=[self.r_identf], writes=[self.r_identb])
            S.op("dve", lambda: nc.vector.memset(self.neghalf[:], -0.5), writes=[self.r_neghalf])
            import os
            kstop = int(os.environ.get("KSTOP", "99"))
            if kstop >= 1:
                self.prologue()
            for l in range(L):
                if kstop >= 2:
                    self.phase_dmask(l)
                if kstop >= 3:
                    self.phase1(l)
                if kstop >= 4:
                    self.phase2(l)
                if kstop >= 5:
                    self.phase3(l)
            S.final_wait()
        return nc

    def next_ps(self):
        p = self.ps[self.ps_i % 4]
        self.ps_i += 1
        return p

    def next_pb(self):
        p = self.pb[self.pb_i % 2]
        self.pb_i += 1
        return p

    def prologue(self):
        nc, S, L = self.nc, self.S, self.L
        with contextlib.ExitStack() as st:
            stg = self.sb(st, "cv_f", [128, 4096], F32, 3)
            stb = self.sb(st, "cv_b", [128, 4096], BF16, 3)
            k = [0]

            def conv(src, dst, shape):
                n = int(np.prod(shape))
                assert n <= 4096
                (f, rf), (b, rb) = stg[k[0] % 3], stb[k[0] % 3]
                if len(shape) == 1:
                    fv, bv = f[:, 0:n], b[:, 0:n]
                else:
                    fv = f[:, 0:n].rearrange("p (a b) -> p a b", b=shape[1])
                    bv = b[:, 0:n].rearrange("p (a b) -> p a b", b=shape[1])
                S.dma("sp", fv, src, writes=[rf])
                if k[0] % 2 == 0:
                    S.op("dve", lambda: nc.vector.tensor_copy(out=b[:, 0:n], in_=f[:, 0:n]), reads=[rf], writes=[rb])
                else:
                    S.op("act", lambda: nc.scalar.copy(out=b[:, 0:n], in_=f[:, 0:n]), reads=[rf], writes=[rb])
                S.dma("pool", dst, bv, reads=[rb])
                k[0] += 1

            for l in range(L):
                wv = self.w_in_p[l].rearrange("(c p) n -> p c n", p=128)
                for c in range(8):
                    conv(wv[:, c, :], self.win_b[l, :, c, :], [WIN_COLS])
                conv(self.w_q_p[l].rearrange("(c p) n -> p c n", p=128), self.wq_b[l], [2, 768])
                conv(self.w_kv_p[l], self.wkv_b[l], [512])
                wv = self.w_out[l].rearrange("(c p) n -> p c n", p=128)
                for c0 in range(0, 8, 4):
                    conv(wv[:, c0:c0 + 4, :], self.wout_b[l, :, c0:c0 + 4, :], [4, 1024])
                wv = self.w_up[l].rearrange("(c p) n -> p c n", p=128)
                for g in range(8):
                    conv(wv[:, :, g * 512:(g + 1) * 512], self.wup_b[l, g], [8, 512])
                wv = self.w_down[l].rearrange("(f p) n -> p f n", p=128)
                for f0 in range(0, 32, 4):
                    conv(wv[:, f0:f0 + 4, :], self.wdown_b[l, :, f0:f0 + 4, :], [4, 1024])
            for h in range(4):
                conv(self.mask_a_d[h], self.maska_b[h], [MA_LEN])
                conv(self.mask_b_d[h], self.maskb_b[h], [MB_LEN])
            S.barrier()

    def phase_dmask(self, l):
        nc, S = self.nc, self.S
        with contextlib.ExitStack() as st:
            (dc, rdc) = self.sb(st, "dm_c", [128, 2 * MD_ARR + 12 * 512], F32)[0]
            bf = self.sb(st, "dm_bf", [128, MD_ARR], F32, 2)
            ex = self.sb(st, "dm_ex", [128, MD_ARR], F32, 2)
            full = self.sb(st, "dm_full", [128, MD_ARR], F32, 2)
            mo = self.sb(st, "dm_out", [128, MD_LEN], BF16, 2)
            S.dma("sp", dc[:], self.dconst_d[:, :], writes=[rdc])
            for h in range(4):
                (b, rb), (e, re), (fu, rfu), (m, rm) = bf[h % 2], ex[h % 2], full[h % 2], mo[h % 2]
                S.dma("sp", b[:], self.biasf_d[l, h], writes=[rb])
                S.op("act", lambda: nc.scalar.activation(out=e[:], in_=b[:], func=AF.Exp), reads=[rb], writes=[re])
                S.op("dve", lambda: nc.vector.tensor_tensor(out=m[:, 0:MD_ARR], in0=e[:], in1=dc[:, 0:MD_ARR], op=ALU.mult),
                     reads=[re, rdc], writes=[rm])
                S.op("dve", lambda: nc.vector.tensor_tensor(out=fu[:], in0=e[:], in1=dc[:, MD_ARR:2 * MD_ARR], op=ALU.mult),
                     reads=[re, rdc], writes=[rfu])
                for kt in range(6):
                    for (which, dr) in ((0, 2 * kt), (1, -4 + 2 * kt)):
                        o0 = MD_ARR + (which * 6 + kt) * 512
                        a0 = (10 - dr) * 64
                        v0 = 2 * MD_ARR + (which * 6 + kt) * 512
                        S.op("dve", lambda o0=o0, a0=a0, v0=v0: nc.vector.tensor_tensor(
                            out=m[:, o0:o0 + 512], in0=fu[:, a0:a0 + 512], in1=dc[:, v0:v0 + 512], op=ALU.mult),
                            reads=[rfu, rdc], writes=[rm])
                S.dma("pool", self.maskd_b[h], m[:], reads=[rm])
            S.barrier()

    def sbm(self, st, name, shape, dt, n, nres):
        out = []
        for i in range(n):
            t = st.enter_context(self.nc.sbuf_tensor("%s_%d_%d" % (name, self.uid, i), shape, dt))
            out.append((t, [Res() for _ in range(nres)]))
        self.uid += 1
        return out

    def rstd_from_acc(self, acc, racc, out, rout, c0, n):
        nc, S = self.nc, self.S
        S.op("dve", lambda: nc.vector.tensor_scalar(out=out[:, c0:c0 + n], in0=acc[:, c0:c0 + n], scalar1=EPS, scalar2=None, op0=ALU.add),
             reads=[racc], writes=[rout])
        S.op("pool", lambda: nc.gpsimd.tensor_tensor(out=out[:, c0:c0 + n], in0=out[:, c0:c0 + n], in1=self.neghalf[:, 0:n], op=ALU.pow),
             reads=[rout, self.r_neghalf], writes=[rout])

    def evac(self, i, out, in_, reads, writes, scale=None):
        nc, S = self.nc, self.S
        import os
        kev = os.environ.get("KEV")
        if kev == "0":
            return
        if kev == "1":
            i = 1
        if kev == "2":
            i = 0
        if kev == "3" and scale is None:
            i = 1
        if kev == "4":
            i = 1 if scale is not None else 0
        if i % 2 == 0:
            if scale is None:
                S.op("act", lambda: nc.scalar.activation(out=out, in_=in_, func=AF.Copy), reads=reads, writes=writes)
            else:
                S.op("act", lambda: nc.scalar.activation(out=out, in_=in_, func=AF.Copy, scale=scale), reads=reads, writes=writes)
        else:
            if scale is None:
                S.op("dve", lambda: nc.vector.tensor_copy(out=out, in_=in_), reads=reads, writes=writes)
            else:
                S.op("dve", lambda: nc.vector.tensor_scalar(out=out, in0=in_, scalar1=scale, scalar2=None, op0=ALU.mult),
                     reads=reads, writes=writes)

    def norm_transpose(self, xt, rxt, xn, rxn, ss, rss, rs, rrs, junk, hT, rhT, gain, rgain, D=1024):
        nc, S = self.nc, self.S
        sc = float(D) ** -0.5
        for sub in range(4):
            S.op("act", lambda: nc.scalar.activation(out=xn[:, sub, :], in_=xt[:, sub, :], func=AF.Square, scale=sc,
                                                     accum_out=ss[:, sub:sub + 1]),
                 reads=[rxt[sub]], writes=[rss, rxn[sub]])
        self.rstd_from_acc(ss, rss, rs, rrs, 0, 4)
        for sub in range(4):
            S.op("dve", lambda: nc.vector.tensor_scalar(out=xn[:, sub, :], in0=xt[:, sub, :], scalar1=rs[:, sub:sub + 1],
                                                        scalar2=None, op0=ALU.mult),
                 reads=[rxt[sub], rrs], writes=[rxn[sub]])
        for c in range(D // 128):
            pb, rpb = self.next_pb()
            for sub in range(4):
                S.op("pe", lambda: nc.tensor.transpose(out=pb[:, sub * 128:(sub + 1) * 128], in_=xn[:, sub, c * 128:(c + 1) * 128],
                                                       identity=self.identb[:]),
                     reads=[rxn[sub], self.r_identb], writes=[rpb], signal=(sub == 3))
            self.evac(c, hT[:, c, :], pb[:, 0:512], [rpb, rgain], [rhT[c]], scale=gain[:, c:c + 1])

    def chk(self, n):
        import os
        if int(os.environ.get('KSUB', '99')) < n:
            raise _Stop()

    def tiles(self):
        out = []
        for si, (s0, sl) in enumerate(zip(self.starts, self.seqs)):
            for t0 in range(s0, s0 + sl, 512):
                out.append((si, s0, t0))
        return out

    def phase1(self, l):
        nc, S = self.nc, self.S
        x_src = self.x_in if l == 0 else self.xres
        with contextlib.ExitStack() as st:
            (win, rwin) = self.sb(st, "p1_win", [128, 8, WIN_COLS], BF16)[0]
            (wq, rwq) = self.sb(st, "p1_wq", [128, 2, 768], BF16)[0]
            (wkv, rwkv) = self.sb(st, "p1_wkv", [128, 512], BF16)[0]
            (pc, rpc) = self.sb(st, "p1_pc", [128, NPCOL], F32)[0]
            xts = self.sbm(st, "p1_xt", [128, 4, 1024], F32, 2, 4)
            xns = self.sbm(st, "p1_xn", [128, 4, 1024], BF16, 2, 4)
            hTs = self.sbm(st, "p1_hT", [128, 8, 512], BF16, 2, 8)
            junk = None
            (junk2, _) = self.sb(st, "p1_junk2", [128, 256], BF16)[0]
            sss = self.sb(st, "p1_ss", [128, 8], F32, 2)
            rss_ = self.sb(st, "p1_rs", [128, 8], F32, 2)
            qks = self.sbm(st, "p1_qk", [128, 11, 512], BF16, 2, 11)
            vss = self.sbm(st, "p1_vs", [128, 4, NVH, 65], BF16, 2, 12)
            cns = self.sbm(st, "p1_cn", [128, 4, 384], BF16, 2, 8)
            cqTs = self.sbm(st, "p1_cqT", [128, 3, 512], BF16, 2, 3)
            ssc = self.sbm(st, "p1_ssc", [128, 8], F32, 2, 4)
            rsc = self.sbm(st, "p1_rsc", [128, 8], F32, 2, 4)
            rts = self.sb(st, "p1_rt", [128, 2, 512], F32, 2)
            tmps = self.sb(st, "p1_tmp", [128, 512], F32, 2)
            kpes = self.sb(st, "p1_kpe", [128, 512], F32, 2)
            qTs = self.sbm(st, "p1_qT", [128, 4, 512], BF16, 2, 8)
            kTs = self.sbm(st, "p1_kT", [128, 4, 512], BF16, 2, 8)

            S.dma("sp", win[:], self.win_b[l], writes=[rwin])
            S.dma("sp", wq[:], self.wq_b[l], writes=[rwq])
            S.dma("sp", wkv[:], self.wkv_b[l], writes=[rwkv])
            S.dma("sp", pc[:], self.pcol_d[l], writes=[rpc])
            for (v, rv) in vss:
                S.op("dve", lambda: nc.vector.memset(v[:], 1.0), writes=rv)

            tl = self.tiles()

            def load(i):
                si, s0, t0 = tl[i]
                xt, rxt = xts[i % 2]
                rt, rrt = rts[i % 2]
                S.dma("sp", xt[:], x_src[t0:t0 + 512, :].rearrange("(s p) d -> p s d", p=128),
                      reads=[self.r_x[t0 // 512]], writes=rxt)
                p0 = t0 - s0
                S.dma("sp", rt[64:96, :, :], self.rope_d[:, :, p0:p0 + 512].rearrange("a r t -> r a t"), writes=[rrt])

            load(0)
            tk = 0
            for i, (si, s0, t0) in enumerate(tl):
                if i + 1 < len(tl):
                    load(i + 1)
                b = i % 2
                xt, rxt = xts[b]
                xn, rxn = xns[b]
                hT, rhT = hTs[b]
                ss, rss = sss[b]
                rs, rrs = rss_[b]
                qk, rqk = qks[b]
                vs, rvs = vss[b]
                cn, rcn = cns[b]
                cqT, rcqT = cqTs[b]
                sc_, rsc_ = ssc[b]
                rc_, rrc_ = rsc[b]
                rt, rrt = rts[b]
                qT, rqT = qTs[b]
                kT, rkT = kTs[b]
                ti = t0 // 512
                try:
                  self.phase1_tile(locals())
                except _Stop:
                  break
            S.barrier()

    def phase1_tile(self, L_):
                nc, S = self.nc, self.S
                globals_ = L_
                (xt, rxt, xn, rxn, ss, rss, rs, rrs, junk, hT, rhT, pc, rpc, win, rwin, qk, rqk, t0, ti, vs, rvs, junk2, sc_, rsc_, rc_, rrc_, cn, rcn, cqT, rcqT, tmps, kpes, rt, rrt, kT, rkT, qT, rqT, wq, rwq, wkv, rwkv) = [L_[k] for k in 'xt rxt xn rxn ss rss rs rrs junk hT rhT pc rpc win rwin qk rqk t0 ti vs rvs junk2 sc_ rsc_ rc_ rrc_ cn rcn cqT rcqT tmps kpes rt rrt kT rkT qT rqT wq rwq wkv rwkv'.split()]
                tk = 0
                self.chk(1)
                self.norm_transpose(xt, rxt, xn, rxn, ss, rss, rs, rrs, junk, hT, rhT, pc[:, PC_NATT:PC_NATT + 8], rpc)
                self.chk(2)
                for j in range(11):
                    ps, rps = self.next_ps()
                    for kc in range(8):
                        S.op("pe", lambda: nc.tensor.matmul(ps[:, :], win[:, kc, j * 128:(j + 1) * 128], hT[:, kc, :],
                                                            start=(kc == 0), stop=(kc == 7)),
                             reads=[rwin, rhT[kc]], writes=[rps], signal=(kc == 7))
                    self.evac(j, qk[:, j, :], ps[:, :], [rps], [rqk[j]])
                import os
                if not os.environ.get("KNOSTORE"):
                    S.dma("pool", self.qkT[:, t0:t0 + 512].rearrange("(j p) t -> p j t", p=128), qk[:, :, :],
                          reads=rqk, writes=[self.r_qk[ti]])
                self.chk(3)
                for sub in range(4):
                    psA, rpsA = self.next_ps()
                    for kc in range(8):
                        S.op("pe", lambda: nc.tensor.matmul(psA[:, :], hT[:, kc, sub * 128:(sub + 1) * 128], win[:, kc, TM0:TM0 + 512],
                                                            start=(kc == 0), stop=(kc == 7)),
                             reads=[rwin, rhT[kc]], writes=[rpsA], signal=(kc == 7))
                    self.evac(0, vs[:, sub, 0:8, 0:64], psA[:, :].rearrange("p (h d) -> p h d", d=64), [rpsA], [rvs[sub * 3]])
                    psB, rpsB = self.next_ps()
                    for kc in range(8):
                        S.op("pe", lambda: nc.tensor.matmul(psB[:, :], hT[:, kc, sub * 128:(sub + 1) * 128], win[:, kc, TM0 + 512:TM0 + 1024],
                                                            start=(kc == 0), stop=(kc == 7)),
                             reads=[rwin, rhT[kc]], writes=[rpsB], signal=(kc == 7))
                    self.evac(0, vs[:, sub, 8:10, 0:64], psB[:, 0:128].rearrange("p (h d) -> p h d", d=64), [rpsB], [rvs[sub * 3 + 1]])
                    S.op("act", lambda: nc.scalar.activation(out=junk2[:, 0:256], in_=psB[:, 128:384], func=AF.Square, scale=1.0 / 16.0,
                                                             accum_out=sc_[:, 2 * sub:2 * sub + 1]),
                         reads=[rpsB], writes=[rsc_[sub]])
                    S.op("act", lambda: nc.scalar.activation(out=junk2[:, 0:128], in_=psB[:, 384:512], func=AF.Square, scale=128.0 ** -0.5,
                                                             accum_out=sc_[:, 2 * sub + 1:2 * sub + 2]),
                         reads=[rpsB], writes=[rsc_[sub]])
                    self.rstd_from_acc(sc_, rsc_[sub], rc_, rrc_[sub], 2 * sub, 2)
                    S.op("dve", lambda: nc.vector.tensor_scalar(out=cn[:, sub, 0:256], in0=psB[:, 128:384], scalar1=rc_[:, 2 * sub:2 * sub + 1],
                                                                scalar2=None, op0=ALU.mult),
                         reads=[rpsB, rrc_[sub]], writes=[rcn[2 * sub]])
                    S.op("dve", lambda: nc.vector.tensor_scalar(out=cn[:, sub, 256:384], in0=psB[:, 384:512], scalar1=rc_[:, 2 * sub + 1:2 * sub + 2],
                                                                scalar2=None, op0=ALU.mult),
                         reads=[rpsB, rrc_[sub]], writes=[rcn[2 * sub + 1]])
                self.chk(4)
                for c in range(3):
                    pb, rpb = self.next_pb()
                    for sub in range(4):
                        S.op("pe", lambda: nc.tensor.transpose(out=pb[:, sub * 128:(sub + 1) * 128], in_=cn[:, sub, c * 128:(c + 1) * 128],
                                                               identity=self.identb[:]),
                             reads=[rcn[2 * sub + (1 if c == 2 else 0)], self.r_identb], writes=[rpb], signal=(sub == 3))
                    self.evac(c, cqT[:, c, :], pb[:, 0:512], [rpb, rpc], [rcqT[c]], scale=pc[:, PC_QN + c:PC_QN + c + 1])
                self.chk(5)
                tmp, rtmp = tmps[tk % 2]
                kpe, rkpe = kpes[tk % 2]
                tk += 1
                ps1, rps1 = self.next_ps()
                for kc in range(8):
                    S.op("pe", lambda: nc.tensor.matmul(ps1[0:96, :], win[:, kc, KR0:KR0 + 96], hT[:, kc, :], start=(kc == 0), stop=(kc == 7)),
                         reads=[rwin, rhT[kc]], writes=[rps1], signal=(kc == 7))
                ps2, rps2 = self.next_ps()
                for kc in range(8):
                    S.op("pe", lambda: nc.tensor.matmul(ps2[0:96, :], win[:, kc, KR0 + 96:KR0 + 192], hT[:, kc, :], start=(kc == 0), stop=(kc == 7)),
                         reads=[rwin, rhT[kc]], writes=[rps2], signal=(kc == 7))
                S.op("dve", lambda: nc.vector.tensor_tensor(out=tmp[64:96, :], in0=ps2[64:96, :], in1=rt[64:96, 1, :], op=ALU.mult),
                     reads=[rps2, rrt], writes=[rtmp])
                S.op("dve", lambda: nc.vector.tensor_tensor(out=kpe[64:96, :], in0=ps1[64:96, :], in1=rt[64:96, 0, :], op=ALU.mult),
                     reads=[rps1, rrt], writes=[rkpe])
                for h in range(4):
                    S.op("pool", lambda: nc.gpsimd.tensor_tensor(out=kT[64:96, h, :], in0=kpe[64:96, :], in1=tmp[64:96, :], op=ALU.add),
                         reads=[rkpe, rtmp], writes=[rkT[2 * h + 1]])
                self.chk(6)
                for h in range(4):
                    tmp, rtmp = tmps[tk % 2]
                    kpe, rkpe = kpes[tk % 2]
                    tk += 1
                    pq1, rpq1 = self.next_ps()
                    for kc in range(2):
                        S.op("pe", lambda: nc.tensor.matmul(pq1[0:96, :], wq[:, kc, h * 192:h * 192 + 96], cqT[:, kc, :], start=(kc == 0), stop=(kc == 1)),
                             reads=[rwq, rcqT[kc]], writes=[rpq1], signal=(kc == 1))
                    pq2, rpq2 = self.next_ps()
                    for kc in range(2):
                        S.op("pe", lambda: nc.tensor.matmul(pq2[0:96, :], wq[:, kc, h * 192 + 96:h * 192 + 192], cqT[:, kc, :], start=(kc == 0), stop=(kc == 1)),
                             reads=[rwq, rcqT[kc]], writes=[rpq2], signal=(kc == 1))
                    S.op("act", lambda: nc.scalar.copy(out=qT[0:64, h, :], in_=pq1[0:64, :]), reads=[rpq1], writes=[rqT[2 * h]])
                    S.op("dve", lambda: nc.vector.tensor_tensor(out=tmp[64:96, :], in0=pq2[64:96, :], in1=rt[64:96, 1, :], op=ALU.mult),
                         reads=[rpq2, rrt], writes=[rtmp])
                    S.op("dve", lambda: nc.vector.tensor_tensor(out=kpe[64:96, :], in0=pq1[64:96, :], in1=rt[64:96, 0, :], op=ALU.mult),
                         reads=[rpq1, rrt], writes=[rkpe])
                    S.op("pool", lambda: nc.gpsimd.tensor_tensor(out=qT[64:96, h, :], in0=kpe[64:96, :], in1=tmp[64:96, :], op=ALU.add),
                         reads=[rkpe, rtmp], writes=[rqT[2 * h + 1]])
                    pk, rpk = self.next_ps()
                    S.op("pe", lambda: nc.tensor.matmul(pk[0:64, :], wkv[:, h * 64:(h + 1) * 64], cqT[:, 2, :], start=True, stop=True),
                         reads=[rwkv, rcqT[2]], writes=[rpk])
                    self.evac(h, kT[0:64, h, :], pk[0:64, :], [rpk], [rkT[2 * h]])
                for sub in range(4):
                    pv, rpv = self.next_ps()
                    S.op("pe", lambda: nc.tensor.matmul(pv[:, 0:256], cqT[:, 2, sub * 128:(sub + 1) * 128], wkv[:, 256:512], start=True, stop=True),
                         reads=[rwkv, rcqT[2]], writes=[rpv])
                    self.evac(0, vs[:, sub, 10:14, 0:64], pv[:, 0:256].rearrange("p (h d) -> p h d", d=64), [rpv], [rvs[sub * 3 + 2]])
                self.chk(7)
                S.dma("pool", self.qTc[:, :, t0:t0 + 512].rearrange("h r t -> r h t"), qT[0:96, :, :], reads=rqT, writes=[self.r_qc[ti]])
                S.dma("pool", self.kTc[:, :, t0:t0 + 512].rearrange("h r t -> r h t"), kT[0:96, :, :], reads=rkT, writes=[self.r_kc[ti]])
                S.dma("pool", self.vsc[t0:t0 + 512, :, :].rearrange("(s p) h e -> p s h e", p=128), vs[:, :, :, :],
                      reads=rvs, writes=[self.r_v[ti]])


    def phase2(self, l):
        nc, S = self.nc, self.S
        SPAN = 2048
        SMAXK = self.smax
        with contextlib.ExitStack() as st:
            Ks = self.sb(st, "p2_K", [96, SMAXK], BF16, 2)
            Vs = self.sb(st, "p2_V", [128, SMAXK // 128, 65], BF16, 2)
            Ms = self.sb(st, "p2_M", [128, MASK_MAX], BF16, 2)
            Qs = self.sb(st, "p2_Q", [96, 512], BF16, 3)
            Ps = self.sb(st, "p2_P", [128, 512], BF16, 4)
            P2s = self.sb(st, "p2_P2", [128, 512], BF16, 4)
            osbs = self.sb(st, "p2_osb", [65, 512], F32, 2)
            (ogrp, rogrp) = self.sbm(st, "p2_ogrp", [128, SPAN // 128, 256], F32, 1, SPAN // 512)[0]
            (pc, rpc) = self.sb(st, "p2_pc", [128, NPCOL], F32)[0]
            (esink, resink) = self.sb(st, "p2_esink", [128, 4], F32)[0]
            (junk, _) = self.sb(st, "p2_junk", [128, 256], F32)[0]
            gss = self.sb(st, "p2_gss", [128, 4], F32, 2)
            grs = self.sb(st, "p2_grs", [128, 4], F32, 2)
            recs = self.sb(st, "p2_rec", [128, 4], F32, 2)
            ons = self.sbm(st, "p2_on", [128, 4, 256], BF16, 2, 4)
            onTs = self.sbm(st, "p2_onT", [128, 2, 512], BF16, 2, 2)
            S.dma("sp", pc[:], self.pcol_d[l], writes=[rpc])
            S.op("act", lambda: nc.scalar.activation(out=esink[:], in_=pc[:, PC_SINK:PC_SINK + 4], func=AF.Exp),
                 reads=[rpc], writes=[resink])
            sc_banks = self.ps[0:3]
            o_banks = self.ps[3:5]
            tp_bank = self.ps[5]
            ctr = {"kv": 0, "q": 0, "p": 0, "sc": 0, "ob": 0, "gn": 0, "m": 0}

            mixers = [
                ("A", 4, 64, 0.125, ROW_AQ, ROW_AK, 0, 1),
                ("B", 4, 64, 0.125, ROW_BQ, ROW_BK, 4, 2),
                ("C", 4, 96, 96.0 ** -0.5, None, None, 10, 1),
                ("D", 4, 64, 0.125, ROW_DQ, ROW_DK, 6, 1),
            ]

            for si, (s0, sl) in enumerate(zip(self.starts, self.seqs)):
                for sp0 in range(0, sl, SPAN):
                    span = min(SPAN, sl - sp0)
                    nq = span // 512
                    for mi, (mname, H, dqk, scale, qrow, krow, vbase, grp) in enumerate(mixers):
                        for h in range(H):
                            kvh = h // grp
                            K, rK = Ks[ctr["kv"] % 2]
                            V, rV = Vs[ctr["kv"] % 2]
                            ctr["kv"] += 1
                            tiles_rng = range(s0 // 512, (s0 + sl) // 512)
                            if krow is not None:
                                ksrc = self.qkT[krow + kvh * 64:krow + kvh * 64 + 64, s0:s0 + sl]
                                rdep = [self.r_qk[t] for t in tiles_rng]
                            else:
                                ksrc = self.kTc[h, :, s0:s0 + sl]
                                rdep = [self.r_kc[t] for t in tiles_rng]
                            S.dma("sp", K[0:dqk, 0:sl], ksrc, reads=rdep, writes=[rK])
                            S.dma("sp", V[:, 0:sl // 128, :],
                                  self.vsc[s0:s0 + sl, vbase + kvh, :].rearrange("(k p) e -> p k e", p=128),
                                  reads=[self.r_v[t] for t in tiles_rng], writes=[rV])
                            M = rM = None
                            if mname != "C":
                                M, rM = Ms[ctr["m"] % 2]
                                ctr["m"] += 1
                                if mname == "A":
                                    S.dma("sp", M[:, 0:MA_LEN], self.maska_b[h], writes=[rM])
                                elif mname == "B":
                                    S.dma("sp", M[:, 0:MB_LEN], self.maskb_b[h], writes=[rM])
                                else:
                                    S.dma("sp", M[:, 0:MD_LEN], self.maskd_b[h], writes=[rM])
                            steps = []
                            for qi in range(nq):
                                q0 = sp0 + qi * 512
                                if mname == "A":
                                    lo, hi = q0 - 1024, q0 + 512 + 1024
                                elif mname == "B":
                                    lo, hi = q0 - 128, q0 + 512 + 128
                                elif mname == "C":
                                    lo, hi = 0, sl
                                else:
                                    lo, hi = q0 - 256, q0 + 768
                                lo, hi = max(lo, 0), min(hi, sl)
                                kts = list(range(lo // 128, hi // 128))
                                for ki, kt in enumerate(kts):
                                    steps.append((qi, q0, kt, ki == 0, ki == len(kts) - 1))
                            state = {}
                            deferred = []

                            def mask_ap(q0, kt):
                                dlt = kt * 128 - q0
                                if mname == "A":
                                    off = 1408 - dlt
                                elif mname == "B":
                                    off = 512 - dlt
                                else:
                                    rows = sl // 64
                                    R0 = q0 // 64
                                    if R0 == 0:
                                        off = MD_ARR + (dlt // 128) * 512
                                    elif R0 == rows - 8:
                                        off = MD_ARR + (6 + (dlt + 256) // 128) * 512
                                    else:
                                        off = 640 - dlt
                                return M[:, off:off + 512]

                            def emit_S(i):
                                qi, q0, kt, first, last = steps[i]
                                if first:
                                    Q, rQ = Qs[ctr["q"] % 3]
                                    ctr["q"] += 1
                                    t0 = s0 + q0
                                    if qrow is not None:
                                        qsrc = self.qkT[qrow + h * 64:qrow + h * 64 + 64, t0:t0 + 512]
                                        rd = [self.r_qk[t0 // 512]]
                                    else:
                                        qsrc = self.qTc[h, :, t0:t0 + 512]
                                        rd = [self.r_qc[t0 // 512]]
                                    S.dma("sp", Q[0:dqk, :], qsrc, reads=rd, writes=[rQ])
                                    state[qi] = (Q, rQ)
                                Q, rQ = state[qi]
                                sc, rsc = sc_banks[ctr["sc"] % 3]
                                ctr["sc"] += 1
                                S.op("pe", lambda: nc.tensor.matmul(sc[:, :], K[0:dqk, kt * 128:(kt + 1) * 128], Q[0:dqk, :], start=True, stop=True),
                                     reads=[rK, rQ], writes=[rsc])
                                P, rP = Ps[ctr["p"] % 4]
                                S.op("act", lambda: nc.scalar.activation(out=P[:, :], in_=sc[:, :], func=AF.Exp, scale=scale),
                                     reads=[rsc], writes=[rP])
                                if M is not None:
                                    P2, rP2 = P2s[ctr["p"] % 4]
                                    mk = mask_ap(q0, kt)
                                    S.op("dve", lambda: nc.vector.tensor_tensor(out=P2[:, :], in0=P[:, :], in1=mk, op=ALU.mult),
                                         reads=[rP, rM], writes=[rP2])
                                    state[("p", i)] = (P2, rP2)
                                else:
                                    state[("p", i)] = (P, rP)
                                ctr["p"] += 1

                            def emit_PV(j):
                                qi, q0, kt, first, last = steps[j]
                                P, rP = state.pop(("p", j))
                                if first:
                                    ob, rob = o_banks[ctr["ob"] % 2]
                                    ctr["ob"] += 1
                                    state[("o", qi)] = (ob, rob)
                                ob, rob = state[("o", qi)]
                                S.op("pe", lambda: nc.tensor.matmul(ob[0:65, :], V[:, kt, :], P[:, :], start=first, stop=last),
                                     reads=[rV, rP], writes=[rob], signal=True)
                                if last:
                                    osb, rosb = osbs[qi % 2]
                                    S.op("dve", lambda: nc.vector.tensor_copy(out=osb[:, :], in_=ob[0:65, :]), reads=[rob], writes=[rosb])
                                    deferred.append((qi, osb, rosb))

                            def emit_epilogue(qi, osb, rosb):
                                tp, rtp = tp_bank
                                for sub in range(4):
                                    S.op("pe", lambda: nc.tensor.transpose(out=tp[:, sub * 65:(sub + 1) * 65], in_=osb[0:65, sub * 128:(sub + 1) * 128],
                                                                           identity=self.identf[0:65, 0:65]),
                                         reads=[rosb, self.r_identf], writes=[rtp], signal=(sub == 3))
                                rec, rrec = recs[qi % 2]
                                den = tp[:, 0:260].rearrange("p (s e) -> p s e", e=65)[:, :, 64]
                                if mname == "B":
                                    S.op("dve", lambda: nc.vector.tensor_scalar(out=rec[:, 0:4], in0=den, scalar1=esink[:, h:h + 1], scalar2=None, op0=ALU.add),
                                         reads=[rtp, resink], writes=[rrec])
                                    S.op("dve", lambda: nc.vector.reciprocal(out=rec[:, 0:4], in_=rec[:, 0:4]), reads=[rrec], writes=[rrec])
                                else:
                                    S.op("dve", lambda: nc.vector.reciprocal(out=rec[:, 0:4], in_=den), reads=[rtp], writes=[rrec])
                                for sub in range(4):
                                    S.op("dve", lambda: nc.vector.tensor_scalar(out=ogrp[:, qi * 4 + sub, h * 64:(h + 1) * 64],
                                                                                in0=tp[:, sub * 65:sub * 65 + 64], scalar1=rec[:, sub:sub + 1],
                                                                                scalar2=None, op0=ALU.mult),
                                         reads=[rtp, rrec], writes=[rogrp[qi]])

                            LOOK = 2
                            n = len(steps)
                            for i in range(n + LOOK):
                                if i < n:
                                    emit_S(i)
                                dl = deferred
                                deferred = []
                                for (a, b_, c_) in dl:
                                    emit_epilogue(a, b_, c_)
                                if i - LOOK >= 0:
                                    emit_PV(i - LOOK)
                            for (a, b_, c_) in deferred:
                                emit_epilogue(a, b_, c_)
                            deferred = []
                        for qi in range(nq):
                            t0 = s0 + sp0 + qi * 512
                            g = ctr["gn"] % 2
                            ctr["gn"] += 1
                            gs, rgs = gss[g]
                            gr, rgr = grs[g]
                            on, ron = ons[g]
                            onT, ronT = onTs[g]
                            for sub in range(4):
                                S.op("act", lambda: nc.scalar.activation(out=junk[:, :], in_=ogrp[:, qi * 4 + sub, :], func=AF.Square, scale=1.0 / 16.0,
                                                                         accum_out=gs[:, sub:sub + 1]),
                                     reads=[rogrp[qi]], writes=[rgs])
                            self.rstd_from_acc(gs, rgs, gr, rgr, 0, 4)
                            for sub in range(4):
                                S.op("dve", lambda: nc.vector.tensor_scalar(out=on[:, sub, :], in0=ogrp[:, qi * 4 + sub, :], scalar1=gr[:, sub:sub + 1],
                                                                            scalar2=None, op0=ALU.mult),
                                     reads=[rogrp[qi], rgr], writes=[ron[sub]])
                            for c in range(2):
                                pb, rpb = self.next_pb()
                                for sub in range(4):
                                    S.op("pe", lambda: nc.tensor.transpose(out=pb[:, sub * 128:(sub + 1) * 128], in_=on[:, sub, c * 128:(c + 1) * 128],
                                                                           identity=self.identb[:]),
                                         reads=[ron[sub], self.r_identb], writes=[rpb], signal=(sub == 3))
                                gc = PC_GN + mi * 2 + c
                                S.op("dve", lambda: nc.vector.tensor_scalar(out=onT[:, c, :], in0=pb[:, 0:512], scalar1=pc[:, gc:gc + 1], scalar2=None, op0=ALU.mult),
                                     reads=[rpb, rpc], writes=[ronT[c]])
                            S.dma("pool", self.onT[mi * 256:(mi + 1) * 256, t0:t0 + 512].rearrange("(c p) t -> p c t", p=128), onT[:, :, :],
                                  reads=ronT, writes=[self.r_on[t0 // 512]])
            S.barrier()

    def phase3(self, l):
        nc, S = self.nc, self.S
        last = (l == self.L - 1)
        x_src = self.x_in if l == 0 else self.xres
        with contextlib.ExitStack() as st:
            (wo, rwo) = self.sb(st, "p3_wo", [128, 8, 1024], BF16)[0]
            (wd, rwd) = self.sb(st, "p3_wd", [128, 32, 1024], BF16)[0]
            wus = self.sb(st, "p3_wu", [128, 8, 512], BF16, 2)
            (pc, rpc) = self.sb(st, "p3_pc", [128, NPCOL], F32)[0]
            (nf, rnf) = self.sb(st, "p3_nf", [128, 1024], F32)[0]
            xts = self.sbm(st, "p3_xt", [128, 4, 1024], F32, 2, 4)
            (xn, rxn) = self.sbm(st, "p3_xn", [128, 4, 1024], BF16, 1, 4)[0]
            onTs = self.sb(st, "p3_onT", [128, 8, 512], BF16, 2)
            (hT, rhT) = self.sbm(st, "p3_hT", [128, 8, 512], BF16, 1, 8)[0]
            (uT, ruT) = self.sbm(st, "p3_uT", [128, 32, 512], BF16, 1, 32)[0]
            rls = self.sb(st, "p3_rl", [128, 512], BF16, 3)
            junk = None
            sss = self.sb(st, "p3_ss", [128, 8], F32, 2)
            rss_ = self.sb(st, "p3_rs", [128, 8], F32, 2)
            S.dma("sp", wo[:], self.wout_b[l], writes=[rwo])
            S.dma("sp", pc[:], self.pcol_d[l], writes=[rpc])
            S.dma("sp", nf[:], self.nfin_d[:, :], writes=[rnf])
            S.dma("sp", wd[:], self.wdown_b[l], writes=[rwd])
            tl = self.tiles()
            wu_ctr = [0]

            def load(i):
                si, s0, t0 = tl[i]
                xt, rxt = xts[i % 2]
                oT, roT = onTs[i % 2]
                S.dma("sp", xt[:], x_src[t0:t0 + 512, :].rearrange("(s p) d -> p s d", p=128),
                      reads=[self.r_x[t0 // 512]], writes=rxt)
                S.dma("sp", oT[:], self.onT[:, t0:t0 + 512].rearrange("(c p) t -> p c t", p=128),
                      reads=[self.r_on[t0 // 512]], writes=[roT])

            def load_wu(g):
                wu, rwu = wus[wu_ctr[0] % 2]
                wu_ctr[0] += 1
                S.dma("sp", wu[:], self.wup_b[l, g], writes=[rwu])
                return wu, rwu

            load(0)
            for i, (si, s0, t0) in enumerate(tl):
                ti = t0 // 512
                xt, rxt = xts[i % 2]
                oT, roT = onTs[i % 2]
                ss, rss = sss[i % 2]
                rs, rrs = rss_[i % 2]
                wu_next = load_wu(0)
                k = 0
                for sub in range(4):
                    for half in range(2):
                        ps, rps = self.next_ps()
                        for c in range(8):
                            S.op("pe", lambda: nc.tensor.matmul(ps[:, :], oT[:, c, sub * 128:(sub + 1) * 128], wo[:, c, half * 512:(half + 1) * 512],
                                                                start=(c == 0), stop=(c == 7)),
                                 reads=[roT, rwo], writes=[rps], signal=(c == 7))
                        S.op("dve", lambda: nc.vector.tensor_tensor(out=xt[:, sub, half * 512:(half + 1) * 512], in0=ps[:, :],
                                                                    in1=xt[:, sub, half * 512:(half + 1) * 512], op=ALU.add),
                             reads=[rps, rxt[sub]], writes=[rxt[sub]])
                        k += 1
                if i + 1 < len(tl):
                    load(i + 1)
                self.norm_transpose(xt, rxt, xn, rxn, ss, rss, rs, rrs, junk, hT, rhT, pc[:, PC_NMLP:PC_NMLP + 8], rpc)
                for g in range(8):
                    wu, rwu = wu_next
                    if g + 1 < 8:
                        wu_next = load_wu(g + 1)
                    for f in range(4):
                        fc = g * 4 + f
                        ps, rps = self.next_ps()
                        for c in range(8):
                            S.op("pe", lambda: nc.tensor.matmul(ps[:, :], wu[:, c, f * 128:(f + 1) * 128], hT[:, c, :],
                                                                start=(c == 0), stop=(c == 7)),
                                 reads=[rwu, rhT[c]], writes=[rps], signal=(c == 7))
                        rl, rrl = rls[fc % 3]
                        S.op("act", lambda: nc.scalar.activation(out=rl[:, :], in_=ps[:, :], func=AF.Relu), reads=[rps], writes=[rrl])
                        S.op("dve", lambda: nc.vector.tensor_tensor(out=uT[:, fc, :], in0=rl[:, :], in1=rl[:, :], op=ALU.mult),
                             reads=[rrl], writes=[ruT[fc]])
                for sub in range(4):
                    for half in range(2):
                        ps, rps = self.next_ps()
                        for fc in range(32):
                            S.op("pe", lambda: nc.tensor.matmul(ps[:, :], uT[:, fc, sub * 128:(sub + 1) * 128], wd[:, fc, half * 512:(half + 1) * 512],
                                                                start=(fc == 0), stop=(fc == 31)),
                                 reads=[ruT[fc], rwd], writes=[rps], signal=(fc == 31))
                        S.op("dve", lambda: nc.vector.tensor_tensor(out=xt[:, sub, half * 512:(half + 1) * 512], in0=ps[:, :],
                                                                    in1=xt[:, sub, half * 512:(half + 1) * 512], op=ALU.add),
                             reads=[rps, rxt[sub]], writes=[rxt[sub]])
                if not last:
                    S.dma("pool", self.xres[t0:t0 + 512, :].rearrange("(s p) d -> p s d", p=128), xt[:],
                          reads=rxt, writes=[self.r_x[ti]])
                else:
                    for sub in range(4):
                        S.op("act", lambda: nc.scalar.activation(out=xn[:, sub, :], in_=xt[:, sub, :], func=AF.Square, scale=1.0 / 32.0,
                                                                 accum_out=ss[:, 4 + sub:5 + sub]),
                             reads=[rxt[sub]], writes=[rss, rxn[sub]])
                    self.rstd_from_acc(ss, rss, rs, rrs, 4, 4)
                    for sub in range(4):
                        S.op("dve", lambda: nc.vector.scalar_tensor_tensor(out=xt[:, sub, :], in0=xt[:, sub, :], scalar=rs[:, 4 + sub:5 + sub],
                                                                           in1=nf[:, :], op0=ALU.mult, op1=ALU.mult),
                             reads=[rxt[sub], rrs, rnf], writes=[rxt[sub]])
                    S.dma("pool", self.y_out[t0:t0 + 512, :].rearrange("(s p) d -> p s d", p=128), xt[:], reads=rxt)
            S.barrier()


_PROG_CACHE = {}


def get_prog(seqs, depth):
    key = (tuple(seqs), depth)
    if key not in _PROG_CACHE:
        p = Prog(seqs, depth)
        p.build()
        _PROG_CACHE[key] = p
    return _PROG_CACHE[key]


def run_cores(x_list, shared, seqs, depth, core_ids):
    prog = get_prog(seqs, depth)
    shared = dict(shared)
    shared["rope"] = _rope_tables(max(seqs))
    in_maps = []
    for x in x_list:
        m = dict(shared)
        m["x"] = np.ascontiguousarray(x, dtype=np.float32)
        in_maps.append(m)
    res = run_bass_kernel_spmd(prog.nc, in_maps, core_ids=core_ids)
    return [r["y"] for r in res.results]


def kernel(**inputs):
    xp = np.asarray(inputs["x_prompt"], np.float32)
    xs = np.asarray(inputs["x_sample"], np.float32)
    shared = host_prep(inputs, DEPTH)
    seqs = [2048, 2048, 16384]
    x_list = []
    for c in range(N_CORES):
        x_list.append(np.concatenate([xp[2 * c], xp[2 * c + 1], xs[c % 2]], axis=0))
    ys = run_cores(x_list, shared, seqs, DEPTH, list(range(N_CORES)))
    y_prompt = np.empty_like(xp)
    y_sample = np.empty_like(xs)
    for c in range(N_CORES):
        y_prompt[2 * c] = ys[c][0:2048]
        y_prompt[2 * c + 1] = ys[c][2048:4096]
    y_sample[0] = ys[0][4096:]
    y_sample[1] = ys[1][4096:]
    return (y_prompt, y_sample)
```

```python
import contextlib
import numpy as np
import concourse.bass as bass
import concourse.mybir as mybir
from concourse.bass_utils import run_bass_kernel_spmd

F32 = mybir.dt.float32
BF16 = mybir.dt.bfloat16
AF = mybir.ActivationFunctionType
ALU = mybir.AluOpType

D_MODEL = 1024
DEPTH = 4
D_FF = 4096
EPS = 1e-5
N_CORES = 8
FM_COLS = 1408
TM0 = 1408
KR0 = 2432
WIN_COLS = 2624
NPCOL = 40
PC_NATT, PC_NMLP, PC_GN, PC_QN, PC_KVN, PC_SINK = 0, 8, 16, 24, 26, 27
MA_LEN, MB_LEN, MD_ARR = 2944, 1152, 1408
MD_LEN = MD_ARR + 12 * 512
MASK_MAX = MD_LEN
ROW_AQ, ROW_AK, ROW_BQ, ROW_BK, ROW_DQ, ROW_DK = 0, 256, 512, 768, 896, 1152
NVH = 14
import os as _os
DUMMY_COLS = int(_os.environ.get("KDUM", "0"))


class _Stop(Exception):
    pass


class Res:
    __slots__ = ("w", "r")

    def __init__(self):
        self.w = {}
        self.r = {}


class Sched:
    def __init__(self, nc, stack, n_dma_sems=12):
        self.nc = nc
        self.eng = {"pe": nc.tensor, "act": nc.scalar, "dve": nc.vector, "pool": nc.gpsimd, "sp": nc.sync}
        self.names = list(self.eng.keys())
        self.sems = {}
        for n in self.names:
            self.sems[n] = stack.enter_context(nc.semaphore("s_" + n))
        self.cnt = {n: 0 for n in self.names}
        self.dq = {}
        for q in ("sp", "pool"):
            lst = []
            for i in range(n_dma_sems):
                key = "d_%s_%d" % (q, i)
                self.sems[key] = stack.enter_context(nc.semaphore(key))
                self.cnt[key] = 0
                lst.append(key)
            self.dq[q] = [lst, 0]
        self.waited = {n: {} for n in self.names}
        self.pending = {n: [] for n in self.names}
        self.nops = 0

    def _needs(self, X, reads, writes):
        need = {}
        for r in reads:
            for k, v in r.w.items():
                if need.get(k, 0) < v:
                    need[k] = v
        for w in writes:
            for k, v in w.w.items():
                if need.get(k, 0) < v:
                    need[k] = v
            for k, v in w.r.items():
                if need.get(k, 0) < v:
                    need[k] = v
        return need

    def _emit_waits(self, X, need):
        wd = self.waited[X]
        e = self.eng[X]
        for k, v in need.items():
            if k == X and X == "pe":
                continue
            if wd.get(k, 0) < v:
                e.wait_ge(self.sems[k], v)
                wd[k] = v

    def _commit(self, ev_key, ev_val, reads, writes):
        for r in reads:
            if r.r.get(ev_key, 0) < ev_val:
                r.r[ev_key] = ev_val
        for w in writes:
            w.w = {ev_key: ev_val}
            w.r = {}

    def op(self, X, fn, reads=(), writes=(), signal=True):
        need = self._needs(X, reads, writes)
        self._emit_waits(X, need)
        inst = fn()
        self.nops += 1
        if signal:
            self.cnt[X] += 1
            inst.then_inc(self.sems[X], 1)
            c = self.cnt[X]
            for (rr, ww) in self.pending[X]:
                self._commit(X, c, rr, ww)
            self.pending[X] = []
            self._commit(X, c, reads, writes)
        else:
            self.pending[X].append((reads, writes))
        return inst

    def dma(self, q, out, in_, reads=(), writes=(), **kw):
        assert not self.pending[q]
        need = self._needs(q, reads, writes)
        lst, idx = self.dq[q]
        key = lst[idx % len(lst)]
        self.dq[q][1] = idx + 1
        if self.cnt[key] > 0:
            need[key] = max(need.get(key, 0), self.cnt[key])
        self._emit_waits(q, need)
        inst = self.eng[q].dma_start(out=out, in_=in_, **kw)
        self.nops += 1
        self.cnt[key] += 16
        inst.then_inc(self.sems[key], 16)
        self._commit(key, self.cnt[key], reads, writes)
        return inst

    def barrier(self):
        for n in self.names:
            assert not self.pending[n], n
        allev = {k: v for k, v in self.cnt.items() if v > 0}
        for n in self.names:
            need = {k: v for k, v in allev.items() if k != n}
            wd = self.waited[n]
            for k, v in need.items():
                if wd.get(k, 0) < v:
                    self.eng[n].wait_ge(self.sems[k], v)
                    wd[k] = v

    def final_wait(self):
        self.barrier()


def _alibi_slopes():
    s = np.exp2(-8.0 * np.arange(1, 9, dtype=np.float64) / 8.0)
    return s[1::2], s[0::2]


def _mask_a():
    sa, _ = _alibi_slopes()
    d = np.arange(-4096, 4097)
    ad = np.abs(d)
    m = ((ad <= 64).astype(np.float64) + ((d % 4 == 0) & (ad <= 256)) + ((d % 16 == 0) & (ad <= 1024)))
    out = np.zeros((4, 128, MA_LEN), np.float32)
    i = np.arange(128)[:, None]
    n = np.arange(MA_LEN)[None, :]
    dd = i + 1408 - n
    for h in range(4):
        f = m * np.exp(-sa[h] * ad)
        out[h] = f[dd + 4096]
    return out


def _mask_b():
    _, sb = _alibi_slopes()
    out = np.zeros((4, 128, MB_LEN), np.float32)
    i = np.arange(128)[:, None]
    n = np.arange(MB_LEN)[None, :]
    dd = i + 512 - n
    ad = np.abs(dd)
    for h in range(4):
        out[h] = np.where(ad <= 128, np.exp(-sb[h] * ad), 0.0)
    return out


def _d_tables():
    i = np.arange(128)
    rk_l = (i // 64)[:, None, None]
    ck = (i % 64)[:, None, None]
    n = np.arange(22)[None, :, None]
    cq = np.arange(64)[None, None, :]
    delta = rk_l + 10 - n + 0 * cq
    dr_idx = np.clip(delta, -7, 7) + 7
    dc_idx = np.clip(ck - cq, -15, 15) + 15 + 0 * n
    cs = np.clip(cq - 8, 0, 64 - 16)
    colmask = ((ck >= cs) & (ck < cs + 16)) & (n >= 0)
    cw = (colmask & (delta >= -4) & (delta <= 3)).astype(np.float32).reshape(128, MD_ARR)
    cf = (colmask & (delta >= -7) & (delta <= 7)).astype(np.float32).reshape(128, MD_ARR)
    vb = np.zeros((12, 128, 512), np.float32)
    rkl = (i // 64)[:, None]
    rql = (np.arange(512) // 64)[None, :]
    for kt in range(6):
        rk = 2 * kt + rkl
        rs = np.maximum(rql - 4, 0)
        vb[kt] = ((rk >= rs) & (rk < rs + 8)).astype(np.float32)
        rk2 = -4 + 2 * kt + rkl
        rs2 = np.minimum(rql - 4, 0)
        vb[6 + kt] = ((rk2 >= rs2) & (rk2 < rs2 + 8)).astype(np.float32)
    return dr_idx.reshape(128, MD_ARR), dc_idx.reshape(128, MD_ARR), cw, cf, vb


def _rope_tables(smax):
    inv_freq = (1.0 / (np.float32(10000.0) ** (np.arange(0, 32, 2, dtype=np.float32) / np.float32(32)))).astype(np.float32)
    ang = (np.arange(smax, dtype=np.float32)[:, None] * inv_freq[None, :]).astype(np.float32)
    c = np.cos(ang.astype(np.float64)).astype(np.float32).T
    s = np.sin(ang.astype(np.float64)).astype(np.float32).T
    out = np.zeros((2, 32, smax), np.float32)
    out[0, :16] = c
    out[0, 16:] = c
    out[1, :16] = -s
    out[1, 16:] = s
    return out


def host_prep(inp, depth):
    L = depth
    w_in = np.asarray(inp["w_in"], np.float32)[:L]
    sec = np.cumsum([0, 256, 256, 256, 256, 128, 128, 256, 128, 32, 256, 256, 256])
    a_q, a_k, a_v, b_q, b_k, b_v, c_q, c_kv, c_kr, d_q, d_k, d_v = [w_in[:, :, sec[i]:sec[i + 1]] for i in range(12)]
    z64 = np.zeros((L, 1024, 64), np.float32)
    c_kr_sw = np.concatenate([c_kr[:, :, 16:32], c_kr[:, :, 0:16]], axis=-1)
    w_in_p = np.concatenate([a_q, a_k, b_q, b_k, d_q, d_k, a_v, b_v, d_v, c_q, c_kv, z64, c_kr, z64, c_kr_sw], axis=-1)
    assert w_in_p.shape[-1] == WIN_COLS
    wq = np.asarray(inp["w_q_up"], np.float32)[:L].reshape(L, 256, 4, 96)
    z = np.zeros((L, 256, 4, 64), np.float32)
    pe = wq[..., 64:96]
    pe_sw = np.concatenate([pe[..., 16:32], pe[..., 0:16]], axis=-1)
    wq_p = np.concatenate([wq, z, pe_sw], axis=-1).reshape(L, 256, 768)
    wkv = np.asarray(inp["w_kv_up"], np.float32)[:L].reshape(L, 128, 4, 128)
    wkv_p = np.concatenate([wkv[..., :64].reshape(L, 128, 256), wkv[..., 64:].reshape(L, 128, 256)], axis=-1)
    pcol = np.zeros((L, 128, NPCOL), np.float32)
    pcol[:, :, PC_NATT:PC_NATT + 8] = np.asarray(inp["norm_attn"], np.float32)[:L].reshape(L, 8, 128).transpose(0, 2, 1)
    pcol[:, :, PC_NMLP:PC_NMLP + 8] = np.asarray(inp["norm_mlp"], np.float32)[:L].reshape(L, 8, 128).transpose(0, 2, 1)
    pcol[:, :, PC_GN:PC_GN + 8] = np.asarray(inp["group_norm"], np.float32)[:L].reshape(L, 8, 128).transpose(0, 2, 1)
    pcol[:, :, PC_QN:PC_QN + 2] = np.asarray(inp["mla_q_norm"], np.float32)[:L].reshape(L, 2, 128).transpose(0, 2, 1)
    pcol[:, :, PC_KVN] = np.asarray(inp["mla_kv_norm"], np.float32)[:L]
    pcol[:, :, PC_SINK:PC_SINK + 4] = np.asarray(inp["sink_logits"], np.float32)[:L][:, None, :]
    nfin = np.broadcast_to(np.asarray(inp["norm_final"], np.float32)[None, :], (128, 1024)).copy()
    dr_idx, dc_idx, cw, cf, vb = _d_tables()
    rpb = np.asarray(inp["na_rpb"], np.float32)[:L]
    biasf = rpb[:, :, dr_idx, dc_idx]
    return {
        "w_in_p": np.ascontiguousarray(w_in_p),
        "w_q_p": np.ascontiguousarray(wq_p),
        "w_kv_p": np.ascontiguousarray(wkv_p),
        "w_out": np.ascontiguousarray(np.asarray(inp["w_out"], np.float32)[:L]),
        "w_up": np.ascontiguousarray(np.asarray(inp["w_mlp_up"], np.float32)[:L]),
        "w_down": np.ascontiguousarray(np.asarray(inp["w_mlp_down"], np.float32)[:L]),
        "pcol": pcol,
        "nfin": nfin,
        "ident": np.eye(128, dtype=np.float32),
        "mask_a": _mask_a(),
        "mask_b": _mask_b(),
        "biasf": np.ascontiguousarray(biasf),
        "dconst": np.ascontiguousarray(np.concatenate([cw, cf, vb.transpose(1, 0, 2).reshape(128, 12 * 512)], axis=1)),
    }


class Prog:
    def __init__(self, seqs, depth):
        self.seqs = list(seqs)
        self.L = depth
        self.T = sum(seqs)
        self.smax = max(seqs)
        self.starts = [int(x) for x in np.cumsum([0] + self.seqs[:-1])]

    def sb(self, st, name, shape, dt, n=1):
        out = []
        for i in range(n):
            t = st.enter_context(self.nc.sbuf_tensor("%s_%d_%d" % (name, self.uid, i), shape, dt))
            out.append((t, Res()))
        self.uid += 1
        return out

    def build(self):
        nc = bass.Bass("TRN2", target_bir_lowering=False)
        self.nc = nc
        self.uid = 0
        L, T = self.L, self.T

        def din(name, shape):
            return nc.dram_tensor(name, shape, F32, kind="ExternalInput").ap()

        def dscr(name, shape, dt):
            return nc.dram_tensor(name, shape, dt, kind="Internal").ap()

        self.x_in = din("x", [T, 1024])
        self.w_in_p = din("w_in_p", [L, 1024, WIN_COLS])
        self.w_q_p = din("w_q_p", [L, 256, 768])
        self.w_kv_p = din("w_kv_p", [L, 128, 512])
        self.w_out = din("w_out", [L, 1024, 1024])
        self.w_up = din("w_up", [L, 1024, 4096])
        self.w_down = din("w_down", [L, 4096, 1024])
        self.pcol_d = din("pcol", [L, 128, NPCOL])
        self.nfin_d = din("nfin", [128, 1024])
        self.ident_d = din("ident", [128, 128])
        self.mask_a_d = din("mask_a", [4, 128, MA_LEN])
        self.mask_b_d = din("mask_b", [4, 128, MB_LEN])
        self.biasf_d = din("biasf", [L, 4, 128, MD_ARR])
        self.dconst_d = din("dconst", [128, 2 * MD_ARR + 12 * 512])
        self.rope_d = din("rope", [2, 32, self.smax])
        self.y_out = nc.dram_tensor("y", [T, 1024], F32, kind="ExternalOutput").ap()

        self.xres = dscr("xres", [T, 1024], F32)
        self.win_b = dscr("win_b", [L, 128, 8, WIN_COLS], BF16)
        self.wq_b = dscr("wq_b", [L, 128, 2, 768], BF16)
        self.wkv_b = dscr("wkv_b", [L, 128, 512], BF16)
        self.wout_b = dscr("wout_b", [L, 128, 8, 1024], BF16)
        self.wup_b = dscr("wup_b", [L, 8, 128, 8, 512], BF16)
        self.wdown_b = dscr("wdown_b", [L, 128, 32, 1024], BF16)
        self.maska_b = dscr("maska_b", [4, 128, MA_LEN], BF16)
        self.maskb_b = dscr("maskb_b", [4, 128, MB_LEN], BF16)
        self.maskd_b = dscr("maskd_b", [4, 128, MD_LEN], BF16)
        self.qkT = dscr("qkT", [FM_COLS, T], BF16)
        self.qTc = dscr("qTc", [4, 96, T], BF16)
        self.kTc = dscr("kTc", [4, 96, T], BF16)
        self.vsc = dscr("vsc", [T, NVH, 65], BF16)
        self.onT = dscr("onT", [1024, T], BF16)
        nt = T // 512
        self.r_x = [Res() for _ in range(nt)]
        self.r_qk = [Res() for _ in range(nt)]
        self.r_on = [Res() for _ in range(nt)]
        self.r_qc = [Res() for _ in range(nt)]
        self.r_kc = [Res() for _ in range(nt)]
        self.r_v = [Res() for _ in range(nt)]
        self.r_w = Res()
        self.r_maskd = Res()

        with contextlib.ExitStack() as top:
            self.S = Sched(nc, top)
            cst = self.sb(top, "identf", [128, 128], F32)[0]
            self.identf, self.r_identf = cst
            cst = self.sb(top, "identb", [128, 128], BF16)[0]
            self.identb, self.r_identb = cst
            cst = self.sb(top, "neghalf", [128, 8], F32)[0]
            self.neghalf, self.r_neghalf = cst
            S = self.S
            S.dma("sp", self.identf[:], self.ident_d[:, :], writes=[self.r_identf])
            S.op("dve", lambda: nc.vector.tensor_copy(out=self.identb[:], in_=self.identf[:]),
                 reads=[self.r_identf], writes=[self.r_identb])
            S.op("dve", lambda: nc.vector.memset(self.neghalf[:], -0.5), writes=[self.r_neghalf])
            import os
            kstop = int(os.environ.get("KSTOP", "99"))
            if kstop >= 1:
                self.prologue()
            for l in range(L):
                if kstop >= 2:
                    self.phase_dmask(l)
                if kstop >= 3:
                    self.phase1(l)
                if kstop >= 4:
                    self.phase2(l)
                if kstop >= 5:
                    self.phase3(l)
            S.final_wait()
        return nc

    def alloc_psum(self, st, nps, npb):
        nc = self.nc
        self.ps = [(st.enter_context(nc.psum_tensor("ps%d_%d" % (i, self.uid), [128, 512], F32)), Res()) for i in range(nps)]
        self.pb = [(st.enter_context(nc.psum_tensor("pb%d_%d" % (i, self.uid), [128, 1024], BF16)), Res()) for i in range(npb)]
        self.uid += 1
        self.ps_i = 0
        self.pb_i = 0

    def next_ps(self):
        p = self.ps[self.ps_i % len(self.ps)]
        self.ps_i += 1
        return p

    def next_pb(self):
        p = self.pb[self.pb_i % len(self.pb)]
        self.pb_i += 1
        return p

    def prologue(self):
        nc, S, L = self.nc, self.S, self.L
        with contextlib.ExitStack() as st:
            stg = self.sb(st, "cv_f", [128, 4096], F32, 3)
            stb = self.sb(st, "cv_b", [128, 4096], BF16, 3)
            k = [0]

            def conv(src, dst, shape):
                n = int(np.prod(shape))
                assert n <= 4096
                (f, rf), (b, rb) = stg[k[0] % 3], stb[k[0] % 3]
                if len(shape) == 1:
                    fv, bv = f[:, 0:n], b[:, 0:n]
                else:
                    fv = f[:, 0:n].rearrange("p (a b) -> p a b", b=shape[1])
                    bv = b[:, 0:n].rearrange("p (a b) -> p a b", b=shape[1])
                S.dma("sp", fv, src, writes=[rf])
                if k[0] % 2 == 0:
                    S.op("dve", lambda: nc.vector.tensor_copy(out=b[:, 0:n], in_=f[:, 0:n]), reads=[rf], writes=[rb])
                else:
                    S.op("act", lambda: nc.scalar.copy(out=b[:, 0:n], in_=f[:, 0:n]), reads=[rf], writes=[rb])
                S.dma("pool", dst, bv, reads=[rb])
                k[0] += 1

            for l in range(L):
                wv = self.w_in_p[l].rearrange("(c p) n -> p c n", p=128)
                for c in range(8):
                    conv(wv[:, c, :], self.win_b[l, :, c, :], [WIN_COLS])
                conv(self.w_q_p[l].rearrange("(c p) n -> p c n", p=128), self.wq_b[l], [2, 768])
                conv(self.w_kv_p[l], self.wkv_b[l], [512])
                wv = self.w_out[l].rearrange("(c p) n -> p c n", p=128)
                for c0 in range(0, 8, 4):
                    conv(wv[:, c0:c0 + 4, :], self.wout_b[l, :, c0:c0 + 4, :], [4, 1024])
                wv = self.w_up[l].rearrange("(c p) n -> p c n", p=128)
                for g in range(8):
                    conv(wv[:, :, g * 512:(g + 1) * 512], self.wup_b[l, g], [8, 512])
                wv = self.w_down[l].rearrange("(f p) n -> p f n", p=128)
                for f0 in range(0, 32, 4):
                    conv(wv[:, f0:f0 + 4, :], self.wdown_b[l, :, f0:f0 + 4, :], [4, 1024])
            for h in range(4):
                conv(self.mask_a_d[h], self.maska_b[h], [MA_LEN])
                conv(self.mask_b_d[h], self.maskb_b[h], [MB_LEN])
            S.barrier()

    def phase_dmask(self, l):
        nc, S = self.nc, self.S
        with contextlib.ExitStack() as st:
            (dc, rdc) = self.sb(st, "dm_c", [128, 2 * MD_ARR + 12 * 512], F32)[0]
            bf = self.sb(st, "dm_bf", [128, MD_ARR], F32, 2)
            ex = self.sb(st, "dm_ex", [128, MD_ARR], F32, 2)
            full = self.sb(st, "dm_full", [128, MD_ARR], F32, 2)
            mo = self.sb(st, "dm_out", [128, MD_LEN], BF16, 2)
            S.dma("sp", dc[:], self.dconst_d[:, :], writes=[rdc])
            for h in range(4):
                (b, rb), (e, re), (fu, rfu), (m, rm) = bf[h % 2], ex[h % 2], full[h % 2], mo[h % 2]
                S.dma("sp", b[:], self.biasf_d[l, h], writes=[rb])
                S.op("act", lambda: nc.scalar.activation(out=e[:], in_=b[:], func=AF.Exp), reads=[rb], writes=[re])
                S.op("dve", lambda: nc.vector.tensor_tensor(out=m[:, 0:MD_ARR], in0=e[:], in1=dc[:, 0:MD_ARR], op=ALU.mult),
                     reads=[re, rdc], writes=[rm])
                S.op("dve", lambda: nc.vector.tensor_tensor(out=fu[:], in0=e[:], in1=dc[:, MD_ARR:2 * MD_ARR], op=ALU.mult),
                     reads=[re, rdc], writes=[rfu])
                for kt in range(6):
                    for (which, dr) in ((0, 2 * kt), (1, -4 + 2 * kt)):
                        o0 = MD_ARR + (which * 6 + kt) * 512
                        a0 = (10 - dr) * 64
                        v0 = 2 * MD_ARR + (which * 6 + kt) * 512
                        S.op("dve", lambda o0=o0, a0=a0, v0=v0: nc.vector.tensor_tensor(
                            out=m[:, o0:o0 + 512], in0=fu[:, a0:a0 + 512], in1=dc[:, v0:v0 + 512], op=ALU.mult),
                            reads=[rfu, rdc], writes=[rm])
                S.dma("pool", self.maskd_b[h], m[:], reads=[rm])
            S.barrier()

    def sbm(self, st, name, shape, dt, n, nres):
        out = []
        for i in range(n):
            t = st.enter_context(self.nc.sbuf_tensor("%s_%d_%d" % (name, self.uid, i), shape, dt))
            out.append((t, [Res() for _ in range(nres)]))
        self.uid += 1
        return out

    def rstd_from_acc(self, acc, racc, out, rout, c0, n):
        nc, S = self.nc, self.S
        S.op("dve", lambda: nc.vector.tensor_scalar(out=out[:, c0:c0 + n], in0=acc[:, c0:c0 + n], scalar1=EPS, scalar2=None, op0=ALU.add),
             reads=[racc], writes=[rout])
        S.op("pool", lambda: nc.gpsimd.tensor_tensor(out=out[:, c0:c0 + n], in0=out[:, c0:c0 + n], in1=self.neghalf[:, 0:n], op=ALU.pow),
             reads=[rout, self.r_neghalf], writes=[rout])

    def evac(self, i, out, in_, reads, writes, scale=None):
        nc, S = self.nc, self.S
        import os
        kev = os.environ.get("KEV")
        if kev == "0":
            return
        if kev == "1":
            i = 1
        if kev == "2":
            i = 0
        if kev == "3" and scale is None:
            i = 1
        if kev == "4":
            i = 1 if scale is not None else 0
        if i % 2 == 0:
            if scale is None:
                S.op("act", lambda: nc.scalar.activation(out=out, in_=in_, func=AF.Copy), reads=reads, writes=writes)
            else:
                S.op("act", lambda: nc.scalar.activation(out=out, in_=in_, func=AF.Copy, scale=scale), reads=reads, writes=writes)
        else:
            if scale is None:
                S.op("dve", lambda: nc.vector.tensor_copy(out=out, in_=in_), reads=reads, writes=writes)
            else:
                S.op("dve", lambda: nc.vector.tensor_scalar(out=out, in0=in_, scalar1=scale, scalar2=None, op0=ALU.mult),
                     reads=reads, writes=writes)

    def norm_transpose(self, xt, rxt, xn, rxn, ss, rss, rs, rrs, junk, hT, rhT, gain, rgain, D=1024):
        nc, S = self.nc, self.S
        sc = float(D) ** -0.5
        for sub in range(4):
            S.op("act", lambda: nc.scalar.activation(out=xn[:, sub, :], in_=xt[:, sub, :], func=AF.Square, scale=sc,
                                                     accum_out=ss[:, sub:sub + 1]),
                 reads=[rxt[sub]], writes=[rss, rxn[sub]])
        self.rstd_from_acc(ss, rss, rs, rrs, 0, 4)
        for sub in range(4):
            S.op("dve", lambda: nc.vector.tensor_scalar(out=xn[:, sub, :], in0=xt[:, sub, :], scalar1=rs[:, sub:sub + 1],
                                                        scalar2=None, op0=ALU.mult),
                 reads=[rxt[sub], rrs], writes=[rxn[sub]])
        for c in range(D // 128):
            pb, rpb = self.next_pb()
            for sub in range(4):
                S.op("pe", lambda: nc.tensor.transpose(out=pb[:, sub * 128:(sub + 1) * 128], in_=xn[:, sub, c * 128:(c + 1) * 128],
                                                       identity=self.identb[:]),
                     reads=[rxn[sub], self.r_identb], writes=[rpb], signal=(sub == 3))
            self.evac(c, hT[:, c, :], pb[:, 0:512], [rpb, rgain], [rhT[c]], scale=gain[:, c:c + 1])

    def chk(self, n):
        import os
        if int(os.environ.get('KSUB', '99')) < n:
            raise _Stop()

    def tiles(self):
        out = []
        for si, (s0, sl) in enumerate(zip(self.starts, self.seqs)):
            for t0 in range(s0, s0 + sl, 512):
                out.append((si, s0, t0))
        return out

    def phase1(self, l):
        nc, S = self.nc, self.S
        x_src = self.x_in if l == 0 else self.xres
        with contextlib.ExitStack() as st:
            self.alloc_psum(st, 6, 2)
            (win, rwin) = self.sb(st, "p1_win", [128, 8, WIN_COLS], BF16)[0]
            (wq, rwq) = self.sb(st, "p1_wq", [128, 2, 768], BF16)[0]
            (wkv, rwkv) = self.sb(st, "p1_wkv", [128, 512], BF16)[0]
            (pc, rpc) = self.sb(st, "p1_pc", [128, NPCOL], F32)[0]
            xts = self.sbm(st, "p1_xt", [128, 4, 1024], F32, 2, 4)
            xns = self.sbm(st, "p1_xn", [128, 4, 1024], BF16, 2, 4)
            hTs = self.sbm(st, "p1_hT", [128, 8, 512], BF16, 2, 8)
            junk = None
            (junk2, _) = self.sb(st, "p1_junk2", [128, 256], BF16)[0]
            sss = self.sb(st, "p1_ss", [128, 8], F32, 2)
            rss_ = self.sb(st, "p1_rs", [128, 8], F32, 2)
            qks = self.sbm(st, "p1_qk", [128, 11, 512], BF16, 2, 11)
            vss = self.sbm(st, "p1_vs", [128, 4, NVH, 65], BF16, 2, 12)
            cns = self.sbm(st, "p1_cn", [128, 4, 384], BF16, 2, 8)
            cqTs = self.sbm(st, "p1_cqT", [128, 3, 512], BF16, 2, 3)
            ssc = self.sbm(st, "p1_ssc", [128, 8], F32, 2, 4)
            rsc = self.sbm(st, "p1_rsc", [128, 8], F32, 2, 4)
            rts = self.sb(st, "p1_rt", [128, 2, 512], F32, 2)
            tmps = self.sb(st, "p1_tmp", [128, 512], F32, 2)
            kpes = self.sb(st, "p1_kpe", [128, 512], F32, 2)
            qTs = self.sbm(st, "p1_qT", [128, 4, 512], BF16, 2, 8)
            kTs = self.sbm(st, "p1_kT", [128, 4, 512], BF16, 2, 8)

            S.dma("sp", win[:], self.win_b[l], writes=[rwin])
            S.dma("sp", wq[:], self.wq_b[l], writes=[rwq])
            S.dma("sp", wkv[:], self.wkv_b[l], writes=[rwkv])
            S.dma("sp", pc[:], self.pcol_d[l], writes=[rpc])
            for (v, rv) in vss:
                S.op("dve", lambda: nc.vector.memset(v[:], 1.0), writes=rv)

            tl = self.tiles()

            def load(i):
                si, s0, t0 = tl[i]
                xt, rxt = xts[i % 2]
                rt, rrt = rts[i % 2]
                S.dma("sp", xt[:], x_src[t0:t0 + 512, :].rearrange("(s p) d -> p s d", p=128),
                      reads=[self.r_x[t0 // 512]], writes=rxt)
                p0 = t0 - s0
                S.dma("sp", rt[64:96, :, :], self.rope_d[:, :, p0:p0 + 512].rearrange("a r t -> r a t"), writes=[rrt])

            load(0)
            tk = 0
            for i, (si, s0, t0) in enumerate(tl):
                if i + 1 < len(tl):
                    load(i + 1)
                b = i % 2
                xt, rxt = xts[b]
                xn, rxn = xns[b]
                hT, rhT = hTs[b]
                ss, rss = sss[b]
                rs, rrs = rss_[b]
                qk, rqk = qks[b]
                vs, rvs = vss[b]
                cn, rcn = cns[b]
                cqT, rcqT = cqTs[b]
                sc_, rsc_ = ssc[b]
                rc_, rrc_ = rsc[b]
                rt, rrt = rts[b]
                qT, rqT = qTs[b]
                kT, rkT = kTs[b]
                ti = t0 // 512
                try:
                  self.phase1_tile(locals())
                except _Stop:
                  break
            S.barrier()

    def phase1_tile(self, L_):
                nc, S = self.nc, self.S
                globals_ = L_
                (xt, rxt, xn, rxn, ss, rss, rs, rrs, junk, hT, rhT, pc, rpc, win, rwin, qk, rqk, t0, ti, vs, rvs, junk2, sc_, rsc_, rc_, rrc_, cn, rcn, cqT, rcqT, tmps, kpes, rt, rrt, kT, rkT, qT, rqT, wq, rwq, wkv, rwkv) = [L_[k] for k in 'xt rxt xn rxn ss rss rs rrs junk hT rhT pc rpc win rwin qk rqk t0 ti vs rvs junk2 sc_ rsc_ rc_ rrc_ cn rcn cqT rcqT tmps kpes rt rrt kT rkT qT rqT wq rwq wkv rwkv'.split()]
                tk = 0
                self.chk(1)
                self.norm_transpose(xt, rxt, xn, rxn, ss, rss, rs, rrs, junk, hT, rhT, pc[:, PC_NATT:PC_NATT + 8], rpc)
                self.chk(2)
                for j in range(11):
                    ps, rps = self.next_ps()
                    for kc in range(8):
                        S.op("pe", lambda: nc.tensor.matmul(ps[:, :], win[:, kc, j * 128:(j + 1) * 128], hT[:, kc, :],
                                                            start=(kc == 0), stop=(kc == 7)),
                             reads=[rwin, rhT[kc]], writes=[rps], signal=(kc == 7))
                    self.evac(j, qk[:, j, :], ps[:, :], [rps], [rqk[j]])
                import os
                if not os.environ.get("KNOSTORE"):
                    S.dma("sp", self.qkT[:, t0:t0 + 512].rearrange("(j p) t -> p j t", p=128), qk[:, :, :],
                          reads=rqk, writes=[self.r_qk[ti]])
                self.chk(3)
                for sub in range(4):
                    psA, rpsA = self.next_ps()
                    for kc in range(8):
                        S.op("pe", lambda: nc.tensor.matmul(psA[:, :], hT[:, kc, sub * 128:(sub + 1) * 128], win[:, kc, TM0:TM0 + 512],
                                                            start=(kc == 0), stop=(kc == 7)),
                             reads=[rwin, rhT[kc]], writes=[rpsA], signal=(kc == 7))
                    self.evac(0, vs[:, sub, 0:8, 0:64], psA[:, :].rearrange("p (h d) -> p h d", d=64), [rpsA], [rvs[sub * 3]])
                    psB, rpsB = self.next_ps()
                    for kc in range(8):
                        S.op("pe", lambda: nc.tensor.matmul(psB[:, :], hT[:, kc, sub * 128:(sub + 1) * 128], win[:, kc, TM0 + 512:TM0 + 1024],
                                                            start=(kc == 0), stop=(kc == 7)),
                             reads=[rwin, rhT[kc]], writes=[rpsB], signal=(kc == 7))
                    self.evac(0, vs[:, sub, 8:10, 0:64], psB[:, 0:128].rearrange("p (h d) -> p h d", d=64), [rpsB], [rvs[sub * 3 + 1]])
                    S.op("act", lambda: nc.scalar.activation(out=junk2[:, 0:256], in_=psB[:, 128:384], func=AF.Square, scale=1.0 / 16.0,
                                                             accum_out=sc_[:, 2 * sub:2 * sub + 1]),
                         reads=[rpsB], writes=[rsc_[sub]])
                    S.op("act", lambda: nc.scalar.activation(out=junk2[:, 0:128], in_=psB[:, 384:512], func=AF.Square, scale=128.0 ** -0.5,
                                                             accum_out=sc_[:, 2 * sub + 1:2 * sub + 2]),
                         reads=[rpsB], writes=[rsc_[sub]])
                    self.rstd_from_acc(sc_, rsc_[sub], rc_, rrc_[sub], 2 * sub, 2)
                    S.op("dve", lambda: nc.vector.tensor_scalar(out=cn[:, sub, 0:256], in0=psB[:, 128:384], scalar1=rc_[:, 2 * sub:2 * sub + 1],
                                                                scalar2=None, op0=ALU.mult),
                         reads=[rpsB, rrc_[sub]], writes=[rcn[2 * sub]])
                    S.op("dve", lambda: nc.vector.tensor_scalar(out=cn[:, sub, 256:384], in0=psB[:, 384:512], scalar1=rc_[:, 2 * sub + 1:2 * sub + 2],
                                                                scalar2=None, op0=ALU.mult),
                         reads=[rpsB, rrc_[sub]], writes=[rcn[2 * sub + 1]])
                self.chk(4)
                for c in range(3):
                    pb, rpb = self.next_pb()
                    for sub in range(4):
                        S.op("pe", lambda: nc.tensor.transpose(out=pb[:, sub * 128:(sub + 1) * 128], in_=cn[:, sub, c * 128:(c + 1) * 128],
                                                               identity=self.identb[:]),
                             reads=[rcn[2 * sub + (1 if c == 2 else 0)], self.r_identb], writes=[rpb], signal=(sub == 3))
                    self.evac(c, cqT[:, c, :], pb[:, 0:512], [rpb, rpc], [rcqT[c]], scale=pc[:, PC_QN + c:PC_QN + c + 1])
                self.chk(5)
                tmp, rtmp = tmps[tk % 2]
                kpe, rkpe = kpes[tk % 2]
                tk += 1
                ps1, rps1 = self.next_ps()
                for kc in range(8):
                    S.op("pe", lambda: nc.tensor.matmul(ps1[0:96, :], win[:, kc, KR0:KR0 + 96], hT[:, kc, :], start=(kc == 0), stop=(kc == 7)),
                         reads=[rwin, rhT[kc]], writes=[rps1], signal=(kc == 7))
                ps2, rps2 = self.next_ps()
                for kc in range(8):
                    S.op("pe", lambda: nc.tensor.matmul(ps2[0:96, :], win[:, kc, KR0 + 96:KR0 + 192], hT[:, kc, :], start=(kc == 0), stop=(kc == 7)),
                         reads=[rwin, rhT[kc]], writes=[rps2], signal=(kc == 7))
                S.op("dve", lambda: nc.vector.tensor_tensor(out=tmp[64:96, :], in0=ps2[64:96, :], in1=rt[64:96, 1, :], op=ALU.mult),
                     reads=[rps2, rrt], writes=[rtmp])
                S.op("dve", lambda: nc.vector.tensor_tensor(out=kpe[64:96, :], in0=ps1[64:96, :], in1=rt[64:96, 0, :], op=ALU.mult),
                     reads=[rps1, rrt], writes=[rkpe])
                for h in range(4):
                    S.op("pool", lambda: nc.gpsimd.tensor_tensor(out=kT[64:96, h, :], in0=kpe[64:96, :], in1=tmp[64:96, :], op=ALU.add),
                         reads=[rkpe, rtmp], writes=[rkT[2 * h + 1]])
                self.chk(6)
                for h in range(4):
                    tmp, rtmp = tmps[tk % 2]
                    kpe, rkpe = kpes[tk % 2]
                    tk += 1
                    pq1, rpq1 = self.next_ps()
                    for kc in range(2):
                        S.op("pe", lambda: nc.tensor.matmul(pq1[0:96, :], wq[:, kc, h * 192:h * 192 + 96], cqT[:, kc, :], start=(kc == 0), stop=(kc == 1)),
                             reads=[rwq, rcqT[kc]], writes=[rpq1], signal=(kc == 1))
                    pq2, rpq2 = self.next_ps()
                    for kc in range(2):
                        S.op("pe", lambda: nc.tensor.matmul(pq2[0:96, :], wq[:, kc, h * 192 + 96:h * 192 + 192], cqT[:, kc, :], start=(kc == 0), stop=(kc == 1)),
                             reads=[rwq, rcqT[kc]], writes=[rpq2], signal=(kc == 1))
                    S.op("act", lambda: nc.scalar.copy(out=qT[0:64, h, :], in_=pq1[0:64, :]), reads=[rpq1], writes=[rqT[2 * h]])
                    S.op("dve", lambda: nc.vector.tensor_tensor(out=tmp[64:96, :], in0=pq2[64:96, :], in1=rt[64:96, 1, :], op=ALU.mult),
                         reads=[rpq2, rrt], writes=[rtmp])
                    S.op("dve", lambda: nc.vector.tensor_tensor(out=kpe[64:96, :], in0=pq1[64:96, :], in1=rt[64:96, 0, :], op=ALU.mult),
                         reads=[rpq1, rrt], writes=[rkpe])
                    S.op("pool", lambda: nc.gpsimd.tensor_tensor(out=qT[64:96, h, :], in0=kpe[64:96, :], in1=tmp[64:96, :], op=ALU.add),
                         reads=[rkpe, rtmp], writes=[rqT[2 * h + 1]])
                    pk, rpk = self.next_ps()
                    S.op("pe", lambda: nc.tensor.matmul(pk[0:64, :], wkv[:, h * 64:(h + 1) * 64], cqT[:, 2, :], start=True, stop=True),
                         reads=[rwkv, rcqT[2]], writes=[rpk])
                    self.evac(h, kT[0:64, h, :], pk[0:64, :], [rpk], [rkT[2 * h]])
                for sub in range(4):
                    pv, rpv = self.next_ps()
                    S.op("pe", lambda: nc.tensor.matmul(pv[:, 0:256], cqT[:, 2, sub * 128:(sub + 1) * 128], wkv[:, 256:512], start=True, stop=True),
                         reads=[rwkv, rcqT[2]], writes=[rpv])
                    self.evac(0, vs[:, sub, 10:14, 0:64], pv[:, 0:256].rearrange("p (h d) -> p h d", d=64), [rpv], [rvs[sub * 3 + 2]])
                self.chk(7)
                S.dma("sp", self.qTc[:, :, t0:t0 + 512].rearrange("h r t -> r h t"), qT[0:96, :, :], reads=rqT, writes=[self.r_qc[ti]])
                S.dma("sp", self.kTc[:, :, t0:t0 + 512].rearrange("h r t -> r h t"), kT[0:96, :, :], reads=rkT, writes=[self.r_kc[ti]])
                S.dma("sp", self.vsc[t0:t0 + 512, :, :].rearrange("(s p) h e -> p s h e", p=128), vs[:, :, :, :],
                      reads=rvs, writes=[self.r_v[ti]])


    def phase2(self, l):
        nc, S = self.nc, self.S
        SPAN = 2048
        SMAXK = self.smax
        with contextlib.ExitStack() as st:
            scs = [(st.enter_context(nc.psum_tensor("p2sc%d_%d" % (i, self.uid), [128, 1024], F32)), Res()) for i in range(2)]
            self.alloc_psum(st, 3, 1)
            o_banks = self.ps[0:2]
            tp_bank = self.ps[2]
            Ks = self.sb(st, "p2_K", [96, SMAXK], BF16, 2)
            Vs = self.sb(st, "p2_V", [128, SMAXK // 128, 65], BF16, 2)
            Ms = self.sb(st, "p2_M", [128, MASK_MAX], BF16, 2)
            Qs = self.sb(st, "p2_Q", [96, 512], BF16, 3)
            Ps = self.sb(st, "p2_P", [128, 1024], BF16, 4)
            P2s = self.sb(st, "p2_P2", [128, 1024], BF16, 4)
            osbs = self.sb(st, "p2_osb", [65, 512], F32, 2)
            (ogrp, rogrp) = self.sbm(st, "p2_ogrp", [128, SPAN // 128, 256], F32, 1, SPAN // 512)[0]
            (pc, rpc) = self.sb(st, "p2_pc", [128, NPCOL], F32)[0]
            (esink, resink) = self.sb(st, "p2_esink", [128, 4], F32)[0]
            (junk, _) = self.sb(st, "p2_junk", [128, 256], BF16)[0]
            gss = self.sb(st, "p2_gss", [128, 4], F32, 2)
            grs = self.sb(st, "p2_grs", [128, 4], F32, 2)
            recs = self.sb(st, "p2_rec", [128, 4], F32, 2)
            ons = self.sbm(st, "p2_on", [128, 4, 256], BF16, 2, 4)
            onTs = self.sbm(st, "p2_onT", [128, 2, 512], BF16, 2, 2)
            S.dma("sp", pc[:], self.pcol_d[l], writes=[rpc])
            S.op("act", lambda: nc.scalar.activation(out=esink[:], in_=pc[:, PC_SINK:PC_SINK + 4], func=AF.Exp),
                 reads=[rpc], writes=[resink])
            ctr = {"kv": 0, "q": 0, "p": 0, "sc": 0, "ob": 0, "gn": 0, "m": 0}

            mixers = [
                ("A", 4, 64, 0.125, ROW_AQ, ROW_AK, 0, 1),
                ("B", 4, 64, 0.125, ROW_BQ, ROW_BK, 4, 2),
                ("C", 4, 96, 96.0 ** -0.5, None, None, 10, 1),
                ("D", 4, 64, 0.125, ROW_DQ, ROW_DK, 6, 1),
            ]

            for si, (s0, sl) in enumerate(zip(self.starts, self.seqs)):
                for sp0 in range(0, sl, SPAN):
                    span = min(SPAN, sl - sp0)
                    nq = span // 512
                    for mi, (mname, H, dqk, scale, qrow, krow, vbase, grp) in enumerate(mixers):
                        for h in range(H):
                            kvh = h // grp
                            K, rK = Ks[ctr["kv"] % 2]
                            V, rV = Vs[ctr["kv"] % 2]
                            ctr["kv"] += 1
                            if mname == "A":
                                wlo, whi = sp0 - 1024, sp0 + span + 1024
                            elif mname == "B":
                                wlo, whi = sp0 - 128, sp0 + span + 128
                            elif mname == "C":
                                wlo, whi = 0, sl
                            else:
                                wlo, whi = sp0 - 256, sp0 + span + 256
                            wlo, whi = max(wlo, 0), min(whi, sl)
                            wlen = whi - wlo
                            kt_base = wlo // 128
                            tiles_rng = range((s0 + wlo) // 512, (s0 + whi + 511) // 512)
                            if krow is not None:
                                ksrc = self.qkT[krow + kvh * 64:krow + kvh * 64 + 64, s0 + wlo:s0 + whi]
                                rdep = [self.r_qk[t] for t in tiles_rng]
                            else:
                                ksrc = self.kTc[h, :, s0 + wlo:s0 + whi]
                                rdep = [self.r_kc[t] for t in tiles_rng]
                            S.dma("sp", K[0:dqk, 0:wlen], ksrc, reads=rdep, writes=[rK])
                            S.dma("sp", V[:, 0:wlen // 128, :],
                                  self.vsc[s0 + wlo:s0 + whi, vbase + kvh, :].rearrange("(k p) e -> p k e", p=128),
                                  reads=[self.r_v[t] for t in tiles_rng], writes=[rV])
                            M = rM = None
                            if mname != "C":
                                M, rM = Ms[ctr["m"] % 2]
                                ctr["m"] += 1
                                if mname == "A":
                                    S.dma("sp", M[:, 0:MA_LEN], self.maska_b[h], writes=[rM])
                                elif mname == "B":
                                    S.dma("sp", M[:, 0:MB_LEN], self.maskb_b[h], writes=[rM])
                                else:
                                    S.dma("sp", M[:, 0:MD_LEN], self.maskd_b[h], writes=[rM])
                            units = []
                            for qi in range(nq):
                                q0 = sp0 + qi * 512
                                if mname == "A":
                                    lo, hi = q0 - 1024, q0 + 512 + 1024
                                elif mname == "B":
                                    lo, hi = q0 - 128, q0 + 512 + 128
                                elif mname == "C":
                                    lo, hi = 0, sl
                                else:
                                    lo, hi = q0 - 256, q0 + 768
                                lo, hi = max(lo, 0), min(hi, sl)
                                kts = list(range(lo // 128, hi // 128))
                                for a in range(0, len(kts), 2):
                                    units.append((qi, q0, kts[a:a + 2], a == 0, a + 2 >= len(kts)))
                            state = {}
                            deferred = []

                            def mask_ap(q0, kt):
                                dlt = kt * 128 - q0
                                if mname == "A":
                                    off = 1408 - dlt
                                elif mname == "B":
                                    off = 512 - dlt
                                else:
                                    rows = sl // 64
                                    R0 = q0 // 64
                                    if R0 == 0:
                                        off = MD_ARR + (dlt // 128) * 512
                                    elif R0 == rows - 8:
                                        off = MD_ARR + (6 + (dlt + 256) // 128) * 512
                                    else:
                                        off = 640 - dlt
                                return M[:, off:off + 512]

                            def emit_S(i):
                                qi, q0, kts, first, last = units[i]
                                nk = len(kts)
                                if first:
                                    Q, rQ = Qs[ctr["q"] % 3]
                                    ctr["q"] += 1
                                    t0 = s0 + q0
                                    if qrow is not None:
                                        qsrc = self.qkT[qrow + h * 64:qrow + h * 64 + 64, t0:t0 + 512]
                                        rd = [self.r_qk[t0 // 512]]
                                    else:
                                        qsrc = self.qTc[h, :, t0:t0 + 512]
                                        rd = [self.r_qc[t0 // 512]]
                                    S.dma("sp", Q[0:dqk, :], qsrc, reads=rd, writes=[rQ])
                                    state[qi] = (Q, rQ)
                                Q, rQ = state[qi]
                                sc, rsc = scs[ctr["sc"] % 2]
                                ctr["sc"] += 1
                                if DUMMY_COLS > 0:
                                    kt0 = kts[0]
                                    S.op("pe", lambda: nc.tensor.matmul(sc[:, 0:DUMMY_COLS], K[0:dqk, (kt0 - kt_base) * 128:(kt0 - kt_base + 1) * 128], Q[0:dqk, 0:DUMMY_COLS],
                                                                        start=True, stop=True),
                                         reads=[rK, rQ], writes=[rsc], signal=False)
                                for a, kt in enumerate(kts):
                                    S.op("pe", lambda: nc.tensor.matmul(sc[:, a * 512:(a + 1) * 512], K[0:dqk, (kt - kt_base) * 128:(kt - kt_base + 1) * 128], Q[0:dqk, :],
                                                                        start=True, stop=True),
                                         reads=[rK, rQ], writes=[rsc], signal=(a == nk - 1))
                                P, rP = Ps[ctr["p"] % 4]
                                S.op("act", lambda: nc.scalar.activation(out=P[:, 0:nk * 512], in_=sc[:, 0:nk * 512], func=AF.Exp, scale=scale),
                                     reads=[rsc], writes=[rP])
                                if M is not None:
                                    P2, rP2 = P2s[ctr["p"] % 4]
                                    for a, kt in enumerate(kts):
                                        mk = mask_ap(q0, kt)
                                        S.op("dve", lambda: nc.vector.tensor_tensor(out=P2[:, a * 512:(a + 1) * 512], in0=P[:, a * 512:(a + 1) * 512],
                                                                                    in1=mk, op=ALU.mult),
                                             reads=[rP, rM], writes=[rP2])
                                    state[("p", i)] = (P2, rP2)
                                else:
                                    state[("p", i)] = (P, rP)
                                ctr["p"] += 1

                            def emit_PV(j):
                                qi, q0, kts, first, last = units[j]
                                nk = len(kts)
                                P, rP = state.pop(("p", j))
                                if first:
                                    ob, rob = o_banks[ctr["ob"] % 2]
                                    ctr["ob"] += 1
                                    state[("o", qi)] = (ob, rob)
                                ob, rob = state[("o", qi)]
                                for a, kt in enumerate(kts):
                                    S.op("pe", lambda: nc.tensor.matmul(ob[0:65, :], V[:, kt - kt_base, :], P[:, a * 512:(a + 1) * 512],
                                                                        start=(first and a == 0), stop=(last and a == nk - 1)),
                                         reads=[rV, rP], writes=[rob], signal=(a == nk - 1))
                                if last:
                                    osb, rosb = osbs[qi % 2]
                                    S.op("dve", lambda: nc.vector.tensor_copy(out=osb[:, :], in_=ob[0:65, :]), reads=[rob], writes=[rosb])
                                    deferred.append((qi, osb, rosb))

                            def emit_epilogue(qi, osb, rosb):
                                tp, rtp = tp_bank
                                for sub in range(4):
                                    S.op("pe", lambda: nc.tensor.transpose(out=tp[:, sub * 65:(sub + 1) * 65], in_=osb[0:65, sub * 128:(sub + 1) * 128],
                                                                           identity=self.identf[0:65, 0:65]),
                                         reads=[rosb, self.r_identf], writes=[rtp], signal=(sub == 3))
                                rec, rrec = recs[qi % 2]
                                den = tp[:, 0:260].rearrange("p (s e) -> p s e", e=65)[:, :, 64]
                                if mname == "B":
                                    S.op("dve", lambda: nc.vector.tensor_scalar(out=rec[:, 0:4], in0=den, scalar1=esink[:, h:h + 1], scalar2=None, op0=ALU.add),
                                         reads=[rtp, resink], writes=[rrec])
                                    S.op("dve", lambda: nc.vector.reciprocal(out=rec[:, 0:4], in_=rec[:, 0:4]), reads=[rrec], writes=[rrec])
                                else:
                                    S.op("dve", lambda: nc.vector.reciprocal(out=rec[:, 0:4], in_=den), reads=[rtp], writes=[rrec])
                                for sub in range(4):
                                    S.op("dve", lambda: nc.vector.tensor_scalar(out=ogrp[:, qi * 4 + sub, h * 64:(h + 1) * 64],
                                                                                in0=tp[:, sub * 65:sub * 65 + 64], scalar1=rec[:, sub:sub + 1],
                                                                                scalar2=None, op0=ALU.mult),
                                         reads=[rtp, rrec], writes=[rogrp[qi]])

                            LOOK = 2
                            n = len(units)
                            for i in range(n + LOOK):
                                if i < n:
                                    emit_S(i)
                                dl = deferred
                                deferred = []
                                for (a_, b_, c_) in dl:
                                    emit_epilogue(a_, b_, c_)
                                if i - LOOK >= 0:
                                    emit_PV(i - LOOK)
                            for (a_, b_, c_) in deferred:
                                emit_epilogue(a_, b_, c_)
                            deferred = []
                        for qi in range(nq):
                            t0 = s0 + sp0 + qi * 512
                            g = ctr["gn"] % 2
                            ctr["gn"] += 1
                            gs, rgs = gss[g]
                            gr, rgr = grs[g]
                            on, ron = ons[g]
                            onT, ronT = onTs[g]
                            for sub in range(4):
                                S.op("act", lambda: nc.scalar.activation(out=junk[:, :], in_=ogrp[:, qi * 4 + sub, :], func=AF.Square, scale=1.0 / 16.0,
                                                                         accum_out=gs[:, sub:sub + 1]),
                                     reads=[rogrp[qi]], writes=[rgs])
                            self.rstd_from_acc(gs, rgs, gr, rgr, 0, 4)
                            for sub in range(4):
                                S.op("dve", lambda: nc.vector.tensor_scalar(out=on[:, sub, :], in0=ogrp[:, qi * 4 + sub, :], scalar1=gr[:, sub:sub + 1],
                                                                            scalar2=None, op0=ALU.mult),
                                     reads=[rogrp[qi], rgr], writes=[ron[sub]])
                            for c in range(2):
                                pb, rpb = self.next_pb()
                                for sub in range(4):
                                    S.op("pe", lambda: nc.tensor.transpose(out=pb[:, sub * 128:(sub + 1) * 128], in_=on[:, sub, c * 128:(c + 1) * 128],
                                                                           identity=self.identb[:]),
                                         reads=[ron[sub], self.r_identb], writes=[rpb], signal=(sub == 3))
                                gc = PC_GN + mi * 2 + c
                                S.op("dve", lambda: nc.vector.tensor_scalar(out=onT[:, c, :], in0=pb[:, 0:512], scalar1=pc[:, gc:gc + 1], scalar2=None, op0=ALU.mult),
                                     reads=[rpb, rpc], writes=[ronT[c]])
                            S.dma("pool", self.onT[mi * 256:(mi + 1) * 256, t0:t0 + 512].rearrange("(c p) t -> p c t", p=128), onT[:, :, :],
                                  reads=ronT, writes=[self.r_on[t0 // 512]])
            S.barrier()

    def phase3(self, l):
        nc, S = self.nc, self.S
        last = (l == self.L - 1)
        x_src = self.x_in if l == 0 else self.xres
        with contextlib.ExitStack() as st:
            self.alloc_psum(st, 6, 2)
            (wo, rwo) = self.sb(st, "p3_wo", [128, 8, 1024], BF16)[0]
            (wd, rwd) = self.sb(st, "p3_wd", [128, 32, 1024], BF16)[0]
            wus = self.sb(st, "p3_wu", [128, 8, 512], BF16, 2)
            (pc, rpc) = self.sb(st, "p3_pc", [128, NPCOL], F32)[0]
            (nf, rnf) = self.sb(st, "p3_nf", [128, 1024], F32)[0]
            xts = self.sbm(st, "p3_xt", [128, 4, 1024], F32, 2, 4)
            (xn, rxn) = self.sbm(st, "p3_xn", [128, 4, 1024], BF16, 1, 4)[0]
            onTs = self.sb(st, "p3_onT", [128, 8, 512], BF16, 2)
            (hT, rhT) = self.sbm(st, "p3_hT", [128, 8, 512], BF16, 1, 8)[0]
            (uT, ruT) = self.sbm(st, "p3_uT", [128, 32, 512], BF16, 1, 32)[0]
            rls = self.sb(st, "p3_rl", [128, 512], BF16, 3)
            junk = None
            sss = self.sb(st, "p3_ss", [128, 8], F32, 2)
            rss_ = self.sb(st, "p3_rs", [128, 8], F32, 2)
            S.dma("sp", wo[:], self.wout_b[l], writes=[rwo])
            S.dma("sp", pc[:], self.pcol_d[l], writes=[rpc])
            S.dma("sp", nf[:], self.nfin_d[:, :], writes=[rnf])
            S.dma("sp", wd[:], self.wdown_b[l], writes=[rwd])
            tl = self.tiles()
            wu_ctr = [0]

            def load(i):
                si, s0, t0 = tl[i]
                xt, rxt = xts[i % 2]
                oT, roT = onTs[i % 2]
                S.dma("sp", xt[:], x_src[t0:t0 + 512, :].rearrange("(s p) d -> p s d", p=128),
                      reads=[self.r_x[t0 // 512]], writes=rxt)
                S.dma("sp", oT[:], self.onT[:, t0:t0 + 512].rearrange("(c p) t -> p c t", p=128),
                      reads=[self.r_on[t0 // 512]], writes=[roT])

            def load_wu(g):
                wu, rwu = wus[wu_ctr[0] % 2]
                wu_ctr[0] += 1
                S.dma("sp", wu[:], self.wup_b[l, g], writes=[rwu])
                return wu, rwu

            load(0)
            for i, (si, s0, t0) in enumerate(tl):
                ti = t0 // 512
                xt, rxt = xts[i % 2]
                oT, roT = onTs[i % 2]
                ss, rss = sss[i % 2]
                rs, rrs = rss_[i % 2]
                wu_next = load_wu(0)
                k = 0
                for sub in range(4):
                    for half in range(2):
                        ps, rps = self.next_ps()
                        for c in range(8):
                            S.op("pe", lambda: nc.tensor.matmul(ps[:, :], oT[:, c, sub * 128:(sub + 1) * 128], wo[:, c, half * 512:(half + 1) * 512],
                                                                start=(c == 0), stop=(c == 7)),
                                 reads=[roT, rwo], writes=[rps], signal=(c == 7))
                        S.op("dve", lambda: nc.vector.tensor_tensor(out=xt[:, sub, half * 512:(half + 1) * 512], in0=ps[:, :],
                                                                    in1=xt[:, sub, half * 512:(half + 1) * 512], op=ALU.add),
                             reads=[rps, rxt[sub]], writes=[rxt[sub]])
                        k += 1
                if i + 1 < len(tl):
                    load(i + 1)
                self.norm_transpose(xt, rxt, xn, rxn, ss, rss, rs, rrs, junk, hT, rhT, pc[:, PC_NMLP:PC_NMLP + 8], rpc)
                for g in range(8):
                    wu, rwu = wu_next
                    if g + 1 < 8:
                        wu_next = load_wu(g + 1)
                    for f in range(4):
                        fc = g * 4 + f
                        ps, rps = self.next_ps()
                        for c in range(8):
                            S.op("pe", lambda: nc.tensor.matmul(ps[:, :], wu[:, c, f * 128:(f + 1) * 128], hT[:, c, :],
                                                                start=(c == 0), stop=(c == 7)),
                                 reads=[rwu, rhT[c]], writes=[rps], signal=(c == 7))
                        rl, rrl = rls[fc % 3]
                        S.op("act", lambda: nc.scalar.activation(out=rl[:, :], in_=ps[:, :], func=AF.Relu), reads=[rps], writes=[rrl])
                        S.op("dve", lambda: nc.vector.tensor_tensor(out=uT[:, fc, :], in0=rl[:, :], in1=rl[:, :], op=ALU.mult),
                             reads=[rrl], writes=[ruT[fc]])
                for sub in range(4):
                    for half in range(2):
                        ps, rps = self.next_ps()
                        for fc in range(32):
                            S.op("pe", lambda: nc.tensor.matmul(ps[:, :], uT[:, fc, sub * 128:(sub + 1) * 128], wd[:, fc, half * 512:(half + 1) * 512],
                                                                start=(fc == 0), stop=(fc == 31)),
                                 reads=[ruT[fc], rwd], writes=[rps], signal=(fc == 31))
                        S.op("dve", lambda: nc.vector.tensor_tensor(out=xt[:, sub, half * 512:(half + 1) * 512], in0=ps[:, :],
                                                                    in1=xt[:, sub, half * 512:(half + 1) * 512], op=ALU.add),
                             reads=[rps, rxt[sub]], writes=[rxt[sub]])
                if not last:
                    S.dma("sp", self.xres[t0:t0 + 512, :].rearrange("(s p) d -> p s d", p=128), xt[:],
                          reads=rxt, writes=[self.r_x[ti]])
                else:
                    for sub in range(4):
                        S.op("act", lambda: nc.scalar.activation(out=xn[:, sub, :], in_=xt[:, sub, :], func=AF.Square, scale=1.0 / 32.0,
                                                                 accum_out=ss[:, 4 + sub:5 + sub]),
                             reads=[rxt[sub]], writes=[rss, rxn[sub]])
                    self.rstd_from_acc(ss, rss, rs, rrs, 4, 4)
                    for sub in range(4):
                        S.op("dve", lambda: nc.vector.scalar_tensor_tensor(out=xt[:, sub, :], in0=xt[:, sub, :], scalar=rs[:, 4 + sub:5 + sub],
                                                                           in1=nf[:, :], op0=ALU.mult, op1=ALU.mult),
                             reads=[rxt[sub], rrs, rnf], writes=[rxt[sub]])
                    S.dma("sp", self.y_out[t0:t0 + 512, :].rearrange("(s p) d -> p s d", p=128), xt[:], reads=rxt)
            S.barrier()


_PROG_CACHE = {}


def get_prog(seqs, depth):
    key = (tuple(seqs), depth)
    if key not in _PROG_CACHE:
        p = Prog(seqs, depth)
        p.build()
        _PROG_CACHE[key] = p
    return _PROG_CACHE[key]


def run_cores(x_list, shared, seqs, depth, core_ids):
    prog = get_prog(seqs, depth)
    shared = dict(shared)
    shared["rope"] = _rope_tables(max(seqs))
    in_maps = []
    for x in x_list:
        m = dict(shared)
        m["x"] = np.ascontiguousarray(x, dtype=np.float32)
        in_maps.append(m)
    res = run_bass_kernel_spmd(prog.nc, in_maps, core_ids=core_ids)
    return [r["y"] for r in res.results]


def kernel(**inputs):
    xp = np.asarray(inputs["x_prompt"], np.float32)
    xs = np.asarray(inputs["x_sample"], np.float32)
    shared = host_prep(inputs, DEPTH)
    seqs = [2048, 2048, 16384]
    x_list = []
    zeros = np.zeros_like(xs[0])
    for c in range(N_CORES):
        samp = xs[c // 4] if c % 4 == 0 else zeros
        x_list.append(np.concatenate([xp[2 * c], xp[2 * c + 1], samp], axis=0))
    ys = run_cores(x_list, shared, seqs, DEPTH, list(range(N_CORES)))
    y_prompt = np.empty_like(xp)
    y_sample = np.empty_like(xs)
    for c in range(N_CORES):
        y_prompt[2 * c] = ys[c][0:2048]
        y_prompt[2 * c + 1] = ys[c][2048:4096]
    y_sample[0] = ys[0][4096:]
    y_sample[1] = ys[4][4096:]
    return (y_prompt, y_sample)
```

```python
import contextlib
import numpy as np
import concourse.bass as bass
import concourse.mybir as mybir
from concourse.bass_utils import run_bass_kernel_spmd

F32 = mybir.dt.float32
BF16 = mybir.dt.bfloat16
AF = mybir.ActivationFunctionType
ALU = mybir.AluOpType

D_MODEL = 1024
DEPTH = 4
D_FF = 4096
EPS = 1e-5
N_CORES = 8
FM_COLS = 1408
TM0 = 1408
KR0 = 2432
WIN_COLS = 2624
NPCOL = 40
PC_NATT, PC_NMLP, PC_GN, PC_QN, PC_KVN, PC_SINK = 0, 8, 16, 24, 26, 27
MA_LEN, MB_LEN, MD_ARR = 2944, 1152, 1408
MD_LEN = MD_ARR + 12 * 512
MASK_MAX = MD_LEN
ROW_AQ, ROW_AK, ROW_BQ, ROW_BK, ROW_DQ, ROW_DK = 0, 256, 512, 768, 896, 1152
NVH = 14
import os as _os
DUMMY_COLS = int(_os.environ.get("KDUM", "0"))


class _Stop(Exception):
    pass


class Res:
    __slots__ = ("w", "r")

    def __init__(self):
        self.w = {}
        self.r = {}


class Sched:
    def __init__(self, nc, stack, n_dma_sems=12):
        self.nc = nc
        self.eng = {"pe": nc.tensor, "act": nc.scalar, "dve": nc.vector, "pool": nc.gpsimd, "sp": nc.sync}
        self.names = list(self.eng.keys())
        self.sems = {}
        for n in self.names:
            self.sems[n] = stack.enter_context(nc.semaphore("s_" + n))
        self.cnt = {n: 0 for n in self.names}
        self.dq = {}
        for q in ("sp", "pool"):
            lst = []
            for i in range(n_dma_sems):
                key = "d_%s_%d" % (q, i)
                self.sems[key] = stack.enter_context(nc.semaphore(key))
                self.cnt[key] = 0
                lst.append(key)
            self.dq[q] = [lst, 0]
        self.waited = {n: {} for n in self.names}
        self.pending = {n: [] for n in self.names}
        self.nops = 0

    def _needs(self, X, reads, writes):
        need = {}
        for r in reads:
            for k, v in r.w.items():
                if need.get(k, 0) < v:
                    need[k] = v
        for w in writes:
            for k, v in w.w.items():
                if need.get(k, 0) < v:
                    need[k] = v
            for k, v in w.r.items():
                if need.get(k, 0) < v:
                    need[k] = v
        return need

    def _emit_waits(self, X, need):
        wd = self.waited[X]
        e = self.eng[X]
        for k, v in need.items():
            if k == X and X == "pe":
                continue
            if wd.get(k, 0) < v:
                e.wait_ge(self.sems[k], v)
                wd[k] = v

    def _commit(self, ev_key, ev_val, reads, writes):
        for r in reads:
            if r.r.get(ev_key, 0) < ev_val:
                r.r[ev_key] = ev_val
        for w in writes:
            w.w = {ev_key: ev_val}
            w.r = {}

    def op(self, X, fn, reads=(), writes=(), signal=True):
        need = self._needs(X, reads, writes)
        self._emit_waits(X, need)
        inst = fn()
        self.nops += 1
        if signal:
            self.cnt[X] += 1
            inst.then_inc(self.sems[X], 1)
            c = self.cnt[X]
            for (rr, ww) in self.pending[X]:
                self._commit(X, c, rr, ww)
            self.pending[X] = []
            self._commit(X, c, reads, writes)
        else:
            self.pending[X].append((reads, writes))
        return inst

    def dma(self, q, out, in_, reads=(), writes=(), **kw):
        assert not self.pending[q]
        need = self._needs(q, reads, writes)
        lst, idx = self.dq[q]
        key = lst[idx % len(lst)]
        self.dq[q][1] = idx + 1
        if self.cnt[key] > 0:
            need[key] = max(need.get(key, 0), self.cnt[key])
        self._emit_waits(q, need)
        inst = self.eng[q].dma_start(out=out, in_=in_, **kw)
        self.nops += 1
        self.cnt[key] += 16
        inst.then_inc(self.sems[key], 16)
        self._commit(key, self.cnt[key], reads, writes)
        return inst

    def barrier(self):
        for n in self.names:
            assert not self.pending[n], n
        allev = {k: v for k, v in self.cnt.items() if v > 0}
        for n in self.names:
            need = {k: v for k, v in allev.items() if k != n}
            wd = self.waited[n]
            for k, v in need.items():
                if wd.get(k, 0) < v:
                    self.eng[n].wait_ge(self.sems[k], v)
                    wd[k] = v

    def final_wait(self):
        self.barrier()


def _alibi_slopes():
    s = np.exp2(-8.0 * np.arange(1, 9, dtype=np.float64) / 8.0)
    return s[1::2], s[0::2]


def _mask_a():
    sa, _ = _alibi_slopes()
    d = np.arange(-4096, 4097)
    ad = np.abs(d)
    m = ((ad <= 64).astype(np.float64) + ((d % 4 == 0) & (ad <= 256)) + ((d % 16 == 0) & (ad <= 1024)))
    out = np.zeros((4, 128, MA_LEN), np.float32)
    i = np.arange(128)[:, None]
    n = np.arange(MA_LEN)[None, :]
    dd = i + 1408 - n
    for h in range(4):
        f = m * np.exp(-sa[h] * ad)
        out[h] = f[dd + 4096]
    return out


def _mask_b():
    _, sb = _alibi_slopes()
    out = np.zeros((4, 128, MB_LEN), np.float32)
    i = np.arange(128)[:, None]
    n = np.arange(MB_LEN)[None, :]
    dd = i + 512 - n
    ad = np.abs(dd)
    for h in range(4):
        out[h] = np.where(ad <= 128, np.exp(-sb[h] * ad), 0.0)
    return out


def _d_tables():
    i = np.arange(128)
    rk_l = (i // 64)[:, None, None]
    ck = (i % 64)[:, None, None]
    n = np.arange(22)[None, :, None]
    cq = np.arange(64)[None, None, :]
    delta = rk_l + 10 - n + 0 * cq
    dr_idx = np.clip(delta, -7, 7) + 7
    dc_idx = np.clip(ck - cq, -15, 15) + 15 + 0 * n
    cs = np.clip(cq - 8, 0, 64 - 16)
    colmask = ((ck >= cs) & (ck < cs + 16)) & (n >= 0)
    cw = (colmask & (delta >= -4) & (delta <= 3)).astype(np.float32).reshape(128, MD_ARR)
    cf = (colmask & (delta >= -7) & (delta <= 7)).astype(np.float32).reshape(128, MD_ARR)
    vb = np.zeros((12, 128, 512), np.float32)
    rkl = (i // 64)[:, None]
    rql = (np.arange(512) // 64)[None, :]
    for kt in range(6):
        rk = 2 * kt + rkl
        rs = np.maximum(rql - 4, 0)
        vb[kt] = ((rk >= rs) & (rk < rs + 8)).astype(np.float32)
        rk2 = -4 + 2 * kt + rkl
        rs2 = np.minimum(rql - 4, 0)
        vb[6 + kt] = ((rk2 >= rs2) & (rk2 < rs2 + 8)).astype(np.float32)
    return dr_idx.reshape(128, MD_ARR), dc_idx.reshape(128, MD_ARR), cw, cf, vb


def _rope_tables(smax):
    inv_freq = (1.0 / (np.float32(10000.0) ** (np.arange(0, 32, 2, dtype=np.float32) / np.float32(32)))).astype(np.float32)
    ang = (np.arange(smax, dtype=np.float32)[:, None] * inv_freq[None, :]).astype(np.float32)
    c = np.cos(ang.astype(np.float64)).astype(np.float32).T
    s = np.sin(ang.astype(np.float64)).astype(np.float32).T
    out = np.zeros((2, 32, smax), np.float32)
    out[0, :16] = c
    out[0, 16:] = c
    out[1, :16] = -s
    out[1, 16:] = s
    return out


def host_prep(inp, depth):
    L = depth
    w_in = np.asarray(inp["w_in"], np.float32)[:L]
    sec = np.cumsum([0, 256, 256, 256, 256, 128, 128, 256, 128, 32, 256, 256, 256])
    a_q, a_k, a_v, b_q, b_k, b_v, c_q, c_kv, c_kr, d_q, d_k, d_v = [w_in[:, :, sec[i]:sec[i + 1]] for i in range(12)]
    z64 = np.zeros((L, 1024, 64), np.float32)
    c_kr_sw = np.concatenate([c_kr[:, :, 16:32], c_kr[:, :, 0:16]], axis=-1)
    w_in_p = np.concatenate([a_q, a_k, b_q, b_k, d_q, d_k, a_v, b_v, d_v, c_q, c_kv, z64, c_kr, z64, c_kr_sw], axis=-1)
    assert w_in_p.shape[-1] == WIN_COLS
    wq = np.asarray(inp["w_q_up"], np.float32)[:L].reshape(L, 256, 4, 96)
    z = np.zeros((L, 256, 4, 64), np.float32)
    pe = wq[..., 64:96]
    pe_sw = np.concatenate([pe[..., 16:32], pe[..., 0:16]], axis=-1)
    wq_p = np.concatenate([wq, z, pe_sw], axis=-1).reshape(L, 256, 768)
    wkv = np.asarray(inp["w_kv_up"], np.float32)[:L].reshape(L, 128, 4, 128)
    wkv_p = np.concatenate([wkv[..., :64].reshape(L, 128, 256), wkv[..., 64:].reshape(L, 128, 256)], axis=-1)
    pcol = np.zeros((L, 128, NPCOL), np.float32)
    pcol[:, :, PC_NATT:PC_NATT + 8] = np.asarray(inp["norm_attn"], np.float32)[:L].reshape(L, 8, 128).transpose(0, 2, 1)
    pcol[:, :, PC_NMLP:PC_NMLP + 8] = np.asarray(inp["norm_mlp"], np.float32)[:L].reshape(L, 8, 128).transpose(0, 2, 1)
    pcol[:, :, PC_GN:PC_GN + 8] = np.asarray(inp["group_norm"], np.float32)[:L].reshape(L, 8, 128).transpose(0, 2, 1)
    pcol[:, :, PC_QN:PC_QN + 2] = np.asarray(inp["mla_q_norm"], np.float32)[:L].reshape(L, 2, 128).transpose(0, 2, 1)
    pcol[:, :, PC_KVN] = np.asarray(inp["mla_kv_norm"], np.float32)[:L]
    pcol[:, :, PC_SINK:PC_SINK + 4] = np.asarray(inp["sink_logits"], np.float32)[:L][:, None, :]
    nfin = np.broadcast_to(np.asarray(inp["norm_final"], np.float32)[None, :], (128, 1024)).copy()
    dr_idx, dc_idx, cw, cf, vb = _d_tables()
    rpb = np.asarray(inp["na_rpb"], np.float32)[:L]
    biasf = rpb[:, :, dr_idx, dc_idx]
    return {
        "w_in_p": np.ascontiguousarray(w_in_p),
        "w_q_p": np.ascontiguousarray(wq_p),
        "w_kv_p": np.ascontiguousarray(wkv_p),
        "w_out": np.ascontiguousarray(np.asarray(inp["w_out"], np.float32)[:L]),
        "w_up": np.ascontiguousarray(np.asarray(inp["w_mlp_up"], np.float32)[:L]),
        "w_down": np.ascontiguousarray(np.asarray(inp["w_mlp_down"], np.float32)[:L]),
        "pcol": pcol,
        "nfin": nfin,
        "ident": np.eye(128, dtype=np.float32),
        "mask_a": _mask_a(),
        "mask_b": _mask_b(),
        "biasf": np.ascontiguousarray(biasf),
        "dconst": np.ascontiguousarray(np.concatenate([cw, cf, vb.transpose(1, 0, 2).reshape(128, 12 * 512)], axis=1)),
    }


class Prog:
    def __init__(self, seqs, depth):
        self.seqs = list(seqs)
        self.L = depth
        self.T = sum(seqs)
        self.smax = max(seqs)
        self.starts = [int(x) for x in np.cumsum([0] + self.seqs[:-1])]

    def sb(self, st, name, shape, dt, n=1):
        out = []
        for i in range(n):
            t = st.enter_context(self.nc.sbuf_tensor("%s_%d_%d" % (name, self.uid, i), shape, dt))
            out.append((t, Res()))
        self.uid += 1
        return out

    def build(self):
        nc = bass.Bass("TRN2", target_bir_lowering=False)
        self.nc = nc
        self.uid = 0
        L, T = self.L, self.T

        def din(name, shape):
            return nc.dram_tensor(name, shape, F32, kind="ExternalInput").ap()

        def dscr(name, shape, dt):
            return nc.dram_tensor(name, shape, dt, kind="Internal").ap()

        self.x_in = din("x", [T, 1024])
        self.w_in_p = din("w_in_p", [L, 1024, WIN_COLS])
        self.w_q_p = din("w_q_p", [L, 256, 768])
        self.w_kv_p = din("w_kv_p", [L, 128, 512])
        self.w_out = din("w_out", [L, 1024, 1024])
        self.w_up = din("w_up", [L, 1024, 4096])
        self.w_down = din("w_down", [L, 4096, 1024])
        self.pcol_d = din("pcol", [L, 128, NPCOL])
        self.nfin_d = din("nfin", [128, 1024])
        self.ident_d = din("ident", [128, 128])
        self.mask_a_d = din("mask_a", [4, 128, MA_LEN])
        self.mask_b_d = din("mask_b", [4, 128, MB_LEN])
        self.biasf_d = din("biasf", [L, 4, 128, MD_ARR])
        self.dconst_d = din("dconst", [128, 2 * MD_ARR + 12 * 512])
        self.rope_d = din("rope", [2, 32, self.smax])
        self.y_out = nc.dram_tensor("y", [T, 1024], F32, kind="ExternalOutput").ap()

        self.xres = dscr("xres", [T, 1024], F32)
        self.win_b = dscr("win_b", [L, 128, 8, WIN_COLS], BF16)
        self.wq_b = dscr("wq_b", [L, 128, 2, 768], BF16)
        self.wkv_b = dscr("wkv_b", [L, 128, 512], BF16)
        self.wout_b = dscr("wout_b", [L, 128, 8, 1024], BF16)
        self.wup_b = dscr("wup_b", [L, 8, 128, 8, 512], BF16)
        self.wdown_b = dscr("wdown_b", [L, 128, 32, 1024], BF16)
        self.maska_b = dscr("maska_b", [4, 128, MA_LEN], BF16)
        self.maskb_b = dscr("maskb_b", [4, 128, MB_LEN], BF16)
        self.maskd_b = dscr("maskd_b", [4, 128, MD_LEN], BF16)
        self.qkT = dscr("qkT", [FM_COLS, T], BF16)
        self.qTc = dscr("qTc", [4, 96, T], BF16)
        self.kTc = dscr("kTc", [4, 96, T], BF16)
        self.vsc = dscr("vsc", [T, NVH, 65], BF16)
        self.onT = dscr("onT", [1024, T], BF16)
        nt = T // 512
        self.r_x = [Res() for _ in range(nt)]
        self.r_qk = [Res() for _ in range(nt)]
        self.r_on = [Res() for _ in range(nt)]
        self.r_qc = [Res() for _ in range(nt)]
        self.r_kc = [Res() for _ in range(nt)]
        self.r_v = [Res() for _ in range(nt)]
        self.r_w = Res()
        self.r_maskd = Res()

        with contextlib.ExitStack() as top:
            self.S = Sched(nc, top)
            cst = self.sb(top, "identf", [128, 128], F32)[0]
            self.identf, self.r_identf = cst
            cst = self.sb(top, "identb", [128, 128], BF16)[0]
            self.identb, self.r_identb = cst
            cst = self.sb(top, "neghalf", [128, 8], F32)[0]
            self.neghalf, self.r_neghalf = cst
            S = self.S
            S.dma("sp", self.identf[:], self.ident_d[:, :], writes=[self.r_identf])
            S.op("dve", lambda: nc.vector.tensor_copy(out=self.identb[:], in_=self.identf[:]),
                 reads=[self.r_identf], writes=[self.r_identb])
            S.op("dve", lambda: nc.vector.memset(self.neghalf[:], -0.5), writes=[self.r_neghalf])
            import os
            kstop = int(os.environ.get("KSTOP", "99"))
            if kstop >= 1:
                self.prologue()
            for l in range(L):
                if kstop >= 2:
                    self.phase_dmask(l)
                if kstop >= 3:
                    self.phase1(l)
                if kstop >= 4:
                    self.phase2(l)
                if kstop >= 5:
                    self.phase3(l)
            S.final_wait()
        return nc

    def alloc_psum(self, st, nps, npb):
        nc = self.nc
        self.ps = [(st.enter_context(nc.psum_tensor("ps%d_%d" % (i, self.uid), [128, 512], F32)), Res()) for i in range(nps)]
        self.pb = [(st.enter_context(nc.psum_tensor("pb%d_%d" % (i, self.uid), [128, 1024], BF16)), Res()) for i in range(npb)]
        self.uid += 1
        self.ps_i = 0
        self.pb_i = 0

    def next_ps(self):
        p = self.ps[self.ps_i % len(self.ps)]
        self.ps_i += 1
        return p

    def next_pb(self):
        p = self.pb[self.pb_i % len(self.pb)]
        self.pb_i += 1
        return p

    def prologue(self):
        nc, S, L = self.nc, self.S, self.L
        with contextlib.ExitStack() as st:
            stg = self.sb(st, "cv_f", [128, 4096], F32, 3)
            stb = self.sb(st, "cv_b", [128, 4096], BF16, 3)
            k = [0]

            def conv(src, dst, shape):
                n = int(np.prod(shape))
                assert n <= 4096
                (f, rf), (b, rb) = stg[k[0] % 3], stb[k[0] % 3]
                if len(shape) == 1:
                    fv, bv = f[:, 0:n], b[:, 0:n]
                else:
                    fv = f[:, 0:n].rearrange("p (a b) -> p a b", b=shape[1])
                    bv = b[:, 0:n].rearrange("p (a b) -> p a b", b=shape[1])
                S.dma("sp", fv, src, writes=[rf])
                if k[0] % 2 == 0:
                    S.op("dve", lambda: nc.vector.tensor_copy(out=b[:, 0:n], in_=f[:, 0:n]), reads=[rf], writes=[rb])
                else:
                    S.op("act", lambda: nc.scalar.copy(out=b[:, 0:n], in_=f[:, 0:n]), reads=[rf], writes=[rb])
                S.dma("pool", dst, bv, reads=[rb])
                k[0] += 1

            for l in range(L):
                wv = self.w_in_p[l].rearrange("(c p) n -> p c n", p=128)
                for c in range(8):
                    conv(wv[:, c, :], self.win_b[l, :, c, :], [WIN_COLS])
                conv(self.w_q_p[l].rearrange("(c p) n -> p c n", p=128), self.wq_b[l], [2, 768])
                conv(self.w_kv_p[l], self.wkv_b[l], [512])
                wv = self.w_out[l].rearrange("(c p) n -> p c n", p=128)
                for c0 in range(0, 8, 4):
                    conv(wv[:, c0:c0 + 4, :], self.wout_b[l, :, c0:c0 + 4, :], [4, 1024])
                wv = self.w_up[l].rearrange("(c p) n -> p c n", p=128)
                for g in range(8):
                    conv(wv[:, :, g * 512:(g + 1) * 512], self.wup_b[l, g], [8, 512])
                wv = self.w_down[l].rearrange("(f p) n -> p f n", p=128)
                for f0 in range(0, 32, 4):
                    conv(wv[:, f0:f0 + 4, :], self.wdown_b[l, :, f0:f0 + 4, :], [4, 1024])
            for h in range(4):
                conv(self.mask_a_d[h], self.maska_b[h], [MA_LEN])
                conv(self.mask_b_d[h], self.maskb_b[h], [MB_LEN])
            S.barrier()

    def phase_dmask(self, l):
        nc, S = self.nc, self.S
        with contextlib.ExitStack() as st:
            (dc, rdc) = self.sb(st, "dm_c", [128, 2 * MD_ARR + 12 * 512], F32)[0]
            bf = self.sb(st, "dm_bf", [128, MD_ARR], F32, 2)
            ex = self.sb(st, "dm_ex", [128, MD_ARR], F32, 2)
            full = self.sb(st, "dm_full", [128, MD_ARR], F32, 2)
            mo = self.sb(st, "dm_out", [128, MD_LEN], BF16, 2)
            S.dma("sp", dc[:], self.dconst_d[:, :], writes=[rdc])
            for h in range(4):
                (b, rb), (e, re), (fu, rfu), (m, rm) = bf[h % 2], ex[h % 2], full[h % 2], mo[h % 2]
                S.dma("sp", b[:], self.biasf_d[l, h], writes=[rb])
                S.op("act", lambda: nc.scalar.activation(out=e[:], in_=b[:], func=AF.Exp), reads=[rb], writes=[re])
                S.op("dve", lambda: nc.vector.tensor_tensor(out=m[:, 0:MD_ARR], in0=e[:], in1=dc[:, 0:MD_ARR], op=ALU.mult),
                     reads=[re, rdc], writes=[rm])
                S.op("dve", lambda: nc.vector.tensor_tensor(out=fu[:], in0=e[:], in1=dc[:, MD_ARR:2 * MD_ARR], op=ALU.mult),
                     reads=[re, rdc], writes=[rfu])
                for kt in range(6):
                    for (which, dr) in ((0, 2 * kt), (1, -4 + 2 * kt)):
                        o0 = MD_ARR + (which * 6 + kt) * 512
                        a0 = (10 - dr) * 64
                        v0 = 2 * MD_ARR + (which * 6 + kt) * 512
                        S.op("dve", lambda o0=o0, a0=a0, v0=v0: nc.vector.tensor_tensor(
                            out=m[:, o0:o0 + 512], in0=fu[:, a0:a0 + 512], in1=dc[:, v0:v0 + 512], op=ALU.mult),
                            reads=[rfu, rdc], writes=[rm])
                S.dma("pool", self.maskd_b[h], m[:], reads=[rm])
            S.barrier()

    def sbm(self, st, name, shape, dt, n, nres):
        out = []
        for i in range(n):
            t = st.enter_context(self.nc.sbuf_tensor("%s_%d_%d" % (name, self.uid, i), shape, dt))
            out.append((t, [Res() for _ in range(nres)]))
        self.uid += 1
        return out

    def rstd_from_acc(self, acc, racc, out, rout, c0, n):
        nc, S = self.nc, self.S
        S.op("dve", lambda: nc.vector.tensor_scalar(out=out[:, c0:c0 + n], in0=acc[:, c0:c0 + n], scalar1=EPS, scalar2=None, op0=ALU.add),
             reads=[racc], writes=[rout])
        S.op("pool", lambda: nc.gpsimd.tensor_tensor(out=out[:, c0:c0 + n], in0=out[:, c0:c0 + n], in1=self.neghalf[:, 0:n], op=ALU.pow),
             reads=[rout, self.r_neghalf], writes=[rout])

    def evac(self, i, out, in_, reads, writes, scale=None):
        nc, S = self.nc, self.S
        import os
        kev = os.environ.get("KEV")
        if kev == "0":
            return
        if kev == "1":
            i = 1
        if kev == "2":
            i = 0
        if kev == "3" and scale is None:
            i = 1
        if kev == "4":
            i = 1 if scale is not None else 0
        if i % 2 == 0:
            if scale is None:
                S.op("act", lambda: nc.scalar.activation(out=out, in_=in_, func=AF.Copy), reads=reads, writes=writes)
            else:
                S.op("act", lambda: nc.scalar.activation(out=out, in_=in_, func=AF.Copy, scale=scale), reads=reads, writes=writes)
        else:
            if scale is None:
                S.op("dve", lambda: nc.vector.tensor_copy(out=out, in_=in_), reads=reads, writes=writes)
            else:
                S.op("dve", lambda: nc.vector.tensor_scalar(out=out, in0=in_, scalar1=scale, scalar2=None, op0=ALU.mult),
                     reads=reads, writes=writes)

    def norm_transpose(self, xt, rxt, xn, rxn, ss, rss, rs, rrs, junk, hT, rhT, gain, rgain, D=1024):
        nc, S = self.nc, self.S
        sc = float(D) ** -0.5
        for sub in range(4):
            S.op("act", lambda: nc.scalar.activation(out=xn[:, sub, :], in_=xt[:, sub, :], func=AF.Square, scale=sc,
                                                     accum_out=ss[:, sub:sub + 1]),
                 reads=[rxt[sub]], writes=[rss, rxn[sub]])
        self.rstd_from_acc(ss, rss, rs, rrs, 0, 4)
        for sub in range(4):
            S.op("dve", lambda: nc.vector.tensor_scalar(out=xn[:, sub, :], in0=xt[:, sub, :], scalar1=rs[:, sub:sub + 1],
                                                        scalar2=None, op0=ALU.mult),
                 reads=[rxt[sub], rrs], writes=[rxn[sub]])
        for c in range(D // 128):
            pb, rpb = self.next_pb()
            for sub in range(4):
                S.op("pe", lambda: nc.tensor.transpose(out=pb[:, sub * 128:(sub + 1) * 128], in_=xn[:, sub, c * 128:(c + 1) * 128],
                                                       identity=self.identb[:]),
                     reads=[rxn[sub], self.r_identb], writes=[rpb], signal=(sub == 3))
            self.evac(c, hT[:, c, :], pb[:, 0:512], [rpb, rgain], [rhT[c]], scale=gain[:, c:c + 1])

    def chk(self, n):
        import os
        if int(os.environ.get('KSUB', '99')) < n:
            raise _Stop()

    def tiles(self):
        out = []
        for si, (s0, sl) in enumerate(zip(self.starts, self.seqs)):
            for t0 in range(s0, s0 + sl, 512):
                out.append((si, s0, t0))
        return out

    def phase1(self, l):
        nc, S = self.nc, self.S
        x_src = self.x_in if l == 0 else self.xres
        with contextlib.ExitStack() as st:
            self.alloc_psum(st, 6, 2)
            (win, rwin) = self.sb(st, "p1_win", [128, 8, WIN_COLS], BF16)[0]
            (wq, rwq) = self.sb(st, "p1_wq", [128, 2, 768], BF16)[0]
            (wkv, rwkv) = self.sb(st, "p1_wkv", [128, 512], BF16)[0]
            (pc, rpc) = self.sb(st, "p1_pc", [128, NPCOL], F32)[0]
            xts = self.sbm(st, "p1_xt", [128, 4, 1024], F32, 2, 4)
            xns = self.sbm(st, "p1_xn", [128, 4, 1024], BF16, 2, 4)
            hTs = self.sbm(st, "p1_hT", [128, 8, 512], BF16, 2, 8)
            junk = None
            (junk2, _) = self.sb(st, "p1_junk2", [128, 256], BF16)[0]
            sss = self.sb(st, "p1_ss", [128, 8], F32, 2)
            rss_ = self.sb(st, "p1_rs", [128, 8], F32, 2)
            qks = self.sbm(st, "p1_qk", [128, 11, 512], BF16, 2, 11)
            vss = self.sbm(st, "p1_vs", [128, 4, NVH, 65], BF16, 2, 12)
            cns = self.sbm(st, "p1_cn", [128, 4, 384], BF16, 2, 8)
            cqTs = self.sbm(st, "p1_cqT", [128, 3, 512], BF16, 2, 3)
            ssc = self.sbm(st, "p1_ssc", [128, 8], F32, 2, 4)
            rsc = self.sbm(st, "p1_rsc", [128, 8], F32, 2, 4)
            rts = self.sb(st, "p1_rt", [128, 2, 512], F32, 2)
            tmps = self.sb(st, "p1_tmp", [128, 512], F32, 2)
            kpes = self.sb(st, "p1_kpe", [128, 512], F32, 2)
            qTs = self.sbm(st, "p1_qT", [128, 4, 512], BF16, 2, 8)
            kTs = self.sbm(st, "p1_kT", [128, 4, 512], BF16, 2, 8)

            S.dma("sp", win[:], self.win_b[l], writes=[rwin])
            S.dma("sp", wq[:], self.wq_b[l], writes=[rwq])
            S.dma("sp", wkv[:], self.wkv_b[l], writes=[rwkv])
            S.dma("sp", pc[:], self.pcol_d[l], writes=[rpc])
            for (v, rv) in vss:
                S.op("dve", lambda: nc.vector.memset(v[:], 1.0), writes=rv)

            tl = self.tiles()

            def load(i):
                si, s0, t0 = tl[i]
                xt, rxt = xts[i % 2]
                rt, rrt = rts[i % 2]
                S.dma("sp", xt[:], x_src[t0:t0 + 512, :].rearrange("(s p) d -> p s d", p=128),
                      reads=[self.r_x[t0 // 512]], writes=rxt)
                p0 = t0 - s0
                S.dma("sp", rt[64:96, :, :], self.rope_d[:, :, p0:p0 + 512].rearrange("a r t -> r a t"), writes=[rrt])

            load(0)
            tk = 0
            for i, (si, s0, t0) in enumerate(tl):
                if i + 1 < len(tl):
                    load(i + 1)
                b = i % 2
                xt, rxt = xts[b]
                xn, rxn = xns[b]
                hT, rhT = hTs[b]
                ss, rss = sss[b]
                rs, rrs = rss_[b]
                qk, rqk = qks[b]
                vs, rvs = vss[b]
                cn, rcn = cns[b]
                cqT, rcqT = cqTs[b]
                sc_, rsc_ = ssc[b]
                rc_, rrc_ = rsc[b]
                rt, rrt = rts[b]
                qT, rqT = qTs[b]
                kT, rkT = kTs[b]
                ti = t0 // 512
                try:
                  self.phase1_tile(locals())
                except _Stop:
                  break
            S.barrier()

    def phase1_tile(self, L_):
                nc, S = self.nc, self.S
                globals_ = L_
                (xt, rxt, xn, rxn, ss, rss, rs, rrs, junk, hT, rhT, pc, rpc, win, rwin, qk, rqk, t0, ti, vs, rvs, junk2, sc_, rsc_, rc_, rrc_, cn, rcn, cqT, rcqT, tmps, kpes, rt, rrt, kT, rkT, qT, rqT, wq, rwq, wkv, rwkv) = [L_[k] for k in 'xt rxt xn rxn ss rss rs rrs junk hT rhT pc rpc win rwin qk rqk t0 ti vs rvs junk2 sc_ rsc_ rc_ rrc_ cn rcn cqT rcqT tmps kpes rt rrt kT rkT qT rqT wq rwq wkv rwkv'.split()]
                tk = 0
                self.chk(1)
                self.norm_transpose(xt, rxt, xn, rxn, ss, rss, rs, rrs, junk, hT, rhT, pc[:, PC_NATT:PC_NATT + 8], rpc)
                self.chk(2)
                for j in range(11):
                    ps, rps = self.next_ps()
                    for kc in range(8):
                        S.op("pe", lambda: nc.tensor.matmul(ps[:, :], win[:, kc, j * 128:(j + 1) * 128], hT[:, kc, :],
                                                            start=(kc == 0), stop=(kc == 7)),
                             reads=[rwin, rhT[kc]], writes=[rps], signal=(kc == 7))
                    self.evac(j, qk[:, j, :], ps[:, :], [rps], [rqk[j]])
                import os
                if not os.environ.get("KNOSTORE"):
                    S.dma("sp", self.qkT[:, t0:t0 + 512].rearrange("(j p) t -> p j t", p=128), qk[:, :, :],
                          reads=rqk, writes=[self.r_qk[ti]])
                self.chk(3)
                for sub in range(4):
                    psA, rpsA = self.next_ps()
                    for kc in range(8):
                        S.op("pe", lambda: nc.tensor.matmul(psA[:, :], hT[:, kc, sub * 128:(sub + 1) * 128], win[:, kc, TM0:TM0 + 512],
                                                            start=(kc == 0), stop=(kc == 7)),
                             reads=[rwin, rhT[kc]], writes=[rpsA], signal=(kc == 7))
                    self.evac(0, vs[:, sub, 0:8, 0:64], psA[:, :].rearrange("p (h d) -> p h d", d=64), [rpsA], [rvs[sub * 3]])
                    psB, rpsB = self.next_ps()
                    for kc in range(8):
                        S.op("pe", lambda: nc.tensor.matmul(psB[:, :], hT[:, kc, sub * 128:(sub + 1) * 128], win[:, kc, TM0 + 512:TM0 + 1024],
                                                            start=(kc == 0), stop=(kc == 7)),
                             reads=[rwin, rhT[kc]], writes=[rpsB], signal=(kc == 7))
                    self.evac(0, vs[:, sub, 8:10, 0:64], psB[:, 0:128].rearrange("p (h d) -> p h d", d=64), [rpsB], [rvs[sub * 3 + 1]])
                    S.op("act", lambda: nc.scalar.activation(out=junk2[:, 0:256], in_=psB[:, 128:384], func=AF.Square, scale=1.0 / 16.0,
                                                             accum_out=sc_[:, 2 * sub:2 * sub + 1]),
                         reads=[rpsB], writes=[rsc_[sub]])
                    S.op("act", lambda: nc.scalar.activation(out=junk2[:, 0:128], in_=psB[:, 384:512], func=AF.Square, scale=128.0 ** -0.5,
                                                             accum_out=sc_[:, 2 * sub + 1:2 * sub + 2]),
                         reads=[rpsB], writes=[rsc_[sub]])
                    self.rstd_from_acc(sc_, rsc_[sub], rc_, rrc_[sub], 2 * sub, 2)
                    S.op("dve", lambda: nc.vector.tensor_scalar(out=cn[:, sub, 0:256], in0=psB[:, 128:384], scalar1=rc_[:, 2 * sub:2 * sub + 1],
                                                                scalar2=None, op0=ALU.mult),
                         reads=[rpsB, rrc_[sub]], writes=[rcn[2 * sub]])
                    S.op("dve", lambda: nc.vector.tensor_scalar(out=cn[:, sub, 256:384], in0=psB[:, 384:512], scalar1=rc_[:, 2 * sub + 1:2 * sub + 2],
                                                                scalar2=None, op0=ALU.mult),
                         reads=[rpsB, rrc_[sub]], writes=[rcn[2 * sub + 1]])
                self.chk(4)
                for c in range(3):
                    pb, rpb = self.next_pb()
                    for sub in range(4):
                        S.op("pe", lambda: nc.tensor.transpose(out=pb[:, sub * 128:(sub + 1) * 128], in_=cn[:, sub, c * 128:(c + 1) * 128],
                                                               identity=self.identb[:]),
                             reads=[rcn[2 * sub + (1 if c == 2 else 0)], self.r_identb], writes=[rpb], signal=(sub == 3))
                    self.evac(c, cqT[:, c, :], pb[:, 0:512], [rpb, rpc], [rcqT[c]], scale=pc[:, PC_QN + c:PC_QN + c + 1])
                self.chk(5)
                tmp, rtmp = tmps[tk % 2]
                kpe, rkpe = kpes[tk % 2]
                tk += 1
                ps1, rps1 = self.next_ps()
                for kc in range(8):
                    S.op("pe", lambda: nc.tensor.matmul(ps1[0:96, :], win[:, kc, KR0:KR0 + 96], hT[:, kc, :], start=(kc == 0), stop=(kc == 7)),
                         reads=[rwin, rhT[kc]], writes=[rps1], signal=(kc == 7))
                ps2, rps2 = self.next_ps()
                for kc in range(8):
                    S.op("pe", lambda: nc.tensor.matmul(ps2[0:96, :], win[:, kc, KR0 + 96:KR0 + 192], hT[:, kc, :], start=(kc == 0), stop=(kc == 7)),
                         reads=[rwin, rhT[kc]], writes=[rps2], signal=(kc == 7))
                S.op("dve", lambda: nc.vector.tensor_tensor(out=tmp[64:96, :], in0=ps2[64:96, :], in1=rt[64:96, 1, :], op=ALU.mult),
                     reads=[rps2, rrt], writes=[rtmp])
                S.op("dve", lambda: nc.vector.tensor_tensor(out=kpe[64:96, :], in0=ps1[64:96, :], in1=rt[64:96, 0, :], op=ALU.mult),
                     reads=[rps1, rrt], writes=[rkpe])
                for h in range(4):
                    S.op("pool", lambda: nc.gpsimd.tensor_tensor(out=kT[64:96, h, :], in0=kpe[64:96, :], in1=tmp[64:96, :], op=ALU.add),
                         reads=[rkpe, rtmp], writes=[rkT[2 * h + 1]])
                self.chk(6)
                for h in range(4):
                    tmp, rtmp = tmps[tk % 2]
                    kpe, rkpe = kpes[tk % 2]
                    tk += 1
                    pq1, rpq1 = self.next_ps()
                    for kc in range(2):
                        S.op("pe", lambda: nc.tensor.matmul(pq1[0:96, :], wq[:, kc, h * 192:h * 192 + 96], cqT[:, kc, :], start=(kc == 0), stop=(kc == 1)),
                             reads=[rwq, rcqT[kc]], writes=[rpq1], signal=(kc == 1))
                    pq2, rpq2 = self.next_ps()
                    for kc in range(2):
                        S.op("pe", lambda: nc.tensor.matmul(pq2[0:96, :], wq[:, kc, h * 192 + 96:h * 192 + 192], cqT[:, kc, :], start=(kc == 0), stop=(kc == 1)),
                             reads=[rwq, rcqT[kc]], writes=[rpq2], signal=(kc == 1))
                    S.op("act", lambda: nc.scalar.copy(out=qT[0:64, h, :], in_=pq1[0:64, :]), reads=[rpq1], writes=[rqT[2 * h]])
                    S.op("dve", lambda: nc.vector.tensor_tensor(out=tmp[64:96, :], in0=pq2[64:96, :], in1=rt[64:96, 1, :], op=ALU.mult),
                         reads=[rpq2, rrt], writes=[rtmp])
                    S.op("dve", lambda: nc.vector.tensor_tensor(out=kpe[64:96, :], in0=pq1[64:96, :], in1=rt[64:96, 0, :], op=ALU.mult),
                         reads=[rpq1, rrt], writes=[rkpe])
                    S.op("pool", lambda: nc.gpsimd.tensor_tensor(out=qT[64:96, h, :], in0=kpe[64:96, :], in1=tmp[64:96, :], op=ALU.add),
                         reads=[rkpe, rtmp], writes=[rqT[2 * h + 1]])
                    pk, rpk = self.next_ps()
                    S.op("pe", lambda: nc.tensor.matmul(pk[0:64, :], wkv[:, h * 64:(h + 1) * 64], cqT[:, 2, :], start=True, stop=True),
                         reads=[rwkv, rcqT[2]], writes=[rpk])
                    self.evac(h, kT[0:64, h, :], pk[0:64, :], [rpk], [rkT[2 * h]])
                for sub in range(4):
                    pv, rpv = self.next_ps()
                    S.op("pe", lambda: nc.tensor.matmul(pv[:, 0:256], cqT[:, 2, sub * 128:(sub + 1) * 128], wkv[:, 256:512], start=True, stop=True),
                         reads=[rwkv, rcqT[2]], writes=[rpv])
                    self.evac(0, vs[:, sub, 10:14, 0:64], pv[:, 0:256].rearrange("p (h d) -> p h d", d=64), [rpv], [rvs[sub * 3 + 2]])
                self.chk(7)
                S.dma("sp", self.qTc[:, :, t0:t0 + 512].rearrange("h r t -> r h t"), qT[0:96, :, :], reads=rqT, writes=[self.r_qc[ti]])
                S.dma("sp", self.kTc[:, :, t0:t0 + 512].rearrange("h r t -> r h t"), kT[0:96, :, :], reads=rkT, writes=[self.r_kc[ti]])
                S.dma("sp", self.vsc[t0:t0 + 512, :, :].rearrange("(s p) h e -> p s h e", p=128), vs[:, :, :, :],
                      reads=rvs, writes=[self.r_v[ti]])


    def phase2(self, l):
        nc, S = self.nc, self.S
        SPAN = 2048
        SMAXK = self.smax
        LOOK, EP_DEFER, GN_DEFER = 2, 2, 3
        with contextlib.ExitStack() as st:
            scs = [(st.enter_context(nc.psum_tensor("p2sc%d_%d" % (i, self.uid), [128, 1024], F32)), Res()) for i in range(2)]
            self.alloc_psum(st, 3, 1)
            o_banks = self.ps[0:2]
            tp_bank = self.ps[2]
            Ks = self.sb(st, "p2_K", [128, SMAXK], BF16, 2)
            Vs = self.sb(st, "p2_V", [128, SMAXK // 128, 65], BF16, 2)
            Ms = self.sb(st, "p2_M", [128, MASK_MAX], BF16, 2)
            Qs = self.sb(st, "p2_Q", [96, 512], BF16, 4)
            QLs = self.sb(st, "p2_QL", [128, 512], BF16, 4)
            QHs = self.sb(st, "p2_QH", [128, 512], BF16, 4)
            Ps = self.sb(st, "p2_P", [128, 1024], BF16, 4)
            P2s = self.sb(st, "p2_P2", [128, 1024], BF16, 4)
            osbs = self.sb(st, "p2_osb", [65, 512], F32, 3)
            ogrps = self.sbm(st, "p2_ogrp", [128, SPAN // 128, 256], F32, 2, SPAN // 512)
            (pc, rpc) = self.sb(st, "p2_pc", [128, NPCOL], F32)[0]
            (esink, resink) = self.sb(st, "p2_esink", [128, 4], F32)[0]
            (junk, _) = self.sb(st, "p2_junk", [128, 256], BF16)[0]
            gss = self.sb(st, "p2_gss", [128, 4], F32, 3)
            grs = self.sb(st, "p2_grs", [128, 4], F32, 3)
            recs = self.sb(st, "p2_rec", [128, 4], F32, 3)
            ons = self.sbm(st, "p2_on", [128, 4, 256], BF16, 3, 4)
            onTs = self.sbm(st, "p2_onT", [128, 2, 512], BF16, 3, 2)
            S.dma("sp", pc[:], self.pcol_d[l], writes=[rpc])
            S.op("act", lambda: nc.scalar.activation(out=esink[:], in_=pc[:, PC_SINK:PC_SINK + 4], func=AF.Exp),
                 reads=[rpc], writes=[resink])
            for (qb, rqb) in QLs + QHs:
                S.op("pool", lambda: nc.gpsimd.memset(qb[:], 0.0), writes=[rqb])
            ctr = {"q": 0, "ql": 0, "qh": 0, "p": 0, "sc": 0, "ob": 0, "gn": 0, "ep": 0}

            mixers = [
                ("A", 64, 0.125, ROW_AQ, ROW_AK, 0, 1),
                ("B", 64, 0.125, ROW_BQ, ROW_BK, 4, 2),
                ("C", 96, 96.0 ** -0.5, None, None, 10, 1),
                ("D", 64, 0.125, ROW_DQ, ROW_DK, 6, 1),
            ]
            halo = {"A": (1024, 1024), "B": (128, 128), "D": (256, 256)}

            for si, (s0, sl) in enumerate(zip(self.starts, self.seqs)):
                for sp0 in range(0, sl, SPAN):
                    span = min(SPAN, sl - sp0)
                    nq = span // 512
                    ctxs = []
                    units = []
                    for mi, (mname, dqk, scale, qrow, krow, vbase, grp) in enumerate(mixers):
                        for h in range(4):
                            if mname == "C":
                                wlo, whi = 0, sl
                            else:
                                wlo, whi = max(sp0 - halo[mname][0], 0), min(sp0 + span + halo[mname][1], sl)
                            c = dict(mi=mi, mname=mname, dqk=dqk, scale=scale, qrow=qrow, krow=krow, vbase=vbase, h=h, kvh=h // grp,
                                     wlo=wlo, whi=whi, kt_base=wlo // 128, start=len(units))
                            ci = len(ctxs)
                            ctxs.append(c)
                            for qi in range(nq):
                                q0 = sp0 + qi * 512
                                if mname == "A":
                                    lo, hi = q0 - 1024, q0 + 512 + 1024
                                elif mname == "B":
                                    lo, hi = q0 - 128, q0 + 512 + 128
                                elif mname == "C":
                                    lo, hi = 0, sl
                                else:
                                    lo, hi = q0 - 256, q0 + 768
                                lo, hi = max(lo, 0), min(hi, sl)
                                kts = list(range(lo // 128, hi // 128))
                                for a in range(0, len(kts), 2):
                                    units.append((ci, qi, q0, kts[a:a + 2], a == 0, a + 2 >= len(kts)))
                    state = {}
                    dq = []

                    def load_ctx(ci):
                        c = ctxs[ci]
                        K, rK = Ks[ci % 2]
                        V, rV = Vs[ci % 2]
                        wlo, whi = c["wlo"], c["whi"]
                        wlen = whi - wlo
                        tiles_rng = range((s0 + wlo) // 512, (s0 + whi + 511) // 512)
                        if c["krow"] is not None:
                            r0 = c["krow"] + (c["kvh"] // 2) * 128
                            ksrc = self.qkT[r0:r0 + 128, s0 + wlo:s0 + whi]
                            rdep = [self.r_qk[t] for t in tiles_rng]
                            S.dma("sp", K[0:128, 0:wlen], ksrc, reads=rdep, writes=[rK])
                        else:
                            ksrc = self.kTc[c["h"], :, s0 + wlo:s0 + whi]
                            rdep = [self.r_kc[t] for t in tiles_rng]
                            S.dma("sp", K[0:c["dqk"], 0:wlen], ksrc, reads=rdep, writes=[rK])
                        S.dma("sp", V[:, 0:wlen // 128, :],
                              self.vsc[s0 + wlo:s0 + whi, c["vbase"] + c["kvh"], :].rearrange("(k p) e -> p k e", p=128),
                              reads=[self.r_v[t] for t in tiles_rng], writes=[rV])
                        M = rM = None
                        if c["mname"] != "C":
                            M, rM = Ms[ci % 2]
                            if c["mname"] == "A":
                                S.dma("sp", M[:, 0:MA_LEN], self.maska_b[c["h"]], writes=[rM])
                            elif c["mname"] == "B":
                                S.dma("sp", M[:, 0:MB_LEN], self.maskb_b[c["h"]], writes=[rM])
                            else:
                                S.dma("sp", M[:, 0:MD_LEN], self.maskd_b[c["h"]], writes=[rM])
                        c.update(K=K, rK=rK, V=V, rV=rV, M=M, rM=rM)

                    def mask_ap(c, q0, kt):
                        dlt = kt * 128 - q0
                        if c["mname"] == "A":
                            off = 1408 - dlt
                        elif c["mname"] == "B":
                            off = 512 - dlt
                        else:
                            rows = sl // 64
                            R0 = q0 // 64
                            if R0 == 0:
                                off = MD_ARR + (dlt // 128) * 512
                            elif R0 == rows - 8:
                                off = MD_ARR + (6 + (dlt + 256) // 128) * 512
                            else:
                                off = 640 - dlt
                        return c["M"][:, off:off + 512]

                    def emit_S(i):
                        ci, qi, q0, kts, first, last = units[i]
                        c = ctxs[ci]
                        nk = len(kts)
                        dqk, K, rK, kb = c["dqk"], c["K"], c["rK"], c["kt_base"]
                        if c["qrow"] is not None:
                            dqk = 128
                        if first:
                            t0 = s0 + q0
                            if c["qrow"] is not None:
                                half = c["kvh"] % 2
                                if half == 0:
                                    Q, rQ = QLs[ctr["ql"] % 4]
                                    ctr["ql"] += 1
                                else:
                                    Q, rQ = QHs[ctr["qh"] % 4]
                                    ctr["qh"] += 1
                                r0 = c["qrow"] + c["h"] * 64
                                qsrc = self.qkT[r0:r0 + 64, t0:t0 + 512]
                                rd = [self.r_qk[t0 // 512]]
                                S.dma("sp", Q[half * 64:half * 64 + 64, :], qsrc, reads=rd, writes=[rQ])
                            else:
                                Q, rQ = Qs[ctr["q"] % 4]
                                ctr["q"] += 1
                                qsrc = self.qTc[c["h"], :, t0:t0 + 512]
                                rd = [self.r_qc[t0 // 512]]
                                S.dma("sp", Q[0:dqk, :], qsrc, reads=rd, writes=[rQ])
                            state[("q", ci, qi)] = (Q, rQ)
                        Q, rQ = state[("q", ci, qi)]
                        sc, rsc = scs[ctr["sc"] % 2]
                        ctr["sc"] += 1
                        for a, kt in enumerate(kts):
                            S.op("pe", lambda: nc.tensor.matmul(sc[:, a * 512:(a + 1) * 512], K[0:dqk, (kt - kb) * 128:(kt - kb + 1) * 128], Q[0:dqk, :],
                                                                start=True, stop=True),
                                 reads=[rK, rQ], writes=[rsc], signal=(a == nk - 1))
                        P, rP = Ps[ctr["p"] % 4]
                        S.op("act", lambda: nc.scalar.activation(out=P[:, 0:nk * 512], in_=sc[:, 0:nk * 512], func=AF.Exp, scale=c["scale"]),
                             reads=[rsc], writes=[rP])
                        if c["M"] is not None:
                            P2, rP2 = P2s[ctr["p"] % 4]
                            for a, kt in enumerate(kts):
                                mk = mask_ap(c, q0, kt)
                                S.op("dve", lambda: nc.vector.tensor_tensor(out=P2[:, a * 512:(a + 1) * 512], in0=P[:, a * 512:(a + 1) * 512],
                                                                            in1=mk, op=ALU.mult),
                                     reads=[rP, c["rM"]], writes=[rP2])
                            state[("p", i)] = (P2, rP2)
                        else:
                            state[("p", i)] = (P, rP)
                        ctr["p"] += 1

                    def emit_PV(j):
                        ci, qi, q0, kts, first, last = units[j]
                        c = ctxs[ci]
                        nk = len(kts)
                        V, rV, kb = c["V"], c["rV"], c["kt_base"]
                        P, rP = state.pop(("p", j))
                        if first:
                            ob, rob = o_banks[ctr["ob"] % 2]
                            ctr["ob"] += 1
                            state[("o", ci, qi)] = (ob, rob)
                        ob, rob = state[("o", ci, qi)]
                        for a, kt in enumerate(kts):
                            S.op("pe", lambda: nc.tensor.matmul(ob[0:65, :], V[:, kt - kb, :], P[:, a * 512:(a + 1) * 512],
                                                                start=(first and a == 0), stop=(last and a == nk - 1)),
                                 reads=[rV, rP], writes=[rob], signal=(a == nk - 1))
                        if last:
                            e = ctr["ep"] % 3
                            ctr["ep"] += 1
                            osb, rosb = osbs[e]
                            S.op("dve", lambda: nc.vector.tensor_copy(out=osb[:, :], in_=ob[0:65, :]), reads=[rob], writes=[rosb])
                            dq.append([EP_DEFER, lambda: emit_epilogue(ci, qi, e)])

                    def emit_epilogue(ci, qi, e):
                        c = ctxs[ci]
                        h, mi = c["h"], c["mi"]
                        osb, rosb = osbs[e]
                        rec, rrec = recs[e]
                        ogrp, rogrp = ogrps[mi % 2]
                        tp, rtp = tp_bank
                        for sub in range(4):
                            S.op("pe", lambda: nc.tensor.transpose(out=tp[:, sub * 65:(sub + 1) * 65], in_=osb[0:65, sub * 128:(sub + 1) * 128],
                                                                   identity=self.identf[0:65, 0:65]),
                                 reads=[rosb, self.r_identf], writes=[rtp], signal=(sub == 3))
                        den = tp[:, 0:260].rearrange("p (s e) -> p s e", e=65)[:, :, 64]
                        if c["mname"] == "B":
                            S.op("dve", lambda: nc.vector.tensor_scalar(out=rec[:, 0:4], in0=den, scalar1=esink[:, h:h + 1], scalar2=None, op0=ALU.add),
                                 reads=[rtp, resink], writes=[rrec])
                            S.op("dve", lambda: nc.vector.reciprocal(out=rec[:, 0:4], in_=rec[:, 0:4]), reads=[rrec], writes=[rrec])
                        else:
                            S.op("dve", lambda: nc.vector.reciprocal(out=rec[:, 0:4], in_=den), reads=[rtp], writes=[rrec])
                        for sub in range(4):
                            S.op("dve", lambda: nc.vector.tensor_scalar(out=ogrp[:, qi * 4 + sub, h * 64:(h + 1) * 64],
                                                                        in0=tp[:, sub * 65:sub * 65 + 64], scalar1=rec[:, sub:sub + 1],
                                                                        scalar2=None, op0=ALU.mult),
                                 reads=[rtp, rrec], writes=[rogrp[qi]])
                        if h == 3:
                            emit_gn_a(mi, qi)

                    def emit_gn_a(mi, qi):
                        g = ctr["gn"] % 3
                        ctr["gn"] += 1
                        ogrp, rogrp = ogrps[mi % 2]
                        gs, rgs = gss[g]
                        gr, rgr = grs[g]
                        on, ron = ons[g]
                        for sub in range(4):
                            S.op("act", lambda: nc.scalar.activation(out=junk[:, :], in_=ogrp[:, qi * 4 + sub, :], func=AF.Square, scale=1.0 / 16.0,
                                                                     accum_out=gs[:, sub:sub + 1]),
                                 reads=[rogrp[qi]], writes=[rgs])
                        self.rstd_from_acc(gs, rgs, gr, rgr, 0, 4)
                        for sub in range(4):
                            S.op("dve", lambda: nc.vector.tensor_scalar(out=on[:, sub, :], in0=ogrp[:, qi * 4 + sub, :], scalar1=gr[:, sub:sub + 1],
                                                                        scalar2=None, op0=ALU.mult),
                                 reads=[rogrp[qi], rgr], writes=[ron[sub]])
                        dq.append([GN_DEFER, lambda: emit_gn_b(mi, qi, g)])

                    def emit_gn_b(mi, qi, g):
                        t0 = s0 + sp0 + qi * 512
                        on, ron = ons[g]
                        onT, ronT = onTs[g]
                        for cc in range(2):
                            pb, rpb = self.next_pb()
                            for sub in range(4):
                                S.op("pe", lambda: nc.tensor.transpose(out=pb[:, sub * 128:(sub + 1) * 128], in_=on[:, sub, cc * 128:(cc + 1) * 128],
                                                                       identity=self.identb[:]),
                                     reads=[ron[sub], self.r_identb], writes=[rpb], signal=(sub == 3))
                            gc = PC_GN + mi * 2 + cc
                            S.op("dve", lambda: nc.vector.tensor_scalar(out=onT[:, cc, :], in0=pb[:, 0:512], scalar1=pc[:, gc:gc + 1], scalar2=None, op0=ALU.mult),
                                 reads=[rpb, rpc], writes=[ronT[cc]])
                        S.dma("pool", self.onT[mi * 256:(mi + 1) * 256, t0:t0 + 512].rearrange("(c p) t -> p c t", p=128), onT[:, :, :],
                              reads=ronT, writes=[self.r_on[t0 // 512]])

                    def tick():
                        ready = [f for (cnt, f) in dq if cnt <= 0]
                        rest = [[cnt - 1, f] for (cnt, f) in dq if cnt > 0]
                        dq[:] = rest
                        for f in ready:
                            f()

                    n = len(units)
                    load_ctx(0)
                    for i in range(n + LOOK):
                        if i < n:
                            ci = units[i][0]
                            if i == ctxs[ci]["start"] + LOOK and ci + 1 < len(ctxs):
                                load_ctx(ci + 1)
                            emit_S(i)
                        tick()
                        if i - LOOK >= 0:
                            emit_PV(i - LOOK)
                    while dq:
                        tick()
            S.barrier()

    def phase3(self, l):
        nc, S = self.nc, self.S
        last = (l == self.L - 1)
        x_src = self.x_in if l == 0 else self.xres
        with contextlib.ExitStack() as st:
            self.alloc_psum(st, 6, 2)
            (wo, rwo) = self.sb(st, "p3_wo", [128, 8, 1024], BF16)[0]
            (wd, rwd) = self.sb(st, "p3_wd", [128, 32, 1024], BF16)[0]
            wus = self.sb(st, "p3_wu", [128, 8, 512], BF16, 2)
            (pc, rpc) = self.sb(st, "p3_pc", [128, NPCOL], F32)[0]
            (nf, rnf) = self.sb(st, "p3_nf", [128, 1024], F32)[0]
            xts = self.sbm(st, "p3_xt", [128, 4, 1024], F32, 2, 4)
            (xn, rxn) = self.sbm(st, "p3_xn", [128, 4, 1024], BF16, 1, 4)[0]
            onTs = self.sb(st, "p3_onT", [128, 8, 512], BF16, 2)
            (hT, rhT) = self.sbm(st, "p3_hT", [128, 8, 512], BF16, 1, 8)[0]
            (uT, ruT) = self.sbm(st, "p3_uT", [128, 32, 512], BF16, 1, 32)[0]
            rls = self.sb(st, "p3_rl", [128, 512], BF16, 3)
            junk = None
            sss = self.sb(st, "p3_ss", [128, 8], F32, 2)
            rss_ = self.sb(st, "p3_rs", [128, 8], F32, 2)
            S.dma("sp", wo[:], self.wout_b[l], writes=[rwo])
            S.dma("sp", pc[:], self.pcol_d[l], writes=[rpc])
            S.dma("sp", nf[:], self.nfin_d[:, :], writes=[rnf])
            S.dma("sp", wd[:], self.wdown_b[l], writes=[rwd])
            tl = self.tiles()
            wu_ctr = [0]

            def load(i):
                si, s0, t0 = tl[i]
                xt, rxt = xts[i % 2]
                oT, roT = onTs[i % 2]
                S.dma("sp", xt[:], x_src[t0:t0 + 512, :].rearrange("(s p) d -> p s d", p=128),
                      reads=[self.r_x[t0 // 512]], writes=rxt)
                S.dma("sp", oT[:], self.onT[:, t0:t0 + 512].rearrange("(c p) t -> p c t", p=128),
                      reads=[self.r_on[t0 // 512]], writes=[roT])

            def load_wu(g):
                wu, rwu = wus[wu_ctr[0] % 2]
                wu_ctr[0] += 1
                S.dma("sp", wu[:], self.wup_b[l, g], writes=[rwu])
                return wu, rwu

            load(0)
            for i, (si, s0, t0) in enumerate(tl):
                ti = t0 // 512
                xt, rxt = xts[i % 2]
                oT, roT = onTs[i % 2]
                ss, rss = sss[i % 2]
                rs, rrs = rss_[i % 2]
                wu_next = load_wu(0)
                k = 0
                for sub in range(4):
                    for half in range(2):
                        ps, rps = self.next_ps()
                        for c in range(8):
                            S.op("pe", lambda: nc.tensor.matmul(ps[:, :], oT[:, c, sub * 128:(sub + 1) * 128], wo[:, c, half * 512:(half + 1) * 512],
                                                                start=(c == 0), stop=(c == 7)),
                                 reads=[roT, rwo], writes=[rps], signal=(c == 7))
                        S.op("dve", lambda: nc.vector.tensor_tensor(out=xt[:, sub, half * 512:(half + 1) * 512], in0=ps[:, :],
                                                                    in1=xt[:, sub, half * 512:(half + 1) * 512], op=ALU.add),
                             reads=[rps, rxt[sub]], writes=[rxt[sub]])
                        k += 1
                if i + 1 < len(tl):
                    load(i + 1)
                self.norm_transpose(xt, rxt, xn, rxn, ss, rss, rs, rrs, junk, hT, rhT, pc[:, PC_NMLP:PC_NMLP + 8], rpc)
                for g in range(8):
                    wu, rwu = wu_next
                    if g + 1 < 8:
                        wu_next = load_wu(g + 1)
                    for f in range(4):
                        fc = g * 4 + f
                        ps, rps = self.next_ps()
                        for c in range(8):
                            S.op("pe", lambda: nc.tensor.matmul(ps[:, :], wu[:, c, f * 128:(f + 1) * 128], hT[:, c, :],
                                                                start=(c == 0), stop=(c == 7)),
                                 reads=[rwu, rhT[c]], writes=[rps], signal=(c == 7))
                        rl, rrl = rls[fc % 3]
                        S.op("act", lambda: nc.scalar.activation(out=rl[:, :], in_=ps[:, :], func=AF.Relu), reads=[rps], writes=[rrl])
                        S.op("dve", lambda: nc.vector.tensor_tensor(out=uT[:, fc, :], in0=rl[:, :], in1=rl[:, :], op=ALU.mult),
                             reads=[rrl], writes=[ruT[fc]])
                for sub in range(4):
                    for half in range(2):
                        ps, rps = self.next_ps()
                        for fc in range(32):
                            S.op("pe", lambda: nc.tensor.matmul(ps[:, :], uT[:, fc, sub * 128:(sub + 1) * 128], wd[:, fc, half * 512:(half + 1) * 512],
                                                                start=(fc == 0), stop=(fc == 31)),
                                 reads=[ruT[fc], rwd], writes=[rps], signal=(fc == 31))
                        S.op("dve", lambda: nc.vector.tensor_tensor(out=xt[:, sub, half * 512:(half + 1) * 512], in0=ps[:, :],
                                                                    in1=xt[:, sub, half * 512:(half + 1) * 512], op=ALU.add),
                             reads=[rps, rxt[sub]], writes=[rxt[sub]])
                if not last:
                    S.dma("sp", self.xres[t0:t0 + 512, :].rearrange("(s p) d -> p s d", p=128), xt[:],
                          reads=rxt, writes=[self.r_x[ti]])
                else:
                    for sub in range(4):
                        S.op("act", lambda: nc.scalar.activation(out=xn[:, sub, :], in_=xt[:, sub, :], func=AF.Square, scale=1.0 / 32.0,
                                                                 accum_out=ss[:, 4 + sub:5 + sub]),
                             reads=[rxt[sub]], writes=[rss, rxn[sub]])
                    self.rstd_from_acc(ss, rss, rs, rrs, 4, 4)
                    for sub in range(4):
                        S.op("dve", lambda: nc.vector.scalar_tensor_tensor(out=xt[:, sub, :], in0=xt[:, sub, :], scalar=rs[:, 4 + sub:5 + sub],
                                                                           in1=nf[:, :], op0=ALU.mult, op1=ALU.mult),
                             reads=[rxt[sub], rrs, rnf], writes=[rxt[sub]])
                    S.dma("sp", self.y_out[t0:t0 + 512, :].rearrange("(s p) d -> p s d", p=128), xt[:], reads=rxt)
            S.barrier()


_PROG_CACHE = {}


def get_prog(seqs, depth):
    key = (tuple(seqs), depth)
    if key not in _PROG_CACHE:
        p = Prog(seqs, depth)
        p.build()
        _PROG_CACHE[key] = p
    return _PROG_CACHE[key]


def run_cores(x_list, shared, seqs, depth, core_ids):
    prog = get_prog(seqs, depth)
    shared = dict(shared)
    shared["rope"] = _rope_tables(max(seqs))
    in_maps = []
    for x in x_list:
        m = dict(shared)
        m["x"] = np.ascontiguousarray(x, dtype=np.float32)
        in_maps.append(m)
    res = run_bass_kernel_spmd(prog.nc, in_maps, core_ids=core_ids)
    return [r["y"] for r in res.results]


def kernel(**inputs):
    xp = np.asarray(inputs["x_prompt"], np.float32)
    xs = np.asarray(inputs["x_sample"], np.float32)
    shared = host_prep(inputs, DEPTH)
    seqs = [2048, 2048, 16384]
    x_list = []
    zeros = np.zeros_like(xs[0])
    for c in range(N_CORES):
        samp = xs[c // 4] if c % 4 == 0 else zeros
        x_list.append(np.concatenate([xp[2 * c], xp[2 * c + 1], samp], axis=0))
    ys = run_cores(x_list, shared, seqs, DEPTH, list(range(N_CORES)))
    y_prompt = np.empty_like(xp)
    y_sample = np.empty_like(xs)
    for c in range(N_CORES):
        y_prompt[2 * c] = ys[c][0:2048]
        y_prompt[2 * c + 1] = ys[c][2048:4096]
    y_sample[0] = ys[0][4096:]
    y_sample[1] = ys[4][4096:]
    return (y_prompt, y_sample)
```

```python
import contextlib
import numpy as np
import concourse.bass as bass
import concourse.mybir as mybir
from concourse.bass_utils import run_bass_kernel_spmd

F32 = mybir.dt.float32
BF16 = mybir.dt.bfloat16
AF = mybir.ActivationFunctionType
ALU = mybir.AluOpType

D_MODEL = 1024
DEPTH = 4
D_FF = 4096
EPS = 1e-5
N_CORES = 8
FM_COLS = 1408
TM0 = 1408
KR0 = 2432
WIN_COLS = 2624
NPCOL = 40
PC_NATT, PC_NMLP, PC_GN, PC_QN, PC_KVN, PC_SINK = 0, 8, 16, 24, 26, 27
MA_LEN, MB_LEN, MD_ARR = 2944, 1152, 1408
MD_LEN = MD_ARR + 12 * 512
MASK_MAX = MD_LEN
ROW_AQ, ROW_AK, ROW_BQ, ROW_BK, ROW_DQ, ROW_DK = 0, 256, 512, 768, 896, 1152
NVH = 14
import os as _os
DUMMY_COLS = int(_os.environ.get("KDUM", "0"))


class _Stop(Exception):
    pass


class Res:
    __slots__ = ("w", "r")

    def __init__(self):
        self.w = {}
        self.r = {}


class Sched:
    def __init__(self, nc, stack, n_dma_sems=12):
        self.nc = nc
        self.eng = {"pe": nc.tensor, "act": nc.scalar, "dve": nc.vector, "pool": nc.gpsimd, "sp": nc.sync}
        self.names = list(self.eng.keys())
        self.sems = {}
        for n in self.names:
            self.sems[n] = stack.enter_context(nc.semaphore("s_" + n))
        self.cnt = {n: 0 for n in self.names}
        self.dq = {}
        for q in ("sp", "pool"):
            lst = []
            for i in range(n_dma_sems):
                key = "d_%s_%d" % (q, i)
                self.sems[key] = stack.enter_context(nc.semaphore(key))
                self.cnt[key] = 0
                lst.append(key)
            self.dq[q] = [lst, 0]
        self.waited = {n: {} for n in self.names}
        self.pending = {n: [] for n in self.names}
        self.nops = 0

    def _needs(self, X, reads, writes):
        need = {}
        for r in reads:
            for k, v in r.w.items():
                if need.get(k, 0) < v:
                    need[k] = v
        for w in writes:
            for k, v in w.w.items():
                if need.get(k, 0) < v:
                    need[k] = v
            for k, v in w.r.items():
                if need.get(k, 0) < v:
                    need[k] = v
        return need

    def _emit_waits(self, X, need):
        wd = self.waited[X]
        e = self.eng[X]
        for k, v in need.items():
            if k == X and X == "pe":
                continue
            if wd.get(k, 0) < v:
                e.wait_ge(self.sems[k], v)
                wd[k] = v

    def _commit(self, ev_key, ev_val, reads, writes):
        for r in reads:
            if r.r.get(ev_key, 0) < ev_val:
                r.r[ev_key] = ev_val
        for w in writes:
            w.w = {ev_key: ev_val}
            w.r = {}

    def op(self, X, fn, reads=(), writes=(), signal=True):
        need = self._needs(X, reads, writes)
        self._emit_waits(X, need)
        inst = fn()
        self.nops += 1
        if signal:
            self.cnt[X] += 1
            inst.then_inc(self.sems[X], 1)
            c = self.cnt[X]
            for (rr, ww) in self.pending[X]:
                self._commit(X, c, rr, ww)
            self.pending[X] = []
            self._commit(X, c, reads, writes)
        else:
            self.pending[X].append((reads, writes))
        return inst

    def dma(self, q, out, in_, reads=(), writes=(), **kw):
        assert not self.pending[q]
        need = self._needs(q, reads, writes)
        lst, idx = self.dq[q]
        key = lst[idx % len(lst)]
        self.dq[q][1] = idx + 1
        if self.cnt[key] > 0:
            need[key] = max(need.get(key, 0), self.cnt[key])
        self._emit_waits(q, need)
        inst = self.eng[q].dma_start(out=out, in_=in_, **kw)
        self.nops += 1
        self.cnt[key] += 16
        inst.then_inc(self.sems[key], 16)
        self._commit(key, self.cnt[key], reads, writes)
        return inst

    def barrier(self):
        for n in self.names:
            assert not self.pending[n], n
        allev = {k: v for k, v in self.cnt.items() if v > 0}
        for n in self.names:
            need = {k: v for k, v in allev.items() if k != n}
            wd = self.waited[n]
            for k, v in need.items():
                if wd.get(k, 0) < v:
                    self.eng[n].wait_ge(self.sems[k], v)
                    wd[k] = v

    def final_wait(self):
        self.barrier()


def _alibi_slopes():
    s = np.exp2(-8.0 * np.arange(1, 9, dtype=np.float64) / 8.0)
    return s[1::2], s[0::2]


def _mask_a():
    sa, _ = _alibi_slopes()
    d = np.arange(-4096, 4097)
    ad = np.abs(d)
    m = ((ad <= 64).astype(np.float64) + ((d % 4 == 0) & (ad <= 256)) + ((d % 16 == 0) & (ad <= 1024)))
    out = np.zeros((4, 128, MA_LEN), np.float32)
    i = np.arange(128)[:, None]
    n = np.arange(MA_LEN)[None, :]
    dd = i + 1408 - n
    for h in range(4):
        f = m * np.exp(-sa[h] * ad)
        out[h] = f[dd + 4096]
    return out


def _mask_b():
    _, sb = _alibi_slopes()
    out = np.zeros((4, 128, MB_LEN), np.float32)
    i = np.arange(128)[:, None]
    n = np.arange(MB_LEN)[None, :]
    dd = i + 512 - n
    ad = np.abs(dd)
    for h in range(4):
        out[h] = np.where(ad <= 128, np.exp(-sb[h] * ad), 0.0)
    return out


def _d_tables():
    i = np.arange(128)
    rk_l = (i // 64)[:, None, None]
    ck = (i % 64)[:, None, None]
    n = np.arange(22)[None, :, None]
    cq = np.arange(64)[None, None, :]
    delta = rk_l + 10 - n + 0 * cq
    dr_idx = np.clip(delta, -7, 7) + 7
    dc_idx = np.clip(ck - cq, -15, 15) + 15 + 0 * n
    cs = np.clip(cq - 8, 0, 64 - 16)
    colmask = ((ck >= cs) & (ck < cs + 16)) & (n >= 0)
    cw = (colmask & (delta >= -4) & (delta <= 3)).astype(np.float32).reshape(128, MD_ARR)
    cf = (colmask & (delta >= -7) & (delta <= 7)).astype(np.float32).reshape(128, MD_ARR)
    vb = np.zeros((12, 128, 512), np.float32)
    rkl = (i // 64)[:, None]
    rql = (np.arange(512) // 64)[None, :]
    for kt in range(6):
        rk = 2 * kt + rkl
        rs = np.maximum(rql - 4, 0)
        vb[kt] = ((rk >= rs) & (rk < rs + 8)).astype(np.float32)
        rk2 = -4 + 2 * kt + rkl
        rs2 = np.minimum(rql - 4, 0)
        vb[6 + kt] = ((rk2 >= rs2) & (rk2 < rs2 + 8)).astype(np.float32)
    return dr_idx.reshape(128, MD_ARR), dc_idx.reshape(128, MD_ARR), cw, cf, vb


def _rope_tables(smax):
    inv_freq = (1.0 / (np.float32(10000.0) ** (np.arange(0, 32, 2, dtype=np.float32) / np.float32(32)))).astype(np.float32)
    ang = (np.arange(smax, dtype=np.float32)[:, None] * inv_freq[None, :]).astype(np.float32)
    c = np.cos(ang.astype(np.float64)).astype(np.float32).T
    s = np.sin(ang.astype(np.float64)).astype(np.float32).T
    out = np.zeros((2, 32, smax), np.float32)
    out[0, :16] = c
    out[0, 16:] = c
    out[1, :16] = -s
    out[1, 16:] = s
    return out


def host_prep(inp, depth):
    L = depth
    w_in = np.asarray(inp["w_in"], np.float32)[:L]
    sec = np.cumsum([0, 256, 256, 256, 256, 128, 128, 256, 128, 32, 256, 256, 256])
    a_q, a_k, a_v, b_q, b_k, b_v, c_q, c_kv, c_kr, d_q, d_k, d_v = [w_in[:, :, sec[i]:sec[i + 1]] for i in range(12)]
    z64 = np.zeros((L, 1024, 64), np.float32)
    c_kr_sw = np.concatenate([c_kr[:, :, 16:32], c_kr[:, :, 0:16]], axis=-1)
    w_in_p = np.concatenate([a_q, a_k, b_q, b_k, d_q, d_k, a_v, b_v, d_v, c_q, c_kv, z64, c_kr, z64, c_kr_sw], axis=-1)
    assert w_in_p.shape[-1] == WIN_COLS
    wq = np.asarray(inp["w_q_up"], np.float32)[:L].reshape(L, 256, 4, 96)
    z = np.zeros((L, 256, 4, 64), np.float32)
    pe = wq[..., 64:96]
    pe_sw = np.concatenate([pe[..., 16:32], pe[..., 0:16]], axis=-1)
    wq_p = np.concatenate([wq, z, pe_sw], axis=-1).reshape(L, 256, 768)
    wkv = np.asarray(inp["w_kv_up"], np.float32)[:L].reshape(L, 128, 4, 128)
    wkv_p = np.concatenate([wkv[..., :64].reshape(L, 128, 256), wkv[..., 64:].reshape(L, 128, 256)], axis=-1)
    pcol = np.zeros((L, 128, NPCOL), np.float32)
    pcol[:, :, PC_NATT:PC_NATT + 8] = np.asarray(inp["norm_attn"], np.float32)[:L].reshape(L, 8, 128).transpose(0, 2, 1)
    pcol[:, :, PC_NMLP:PC_NMLP + 8] = np.asarray(inp["norm_mlp"], np.float32)[:L].reshape(L, 8, 128).transpose(0, 2, 1)
    pcol[:, :, PC_GN:PC_GN + 8] = np.asarray(inp["group_norm"], np.float32)[:L].reshape(L, 8, 128).transpose(0, 2, 1)
    pcol[:, :, PC_QN:PC_QN + 2] = np.asarray(inp["mla_q_norm"], np.float32)[:L].reshape(L, 2, 128).transpose(0, 2, 1)
    pcol[:, :, PC_KVN] = np.asarray(inp["mla_kv_norm"], np.float32)[:L]
    pcol[:, :, PC_SINK:PC_SINK + 4] = np.asarray(inp["sink_logits"], np.float32)[:L][:, None, :]
    nfin = np.broadcast_to(np.asarray(inp["norm_final"], np.float32)[None, :], (128, 1024)).copy()
    dr_idx, dc_idx, cw, cf, vb = _d_tables()
    rpb = np.asarray(inp["na_rpb"], np.float32)[:L]
    biasf = rpb[:, :, dr_idx, dc_idx]
    return {
        "w_in_p": np.ascontiguousarray(w_in_p),
        "w_q_p": np.ascontiguousarray(wq_p),
        "w_kv_p": np.ascontiguousarray(wkv_p),
        "w_out": np.ascontiguousarray(np.asarray(inp["w_out"], np.float32)[:L]),
        "w_up": np.ascontiguousarray(np.asarray(inp["w_mlp_up"], np.float32)[:L]),
        "w_down": np.ascontiguousarray(np.asarray(inp["w_mlp_down"], np.float32)[:L]),
        "pcol": pcol,
        "nfin": nfin,
        "ident": np.eye(128, dtype=np.float32),
        "mask_a": _mask_a(),
        "mask_b": _mask_b(),
        "biasf": np.ascontiguousarray(biasf),
        "dconst": np.ascontiguousarray(np.concatenate([cw, cf, vb.transpose(1, 0, 2).reshape(128, 12 * 512)], axis=1)),
    }


class Prog:
    def __init__(self, seqs, depth):
        self.seqs = list(seqs)
        self.L = depth
        self.T = sum(seqs)
        self.smax = max(seqs)
        self.starts = [int(x) for x in np.cumsum([0] + self.seqs[:-1])]

    def sb(self, st, name, shape, dt, n=1):
        out = []
        for i in range(n):
            t = st.enter_context(self.nc.sbuf_tensor("%s_%d_%d" % (name, self.uid, i), shape, dt))
            out.append((t, Res()))
        self.uid += 1
        return out

    def build(self):
        nc = bass.Bass("TRN2", target_bir_lowering=False)
        self.nc = nc
        self.uid = 0
        L, T = self.L, self.T

        def din(name, shape):
            return nc.dram_tensor(name, shape, F32, kind="ExternalInput").ap()

        def dscr(name, shape, dt):
            return nc.dram_tensor(name, shape, dt, kind="Internal").ap()

        self.x_in = din("x", [T, 1024])
        self.w_in_p = din("w_in_p", [L, 1024, WIN_COLS])
        self.w_q_p = din("w_q_p", [L, 256, 768])
        self.w_kv_p = din("w_kv_p", [L, 128, 512])
        self.w_out = din("w_out", [L, 1024, 1024])
        self.w_up = din("w_up", [L, 1024, 4096])
        self.w_down = din("w_down", [L, 4096, 1024])
        self.pcol_d = din("pcol", [L, 128, NPCOL])
        self.nfin_d = din("nfin", [128, 1024])
        self.ident_d = din("ident", [128, 128])
        self.mask_a_d = din("mask_a", [4, 128, MA_LEN])
        self.mask_b_d = din("mask_b", [4, 128, MB_LEN])
        self.biasf_d = din("biasf", [L, 4, 128, MD_ARR])
        self.dconst_d = din("dconst", [128, 2 * MD_ARR + 12 * 512])
        self.rope_d = din("rope", [2, 32, self.smax])
        self.y_out = nc.dram_tensor("y", [T, 1024], F32, kind="ExternalOutput").ap()

        self.xres = dscr("xres", [T, 1024], F32)
        self.win_b = dscr("win_b", [L, 128, 8, WIN_COLS], BF16)
        self.wq_b = dscr("wq_b", [L, 128, 2, 768], BF16)
        self.wkv_b = dscr("wkv_b", [L, 128, 512], BF16)
        self.wout_b = dscr("wout_b", [L, 128, 8, 1024], BF16)
        self.wup_b = dscr("wup_b", [L, 8, 128, 8, 512], BF16)
        self.wdown_b = dscr("wdown_b", [L, 128, 32, 1024], BF16)
        self.maska_b = dscr("maska_b", [4, 128, MA_LEN], BF16)
        self.maskb_b = dscr("maskb_b", [4, 128, MB_LEN], BF16)
        self.maskd_b = dscr("maskd_b", [4, 128, MD_LEN], BF16)
        self.qkT = dscr("qkT", [FM_COLS, T], BF16)
        self.qTc = dscr("qTc", [4, 96, T], BF16)
        self.kTc = dscr("kTc", [4, 96, T], BF16)
        self.vsc = dscr("vsc", [T, NVH, 65], BF16)
        self.onT = dscr("onT", [1024, T], BF16)
        nt = T // 512
        self.r_x = [Res() for _ in range(nt)]
        self.r_qk = [Res() for _ in range(nt)]
        self.r_on = [Res() for _ in range(nt)]
        self.r_qc = [Res() for _ in range(nt)]
        self.r_kc = [Res() for _ in range(nt)]
        self.r_v = [Res() for _ in range(nt)]
        self.r_w = Res()
        self.r_maskd = Res()

        with contextlib.ExitStack() as top:
            self.S = Sched(nc, top)
            cst = self.sb(top, "identf", [128, 128], F32)[0]
            self.identf, self.r_identf = cst
            cst = self.sb(top, "identb", [128, 128], BF16)[0]
            self.identb, self.r_identb = cst
            cst = self.sb(top, "neghalf", [128, 8], F32)[0]
            self.neghalf, self.r_neghalf = cst
            S = self.S
            S.dma("sp", self.identf[:], self.ident_d[:, :], writes=[self.r_identf])
            S.op("dve", lambda: nc.vector.tensor_copy(out=self.identb[:], in_=self.identf[:]),
                 reads=[self.r_identf], writes=[self.r_identb])
            S.op("dve", lambda: nc.vector.memset(self.neghalf[:], -0.5), writes=[self.r_neghalf])
            import os
            kstop = int(os.environ.get("KSTOP", "99"))
            if kstop >= 1:
                self.prologue()
            for l in range(L):
                if kstop >= 2:
                    self.phase_dmask(l)
                if kstop >= 3:
                    self.phase1(l)
                if kstop >= 4:
                    self.phase2(l)
                if kstop >= 5:
                    self.phase3(l)
            S.final_wait()
        return nc

    def alloc_psum(self, st, nps, npb):
        nc = self.nc
        self.ps = [(st.enter_context(nc.psum_tensor("ps%d_%d" % (i, self.uid), [128, 512], F32)), Res()) for i in range(nps)]
        self.pb = [(st.enter_context(nc.psum_tensor("pb%d_%d" % (i, self.uid), [128, 1024], BF16)), Res()) for i in range(npb)]
        self.uid += 1
        self.ps_i = 0
        self.pb_i = 0

    def next_ps(self):
        p = self.ps[self.ps_i % len(self.ps)]
        self.ps_i += 1
        return p

    def next_pb(self):
        p = self.pb[self.pb_i % len(self.pb)]
        self.pb_i += 1
        return p

    def prologue(self):
        nc, S, L = self.nc, self.S, self.L
        with contextlib.ExitStack() as st:
            stg = self.sb(st, "cv_f", [128, 4096], F32, 3)
            stb = self.sb(st, "cv_b", [128, 4096], BF16, 3)
            k = [0]

            def conv(src, dst, shape):
                n = int(np.prod(shape))
                assert n <= 4096
                (f, rf), (b, rb) = stg[k[0] % 3], stb[k[0] % 3]
                if len(shape) == 1:
                    fv, bv = f[:, 0:n], b[:, 0:n]
                else:
                    fv = f[:, 0:n].rearrange("p (a b) -> p a b", b=shape[1])
                    bv = b[:, 0:n].rearrange("p (a b) -> p a b", b=shape[1])
                S.dma("sp", fv, src, writes=[rf])
                if k[0] % 2 == 0:
                    S.op("dve", lambda: nc.vector.tensor_copy(out=b[:, 0:n], in_=f[:, 0:n]), reads=[rf], writes=[rb])
                else:
                    S.op("act", lambda: nc.scalar.copy(out=b[:, 0:n], in_=f[:, 0:n]), reads=[rf], writes=[rb])
                S.dma("pool", dst, bv, reads=[rb])
                k[0] += 1

            for l in range(L):
                wv = self.w_in_p[l].rearrange("(c p) n -> p c n", p=128)
                for c in range(8):
                    conv(wv[:, c, :], self.win_b[l, :, c, :], [WIN_COLS])
                conv(self.w_q_p[l].rearrange("(c p) n -> p c n", p=128), self.wq_b[l], [2, 768])
                conv(self.w_kv_p[l], self.wkv_b[l], [512])
                wv = self.w_out[l].rearrange("(c p) n -> p c n", p=128)
                for c0 in range(0, 8, 4):
                    conv(wv[:, c0:c0 + 4, :], self.wout_b[l, :, c0:c0 + 4, :], [4, 1024])
                wv = self.w_up[l].rearrange("(c p) n -> p c n", p=128)
                for g in range(8):
                    conv(wv[:, :, g * 512:(g + 1) * 512], self.wup_b[l, g], [8, 512])
                wv = self.w_down[l].rearrange("(f p) n -> p f n", p=128)
                for f0 in range(0, 32, 4):
                    conv(wv[:, f0:f0 + 4, :], self.wdown_b[l, :, f0:f0 + 4, :], [4, 1024])
            for h in range(4):
                conv(self.mask_a_d[h], self.maska_b[h], [MA_LEN])
                conv(self.mask_b_d[h], self.maskb_b[h], [MB_LEN])
            S.barrier()

    def phase_dmask(self, l):
        nc, S = self.nc, self.S
        with contextlib.ExitStack() as st:
            (dc, rdc) = self.sb(st, "dm_c", [128, 2 * MD_ARR + 12 * 512], F32)[0]
            bf = self.sb(st, "dm_bf", [128, MD_ARR], F32, 2)
            ex = self.sb(st, "dm_ex", [128, MD_ARR], F32, 2)
            full = self.sb(st, "dm_full", [128, MD_ARR], F32, 2)
            mo = self.sb(st, "dm_out", [128, MD_LEN], BF16, 2)
            S.dma("sp", dc[:], self.dconst_d[:, :], writes=[rdc])
            for h in range(4):
                (b, rb), (e, re), (fu, rfu), (m, rm) = bf[h % 2], ex[h % 2], full[h % 2], mo[h % 2]
                S.dma("sp", b[:], self.biasf_d[l, h], writes=[rb])
                S.op("act", lambda: nc.scalar.activation(out=e[:], in_=b[:], func=AF.Exp), reads=[rb], writes=[re])
                S.op("dve", lambda: nc.vector.tensor_tensor(out=m[:, 0:MD_ARR], in0=e[:], in1=dc[:, 0:MD_ARR], op=ALU.mult),
                     reads=[re, rdc], writes=[rm])
                S.op("dve", lambda: nc.vector.tensor_tensor(out=fu[:], in0=e[:], in1=dc[:, MD_ARR:2 * MD_ARR], op=ALU.mult),
                     reads=[re, rdc], writes=[rfu])
                for kt in range(6):
                    for (which, dr) in ((0, 2 * kt), (1, -4 + 2 * kt)):
                        o0 = MD_ARR + (which * 6 + kt) * 512
                        a0 = (10 - dr) * 64
                        v0 = 2 * MD_ARR + (which * 6 + kt) * 512
                        S.op("dve", lambda o0=o0, a0=a0, v0=v0: nc.vector.tensor_tensor(
                            out=m[:, o0:o0 + 512], in0=fu[:, a0:a0 + 512], in1=dc[:, v0:v0 + 512], op=ALU.mult),
                            reads=[rfu, rdc], writes=[rm])
                S.dma("pool", self.maskd_b[h], m[:], reads=[rm])
            S.barrier()

    def sbm(self, st, name, shape, dt, n, nres):
        out = []
        for i in range(n):
            t = st.enter_context(self.nc.sbuf_tensor("%s_%d_%d" % (name, self.uid, i), shape, dt))
            out.append((t, [Res() for _ in range(nres)]))
        self.uid += 1
        return out

    def rstd_from_acc(self, acc, racc, out, rout, c0, n):
        nc, S = self.nc, self.S
        S.op("dve", lambda: nc.vector.tensor_scalar(out=out[:, c0:c0 + n], in0=acc[:, c0:c0 + n], scalar1=EPS, scalar2=None, op0=ALU.add),
             reads=[racc], writes=[rout])
        S.op("pool", lambda: nc.gpsimd.tensor_tensor(out=out[:, c0:c0 + n], in0=out[:, c0:c0 + n], in1=self.neghalf[:, 0:n], op=ALU.pow),
             reads=[rout, self.r_neghalf], writes=[rout])

    def evac(self, i, out, in_, reads, writes, scale=None):
        nc, S = self.nc, self.S
        import os
        kev = os.environ.get("KEV")
        if kev == "0":
            return
        if kev == "1":
            i = 1
        if kev == "2":
            i = 0
        if kev == "3" and scale is None:
            i = 1
        if kev == "4":
            i = 1 if scale is not None else 0
        if i % 2 == 0:
            if scale is None:
                S.op("act", lambda: nc.scalar.activation(out=out, in_=in_, func=AF.Copy), reads=reads, writes=writes)
            else:
                S.op("act", lambda: nc.scalar.activation(out=out, in_=in_, func=AF.Copy, scale=scale), reads=reads, writes=writes)
        else:
            if scale is None:
                S.op("dve", lambda: nc.vector.tensor_copy(out=out, in_=in_), reads=reads, writes=writes)
            else:
                S.op("dve", lambda: nc.vector.tensor_scalar(out=out, in0=in_, scalar1=scale, scalar2=None, op0=ALU.mult),
                     reads=reads, writes=writes)

    def norm_front(self, xt, rxt, xn, rxn, ss, rss, rs, rrs, D=1024):
        nc, S = self.nc, self.S
        sc = float(D) ** -0.5
        for sub in range(4):
            S.op("act", lambda: nc.scalar.activation(out=xn[:, sub, :], in_=xt[:, sub, :], func=AF.Square, scale=sc,
                                                     accum_out=ss[:, sub:sub + 1]),
                 reads=[rxt[sub]], writes=[rss, rxn[sub]])
        self.rstd_from_acc(ss, rss, rs, rrs, 0, 4)
        for sub in range(4):
            S.op("dve", lambda: nc.vector.tensor_scalar(out=xn[:, sub, :], in0=xt[:, sub, :], scalar1=rs[:, sub:sub + 1],
                                                        scalar2=None, op0=ALU.mult),
                 reads=[rxt[sub], rrs], writes=[rxn[sub]])

    def norm_tpose(self, xn, rxn, hT, rhT, gain, rgain, D=1024):
        nc, S = self.nc, self.S
        for c in range(D // 128):
            pb, rpb = self.next_pb()
            for sub in range(4):
                S.op("pe", lambda: nc.tensor.transpose(out=pb[:, sub * 128:(sub + 1) * 128], in_=xn[:, sub, c * 128:(c + 1) * 128],
                                                       identity=self.identb[:]),
                     reads=[rxn[sub], self.r_identb], writes=[rpb], signal=(sub == 3))
            self.evac(c, hT[:, c, :], pb[:, 0:512], [rpb, rgain], [rhT[c]], scale=gain[:, c:c + 1])

    def chk(self, n):
        import os
        if int(os.environ.get('KSUB', '99')) < n:
            raise _Stop()

    def tiles(self):
        out = []
        for si, (s0, sl) in enumerate(zip(self.starts, self.seqs)):
            for t0 in range(s0, s0 + sl, 512):
                out.append((si, s0, t0))
        return out

    def phase1(self, l):
        nc, S = self.nc, self.S
        x_src = self.x_in if l == 0 else self.xres
        with contextlib.ExitStack() as st:
            self.alloc_psum(st, 6, 2)
            (win, rwin) = self.sb(st, "p1_win", [128, 8, WIN_COLS], BF16)[0]
            (wq, rwq) = self.sb(st, "p1_wq", [128, 2, 768], BF16)[0]
            (wkv, rwkv) = self.sb(st, "p1_wkv", [128, 512], BF16)[0]
            (pc, rpc) = self.sb(st, "p1_pc", [128, NPCOL], F32)[0]
            xts = self.sbm(st, "p1_xt", [128, 4, 1024], F32, 2, 4)
            xns = self.sbm(st, "p1_xn", [128, 4, 1024], BF16, 2, 4)
            hTs = self.sbm(st, "p1_hT", [128, 8, 512], BF16, 2, 8)
            junk = None
            (junk2, _) = self.sb(st, "p1_junk2", [128, 256], BF16)[0]
            sss = self.sb(st, "p1_ss", [128, 8], F32, 2)
            rss_ = self.sb(st, "p1_rs", [128, 8], F32, 2)
            qks = self.sbm(st, "p1_qk", [128, 11, 512], BF16, 2, 11)
            vss = self.sbm(st, "p1_vs", [128, 4, NVH, 65], BF16, 2, 12)
            cns = self.sbm(st, "p1_cn", [128, 4, 384], BF16, 2, 8)
            cqTs = self.sbm(st, "p1_cqT", [128, 3, 512], BF16, 2, 3)
            ssc = self.sbm(st, "p1_ssc", [128, 8], F32, 2, 4)
            rsc = self.sbm(st, "p1_rsc", [128, 8], F32, 2, 4)
            rts = self.sb(st, "p1_rt", [128, 2, 512], F32, 2)
            tmps = self.sb(st, "p1_tmp", [128, 512], F32, 2)
            kpes = self.sb(st, "p1_kpe", [128, 512], F32, 2)
            qTs = self.sbm(st, "p1_qT", [128, 4, 512], BF16, 2, 8)
            kTs = self.sbm(st, "p1_kT", [128, 4, 512], BF16, 2, 8)

            S.dma("sp", win[:], self.win_b[l], writes=[rwin])
            S.dma("sp", wq[:], self.wq_b[l], writes=[rwq])
            S.dma("sp", wkv[:], self.wkv_b[l], writes=[rwkv])
            S.dma("sp", pc[:], self.pcol_d[l], writes=[rpc])
            for (v, rv) in vss:
                S.op("dve", lambda: nc.vector.memset(v[:], 1.0), writes=rv)

            tl = self.tiles()

            def load_x(i):
                si, s0, t0 = tl[i]
                xt, rxt = xts[i % 2]
                S.dma("sp", xt[:], x_src[t0:t0 + 512, :].rearrange("(s p) d -> p s d", p=128),
                      reads=[self.r_x[t0 // 512]], writes=rxt)

            def load_rt(i):
                si, s0, t0 = tl[i]
                rt, rrt = rts[i % 2]
                p0 = t0 - s0
                S.dma("sp", rt[64:96, :, :], self.rope_d[:, :, p0:p0 + 512].rearrange("a r t -> r a t"), writes=[rrt])

            def front(i):
                xt, rxt = xts[i % 2]
                xn, rxn = xns[i % 2]
                ss, rss = sss[i % 2]
                rs, rrs = rss_[i % 2]
                self.norm_front(xt, rxt, xn, rxn, ss, rss, rs, rrs)

            load_x(0)
            load_rt(0)
            if len(tl) > 1:
                load_x(1)
            front(0)
            tk = 0
            for i, (si, s0, t0) in enumerate(tl):
                if i + 1 < len(tl):
                    load_rt(i + 1)
                b = i % 2
                xt, rxt = xts[b]
                xn, rxn = xns[b]
                hT, rhT = hTs[b]
                ss, rss = sss[b]
                rs, rrs = rss_[b]
                qk, rqk = qks[b]
                vs, rvs = vss[b]
                cn, rcn = cns[b]
                cqT, rcqT = cqTs[b]
                sc_, rsc_ = ssc[b]
                rc_, rrc_ = rsc[b]
                rt, rrt = rts[b]
                qT, rqT = qTs[b]
                kT, rkT = kTs[b]
                ti = t0 // 512
                try:
                  self.phase1_tile(locals())
                except _Stop:
                  break
            S.barrier()

    def phase1_tile(self, L_):
                nc, S = self.nc, self.S
                globals_ = L_
                (xt, rxt, xn, rxn, ss, rss, rs, rrs, junk, hT, rhT, pc, rpc, win, rwin, qk, rqk, t0, ti, vs, rvs, junk2, sc_, rsc_, rc_, rrc_, cn, rcn, cqT, rcqT, tmps, kpes, rt, rrt, kT, rkT, qT, rqT, wq, rwq, wkv, rwkv) = [L_[k] for k in 'xt rxt xn rxn ss rss rs rrs junk hT rhT pc rpc win rwin qk rqk t0 ti vs rvs junk2 sc_ rsc_ rc_ rrc_ cn rcn cqT rcqT tmps kpes rt rrt kT rkT qT rqT wq rwq wkv rwkv'.split()]
                tk = 0
                self.chk(1)
                self.norm_tpose(xn, rxn, hT, rhT, pc[:, PC_NATT:PC_NATT + 8], rpc)
                if L_["i"] + 1 < len(L_["tl"]):
                    L_["front"](L_["i"] + 1)
                if L_["i"] + 2 < len(L_["tl"]):
                    L_["load_x"](L_["i"] + 2)
                self.chk(2)
                for j in range(11):
                    ps, rps = self.next_ps()
                    for kc in range(8):
                        S.op("pe", lambda: nc.tensor.matmul(ps[:, :], win[:, kc, j * 128:(j + 1) * 128], hT[:, kc, :],
                                                            start=(kc == 0), stop=(kc == 7)),
                             reads=[rwin, rhT[kc]], writes=[rps], signal=(kc == 7))
                    self.evac(j, qk[:, j, :], ps[:, :], [rps], [rqk[j]])
                import os
                if not os.environ.get("KNOSTORE"):
                    S.dma("sp", self.qkT[:, t0:t0 + 512].rearrange("(j p) t -> p j t", p=128), qk[:, :, :],
                          reads=rqk, writes=[self.r_qk[ti]])
                self.chk(3)
                for sub in range(4):
                    psA, rpsA = self.next_ps()
                    for kc in range(8):
                        S.op("pe", lambda: nc.tensor.matmul(psA[:, :], hT[:, kc, sub * 128:(sub + 1) * 128], win[:, kc, TM0:TM0 + 512],
                                                            start=(kc == 0), stop=(kc == 7)),
                             reads=[rwin, rhT[kc]], writes=[rpsA], signal=(kc == 7))
                    self.evac(0, vs[:, sub, 0:8, 0:64], psA[:, :].rearrange("p (h d) -> p h d", d=64), [rpsA], [rvs[sub * 3]])
                    psB, rpsB = self.next_ps()
                    for kc in range(8):
                        S.op("pe", lambda: nc.tensor.matmul(psB[:, :], hT[:, kc, sub * 128:(sub + 1) * 128], win[:, kc, TM0 + 512:TM0 + 1024],
                                                            start=(kc == 0), stop=(kc == 7)),
                             reads=[rwin, rhT[kc]], writes=[rpsB], signal=(kc == 7))
                    self.evac(0, vs[:, sub, 8:10, 0:64], psB[:, 0:128].rearrange("p (h d) -> p h d", d=64), [rpsB], [rvs[sub * 3 + 1]])
                    S.op("act", lambda: nc.scalar.activation(out=junk2[:, 0:256], in_=psB[:, 128:384], func=AF.Square, scale=1.0 / 16.0,
                                                             accum_out=sc_[:, 2 * sub:2 * sub + 1]),
                         reads=[rpsB], writes=[rsc_[sub]])
                    S.op("act", lambda: nc.scalar.activation(out=junk2[:, 0:128], in_=psB[:, 384:512], func=AF.Square, scale=128.0 ** -0.5,
                                                             accum_out=sc_[:, 2 * sub + 1:2 * sub + 2]),
                         reads=[rpsB], writes=[rsc_[sub]])
                    self.rstd_from_acc(sc_, rsc_[sub], rc_, rrc_[sub], 2 * sub, 2)
                    S.op("dve", lambda: nc.vector.tensor_scalar(out=cn[:, sub, 0:256], in0=psB[:, 128:384], scalar1=rc_[:, 2 * sub:2 * sub + 1],
                                                                scalar2=None, op0=ALU.mult),
                         reads=[rpsB, rrc_[sub]], writes=[rcn[2 * sub]])
                    S.op("dve", lambda: nc.vector.tensor_scalar(out=cn[:, sub, 256:384], in0=psB[:, 384:512], scalar1=rc_[:, 2 * sub + 1:2 * sub + 2],
                                                                scalar2=None, op0=ALU.mult),
                         reads=[rpsB, rrc_[sub]], writes=[rcn[2 * sub + 1]])
                self.chk(4)
                for c in range(3):
                    pb, rpb = self.next_pb()
                    for sub in range(4):
                        S.op("pe", lambda: nc.tensor.transpose(out=pb[:, sub * 128:(sub + 1) * 128], in_=cn[:, sub, c * 128:(c + 1) * 128],
                                                               identity=self.identb[:]),
                             reads=[rcn[2 * sub + (1 if c == 2 else 0)], self.r_identb], writes=[rpb], signal=(sub == 3))
                    self.evac(c, cqT[:, c, :], pb[:, 0:512], [rpb, rpc], [rcqT[c]], scale=pc[:, PC_QN + c:PC_QN + c + 1])
                self.chk(5)
                tmp, rtmp = tmps[tk % 2]
                kpe, rkpe = kpes[tk % 2]
                tk += 1
                ps1, rps1 = self.next_ps()
                for kc in range(8):
                    S.op("pe", lambda: nc.tensor.matmul(ps1[0:96, :], win[:, kc, KR0:KR0 + 96], hT[:, kc, :], start=(kc == 0), stop=(kc == 7)),
                         reads=[rwin, rhT[kc]], writes=[rps1], signal=(kc == 7))
                ps2, rps2 = self.next_ps()
                for kc in range(8):
                    S.op("pe", lambda: nc.tensor.matmul(ps2[0:96, :], win[:, kc, KR0 + 96:KR0 + 192], hT[:, kc, :], start=(kc == 0), stop=(kc == 7)),
                         reads=[rwin, rhT[kc]], writes=[rps2], signal=(kc == 7))
                S.op("dve", lambda: nc.vector.tensor_tensor(out=tmp[64:96, :], in0=ps2[64:96, :], in1=rt[64:96, 1, :], op=ALU.mult),
                     reads=[rps2, rrt], writes=[rtmp])
                S.op("dve", lambda: nc.vector.tensor_tensor(out=kpe[64:96, :], in0=ps1[64:96, :], in1=rt[64:96, 0, :], op=ALU.mult),
                     reads=[rps1, rrt], writes=[rkpe])
                for h in range(4):
                    S.op("pool", lambda: nc.gpsimd.tensor_tensor(out=kT[64:96, h, :], in0=kpe[64:96, :], in1=tmp[64:96, :], op=ALU.add),
                         reads=[rkpe, rtmp], writes=[rkT[2 * h + 1]])
                self.chk(6)
                for h in range(4):
                    tmp, rtmp = tmps[tk % 2]
                    kpe, rkpe = kpes[tk % 2]
                    tk += 1
                    pq1, rpq1 = self.next_ps()
                    for kc in range(2):
                        S.op("pe", lambda: nc.tensor.matmul(pq1[0:96, :], wq[:, kc, h * 192:h * 192 + 96], cqT[:, kc, :], start=(kc == 0), stop=(kc == 1)),
                             reads=[rwq, rcqT[kc]], writes=[rpq1], signal=(kc == 1))
                    pq2, rpq2 = self.next_ps()
                    for kc in range(2):
                        S.op("pe", lambda: nc.tensor.matmul(pq2[0:96, :], wq[:, kc, h * 192 + 96:h * 192 + 192], cqT[:, kc, :], start=(kc == 0), stop=(kc == 1)),
                             reads=[rwq, rcqT[kc]], writes=[rpq2], signal=(kc == 1))
                    S.op("act", lambda: nc.scalar.copy(out=qT[0:64, h, :], in_=pq1[0:64, :]), reads=[rpq1], writes=[rqT[2 * h]])
                    S.op("dve", lambda: nc.vector.tensor_tensor(out=tmp[64:96, :], in0=pq2[64:96, :], in1=rt[64:96, 1, :], op=ALU.mult),
                         reads=[rpq2, rrt], writes=[rtmp])
                    S.op("dve", lambda: nc.vector.tensor_tensor(out=kpe[64:96, :], in0=pq1[64:96, :], in1=rt[64:96, 0, :], op=ALU.mult),
                         reads=[rpq1, rrt], writes=[rkpe])
                    S.op("pool", lambda: nc.gpsimd.tensor_tensor(out=qT[64:96, h, :], in0=kpe[64:96, :], in1=tmp[64:96, :], op=ALU.add),
                         reads=[rkpe, rtmp], writes=[rqT[2 * h + 1]])
                    pk, rpk = self.next_ps()
                    S.op("pe", lambda: nc.tensor.matmul(pk[0:64, :], wkv[:, h * 64:(h + 1) * 64], cqT[:, 2, :], start=True, stop=True),
                         reads=[rwkv, rcqT[2]], writes=[rpk])
                    self.evac(h, kT[0:64, h, :], pk[0:64, :], [rpk], [rkT[2 * h]])
                for sub in range(4):
                    pv, rpv = self.next_ps()
                    S.op("pe", lambda: nc.tensor.matmul(pv[:, 0:256], cqT[:, 2, sub * 128:(sub + 1) * 128], wkv[:, 256:512], start=True, stop=True),
                         reads=[rwkv, rcqT[2]], writes=[rpv])
                    self.evac(0, vs[:, sub, 10:14, 0:64], pv[:, 0:256].rearrange("p (h d) -> p h d", d=64), [rpv], [rvs[sub * 3 + 2]])
                self.chk(7)
                S.dma("sp", self.qTc[:, :, t0:t0 + 512].rearrange("h r t -> r h t"), qT[0:96, :, :], reads=rqT, writes=[self.r_qc[ti]])
                S.dma("sp", self.kTc[:, :, t0:t0 + 512].rearrange("h r t -> r h t"), kT[0:96, :, :], reads=rkT, writes=[self.r_kc[ti]])
                S.dma("sp", self.vsc[t0:t0 + 512, :, :].rearrange("(s p) h e -> p s h e", p=128), vs[:, :, :, :],
                      reads=rvs, writes=[self.r_v[ti]])


    def phase2(self, l):
        nc, S = self.nc, self.S
        SPAN = 2048
        SMAXK = self.smax
        LOOK, EP_DEFER, GN_DEFER = 2, 2, 3
        with contextlib.ExitStack() as st:
            scs = [(st.enter_context(nc.psum_tensor("p2sc%d_%d" % (i, self.uid), [128, 1024], F32)), Res()) for i in range(2)]
            self.alloc_psum(st, 3, 1)
            o_banks = self.ps[0:2]
            tp_bank = self.ps[2]
            Ks = self.sb(st, "p2_K", [128, SMAXK], BF16, 2)
            Vs = self.sb(st, "p2_V", [128, SMAXK // 128, 65], BF16, 2)
            Ms = self.sb(st, "p2_M", [128, MASK_MAX], BF16, 2)
            Qs = self.sb(st, "p2_Q", [96, 512], BF16, 4)
            QLs = self.sb(st, "p2_QL", [128, 512], BF16, 4)
            QHs = self.sb(st, "p2_QH", [128, 512], BF16, 4)
            Ps = self.sb(st, "p2_P", [128, 1024], BF16, 4)
            P2s = self.sb(st, "p2_P2", [128, 1024], BF16, 4)
            osbs = self.sb(st, "p2_osb", [65, 512], F32, 3)
            ogrps = self.sbm(st, "p2_ogrp", [128, SPAN // 128, 256], F32, 2, SPAN // 512)
            (pc, rpc) = self.sb(st, "p2_pc", [128, NPCOL], F32)[0]
            (esink, resink) = self.sb(st, "p2_esink", [128, 4], F32)[0]
            (junk, _) = self.sb(st, "p2_junk", [128, 256], BF16)[0]
            gss = self.sb(st, "p2_gss", [128, 4], F32, 3)
            grs = self.sb(st, "p2_grs", [128, 4], F32, 3)
            recs = self.sb(st, "p2_rec", [128, 4], F32, 3)
            ons = self.sbm(st, "p2_on", [128, 4, 256], BF16, 3, 4)
            onTs = self.sbm(st, "p2_onT", [128, 2, 512], BF16, 3, 2)
            S.dma("sp", pc[:], self.pcol_d[l], writes=[rpc])
            S.op("act", lambda: nc.scalar.activation(out=esink[:], in_=pc[:, PC_SINK:PC_SINK + 4], func=AF.Exp),
                 reads=[rpc], writes=[resink])
            for (qb, rqb) in QLs + QHs:
                S.op("pool", lambda: nc.gpsimd.memset(qb[:], 0.0), writes=[rqb])
            ctr = {"q": 0, "ql": 0, "qh": 0, "p": 0, "sc": 0, "ob": 0, "gn": 0, "ep": 0}

            mixers = [
                ("A", 64, 0.125, ROW_AQ, ROW_AK, 0, 1),
                ("B", 64, 0.125, ROW_BQ, ROW_BK, 4, 2),
                ("C", 96, 96.0 ** -0.5, None, None, 10, 1),
                ("D", 64, 0.125, ROW_DQ, ROW_DK, 6, 1),
            ]
            halo = {"A": (1024, 1024), "B": (128, 128), "D": (256, 256)}

            for si, (s0, sl) in enumerate(zip(self.starts, self.seqs)):
                for sp0 in range(0, sl, SPAN):
                    span = min(SPAN, sl - sp0)
                    nq = span // 512
                    ctxs = []
                    units = []
                    for mi, (mname, dqk, scale, qrow, krow, vbase, grp) in enumerate(mixers):
                        for h in range(4):
                            if mname == "C":
                                wlo, whi = 0, sl
                            else:
                                wlo, whi = max(sp0 - halo[mname][0], 0), min(sp0 + span + halo[mname][1], sl)
                            c = dict(mi=mi, mname=mname, dqk=dqk, scale=scale, qrow=qrow, krow=krow, vbase=vbase, h=h, kvh=h // grp,
                                     wlo=wlo, whi=whi, kt_base=wlo // 128, start=len(units))
                            ci = len(ctxs)
                            ctxs.append(c)
                            for qi in range(nq):
                                q0 = sp0 + qi * 512
                                if mname == "A":
                                    lo, hi = q0 - 1024, q0 + 512 + 1024
                                elif mname == "B":
                                    lo, hi = q0 - 128, q0 + 512 + 128
                                elif mname == "C":
                                    lo, hi = 0, sl
                                else:
                                    lo, hi = q0 - 256, q0 + 768
                                lo, hi = max(lo, 0), min(hi, sl)
                                kts = list(range(lo // 128, hi // 128))
                                for a in range(0, len(kts), 2):
                                    units.append((ci, qi, q0, kts[a:a + 2], a == 0, a + 2 >= len(kts)))
                    state = {}
                    dq = []

                    def load_ctx(ci):
                        c = ctxs[ci]
                        K, rK = Ks[ci % 2]
                        V, rV = Vs[ci % 2]
                        wlo, whi = c["wlo"], c["whi"]
                        wlen = whi - wlo
                        tiles_rng = range((s0 + wlo) // 512, (s0 + whi + 511) // 512)
                        if c["krow"] is not None:
                            r0 = c["krow"] + (c["kvh"] // 2) * 128
                            ksrc = self.qkT[r0:r0 + 128, s0 + wlo:s0 + whi]
                            rdep = [self.r_qk[t] for t in tiles_rng]
                            S.dma("sp", K[0:128, 0:wlen], ksrc, reads=rdep, writes=[rK])
                        else:
                            ksrc = self.kTc[c["h"], :, s0 + wlo:s0 + whi]
                            rdep = [self.r_kc[t] for t in tiles_rng]
                            S.dma("sp", K[0:c["dqk"], 0:wlen], ksrc, reads=rdep, writes=[rK])
                        S.dma("sp", V[:, 0:wlen // 128, :],
                              self.vsc[s0 + wlo:s0 + whi, c["vbase"] + c["kvh"], :].rearrange("(k p) e -> p k e", p=128),
                              reads=[self.r_v[t] for t in tiles_rng], writes=[rV])
                        M = rM = None
                        if c["mname"] != "C":
                            M, rM = Ms[ci % 2]
                            if c["mname"] == "A":
                                S.dma("sp", M[:, 0:MA_LEN], self.maska_b[c["h"]], writes=[rM])
                            elif c["mname"] == "B":
                                S.dma("sp", M[:, 0:MB_LEN], self.maskb_b[c["h"]], writes=[rM])
                            else:
                                S.dma("sp", M[:, 0:MD_LEN], self.maskd_b[c["h"]], writes=[rM])
                        c.update(K=K, rK=rK, V=V, rV=rV, M=M, rM=rM)

                    def mask_ap(c, q0, kt):
                        dlt = kt * 128 - q0
                        if c["mname"] == "A":
                            off = 1408 - dlt
                        elif c["mname"] == "B":
                            off = 512 - dlt
                        else:
                            rows = sl // 64
                            R0 = q0 // 64
                            if R0 == 0:
                                off = MD_ARR + (dlt // 128) * 512
                            elif R0 == rows - 8:
                                off = MD_ARR + (6 + (dlt + 256) // 128) * 512
                            else:
                                off = 640 - dlt
                        return c["M"][:, off:off + 512]

                    def emit_S(i):
                        ci, qi, q0, kts, first, last = units[i]
                        c = ctxs[ci]
                        nk = len(kts)
                        dqk, K, rK, kb = c["dqk"], c["K"], c["rK"], c["kt_base"]
                        if c["qrow"] is not None:
                            dqk = 128
                        if first:
                            t0 = s0 + q0
                            if c["qrow"] is not None:
                                half = c["kvh"] % 2
                                if half == 0:
                                    Q, rQ = QLs[ctr["ql"] % 4]
                                    ctr["ql"] += 1
                                else:
                                    Q, rQ = QHs[ctr["qh"] % 4]
                                    ctr["qh"] += 1
                                r0 = c["qrow"] + c["h"] * 64
                                qsrc = self.qkT[r0:r0 + 64, t0:t0 + 512]
                                rd = [self.r_qk[t0 // 512]]
                                S.dma("sp", Q[half * 64:half * 64 + 64, :], qsrc, reads=rd, writes=[rQ])
                            else:
                                Q, rQ = Qs[ctr["q"] % 4]
                                ctr["q"] += 1
                                qsrc = self.qTc[c["h"], :, t0:t0 + 512]
                                rd = [self.r_qc[t0 // 512]]
                                S.dma("sp", Q[0:dqk, :], qsrc, reads=rd, writes=[rQ])
                            state[("q", ci, qi)] = (Q, rQ)
                        Q, rQ = state[("q", ci, qi)]
                        sc, rsc = scs[ctr["sc"] % 2]
                        ctr["sc"] += 1
                        for a, kt in enumerate(kts):
                            S.op("pe", lambda: nc.tensor.matmul(sc[:, a * 512:(a + 1) * 512], K[0:dqk, (kt - kb) * 128:(kt - kb + 1) * 128], Q[0:dqk, :],
                                                                start=True, stop=True),
                                 reads=[rK, rQ], writes=[rsc], signal=(a == nk - 1))
                        P, rP = Ps[ctr["p"] % 4]
                        S.op("act", lambda: nc.scalar.activation(out=P[:, 0:nk * 512], in_=sc[:, 0:nk * 512], func=AF.Exp, scale=c["scale"]),
                             reads=[rsc], writes=[rP])
                        if c["M"] is not None:
                            P2, rP2 = P2s[ctr["p"] % 4]
                            for a, kt in enumerate(kts):
                                mk = mask_ap(c, q0, kt)
                                S.op("dve", lambda: nc.vector.tensor_tensor(out=P2[:, a * 512:(a + 1) * 512], in0=P[:, a * 512:(a + 1) * 512],
                                                                            in1=mk, op=ALU.mult),
                                     reads=[rP, c["rM"]], writes=[rP2])
                            state[("p", i)] = (P2, rP2)
                        else:
                            state[("p", i)] = (P, rP)
                        ctr["p"] += 1

                    def emit_PV(j):
                        ci, qi, q0, kts, first, last = units[j]
                        c = ctxs[ci]
                        nk = len(kts)
                        V, rV, kb = c["V"], c["rV"], c["kt_base"]
                        P, rP = state.pop(("p", j))
                        if first:
                            ob, rob = o_banks[ctr["ob"] % 2]
                            ctr["ob"] += 1
                            state[("o", ci, qi)] = (ob, rob)
                        ob, rob = state[("o", ci, qi)]
                        for a, kt in enumerate(kts):
                            S.op("pe", lambda: nc.tensor.matmul(ob[0:65, :], V[:, kt - kb, :], P[:, a * 512:(a + 1) * 512],
                                                                start=(first and a == 0), stop=(last and a == nk - 1)),
                                 reads=[rV, rP], writes=[rob], signal=(a == nk - 1))
                        if last:
                            e = ctr["ep"] % 3
                            ctr["ep"] += 1
                            osb, rosb = osbs[e]
                            S.op("dve", lambda: nc.vector.tensor_copy(out=osb[:, :], in_=ob[0:65, :]), reads=[rob], writes=[rosb])
                            dq.append([EP_DEFER, lambda: emit_epilogue(ci, qi, e)])

                    def emit_epilogue(ci, qi, e):
                        c = ctxs[ci]
                        h, mi = c["h"], c["mi"]
                        osb, rosb = osbs[e]
                        rec, rrec = recs[e]
                        ogrp, rogrp = ogrps[mi % 2]
                        tp, rtp = tp_bank
                        for sub in range(4):
                            S.op("pe", lambda: nc.tensor.transpose(out=tp[:, sub * 65:(sub + 1) * 65], in_=osb[0:65, sub * 128:(sub + 1) * 128],
                                                                   identity=self.identf[0:65, 0:65]),
                                 reads=[rosb, self.r_identf], writes=[rtp], signal=(sub == 3))
                        den = tp[:, 0:260].rearrange("p (s e) -> p s e", e=65)[:, :, 64]
                        if c["mname"] == "B":
                            S.op("dve", lambda: nc.vector.tensor_scalar(out=rec[:, 0:4], in0=den, scalar1=esink[:, h:h + 1], scalar2=None, op0=ALU.add),
                                 reads=[rtp, resink], writes=[rrec])
                            S.op("dve", lambda: nc.vector.reciprocal(out=rec[:, 0:4], in_=rec[:, 0:4]), reads=[rrec], writes=[rrec])
                        else:
                            S.op("dve", lambda: nc.vector.reciprocal(out=rec[:, 0:4], in_=den), reads=[rtp], writes=[rrec])
                        for sub in range(4):
                            S.op("dve", lambda: nc.vector.tensor_scalar(out=ogrp[:, qi * 4 + sub, h * 64:(h + 1) * 64],
                                                                        in0=tp[:, sub * 65:sub * 65 + 64], scalar1=rec[:, sub:sub + 1],
                                                                        scalar2=None, op0=ALU.mult),
                                 reads=[rtp, rrec], writes=[rogrp[qi]])
                        if h == 3:
                            emit_gn_a(mi, qi)

                    def emit_gn_a(mi, qi):
                        g = ctr["gn"] % 3
                        ctr["gn"] += 1
                        ogrp, rogrp = ogrps[mi % 2]
                        gs, rgs = gss[g]
                        gr, rgr = grs[g]
                        on, ron = ons[g]
                        for sub in range(4):
                            S.op("act", lambda: nc.scalar.activation(out=junk[:, :], in_=ogrp[:, qi * 4 + sub, :], func=AF.Square, scale=1.0 / 16.0,
                                                                     accum_out=gs[:, sub:sub + 1]),
                                 reads=[rogrp[qi]], writes=[rgs])
                        self.rstd_from_acc(gs, rgs, gr, rgr, 0, 4)
                        for sub in range(4):
                            S.op("dve", lambda: nc.vector.tensor_scalar(out=on[:, sub, :], in0=ogrp[:, qi * 4 + sub, :], scalar1=gr[:, sub:sub + 1],
                                                                        scalar2=None, op0=ALU.mult),
                                 reads=[rogrp[qi], rgr], writes=[ron[sub]])
                        dq.append([GN_DEFER, lambda: emit_gn_b(mi, qi, g)])

                    def emit_gn_b(mi, qi, g):
                        t0 = s0 + sp0 + qi * 512
                        on, ron = ons[g]
                        onT, ronT = onTs[g]
                        for cc in range(2):
                            pb, rpb = self.next_pb()
                            for sub in range(4):
                                S.op("pe", lambda: nc.tensor.transpose(out=pb[:, sub * 128:(sub + 1) * 128], in_=on[:, sub, cc * 128:(cc + 1) * 128],
                                                                       identity=self.identb[:]),
                                     reads=[ron[sub], self.r_identb], writes=[rpb], signal=(sub == 3))
                            gc = PC_GN + mi * 2 + cc
                            S.op("dve", lambda: nc.vector.tensor_scalar(out=onT[:, cc, :], in0=pb[:, 0:512], scalar1=pc[:, gc:gc + 1], scalar2=None, op0=ALU.mult),
                                 reads=[rpb, rpc], writes=[ronT[cc]])
                        S.dma("pool", self.onT[mi * 256:(mi + 1) * 256, t0:t0 + 512].rearrange("(c p) t -> p c t", p=128), onT[:, :, :],
                              reads=ronT, writes=[self.r_on[t0 // 512]])

                    def tick():
                        ready = [f for (cnt, f) in dq if cnt <= 0]
                        rest = [[cnt - 1, f] for (cnt, f) in dq if cnt > 0]
                        dq[:] = rest
                        for f in ready:
                            f()

                    n = len(units)
                    load_ctx(0)
                    for i in range(n + LOOK):
                        if i < n:
                            ci = units[i][0]
                            if i == ctxs[ci]["start"] + LOOK and ci + 1 < len(ctxs):
                                load_ctx(ci + 1)
                            emit_S(i)
                        tick()
                        if i - LOOK >= 0:
                            emit_PV(i - LOOK)
                    while dq:
                        tick()
            S.barrier()

    def phase3(self, l):
        nc, S = self.nc, self.S
        last = (l == self.L - 1)
        x_src = self.x_in if l == 0 else self.xres
        with contextlib.ExitStack() as st:
            self.alloc_psum(st, 6, 2)
            (wo, rwo) = self.sb(st, "p3_wo", [128, 8, 1024], BF16)[0]
            (wd, rwd) = self.sb(st, "p3_wd", [128, 32, 1024], BF16)[0]
            wus = self.sb(st, "p3_wu", [128, 8, 512], BF16, 2)
            (pc, rpc) = self.sb(st, "p3_pc", [128, NPCOL], F32)[0]
            (nf, rnf) = self.sb(st, "p3_nf", [128, 1024], F32)[0]
            xts = self.sbm(st, "p3_xt", [128, 4, 1024], F32, 2, 4)
            (xn, rxn) = self.sbm(st, "p3_xn", [128, 4, 1024], BF16, 1, 4)[0]
            onTs = self.sb(st, "p3_onT", [128, 8, 512], BF16, 2)
            (hT, rhT) = self.sbm(st, "p3_hT", [128, 8, 512], BF16, 1, 8)[0]
            (uT, ruT) = self.sbm(st, "p3_uT", [128, 32, 512], BF16, 1, 32)[0]
            rls = self.sb(st, "p3_rl", [128, 512], BF16, 3)
            junk = None
            sss = self.sb(st, "p3_ss", [128, 8], F32, 2)
            rss_ = self.sb(st, "p3_rs", [128, 8], F32, 2)
            S.dma("sp", wo[:], self.wout_b[l], writes=[rwo])
            S.dma("sp", pc[:], self.pcol_d[l], writes=[rpc])
            S.dma("sp", nf[:], self.nfin_d[:, :], writes=[rnf])
            S.dma("sp", wd[:], self.wdown_b[l], writes=[rwd])
            tl = self.tiles()
            wu_ctr = [0]

            def load(i):
                si, s0, t0 = tl[i]
                xt, rxt = xts[i % 2]
                oT, roT = onTs[i % 2]
                S.dma("sp", xt[:], x_src[t0:t0 + 512, :].rearrange("(s p) d -> p s d", p=128),
                      reads=[self.r_x[t0 // 512]], writes=rxt)
                S.dma("sp", oT[:], self.onT[:, t0:t0 + 512].rearrange("(c p) t -> p c t", p=128),
                      reads=[self.r_on[t0 // 512]], writes=[roT])

            def load_wu(g):
                wu, rwu = wus[wu_ctr[0] % 2]
                wu_ctr[0] += 1
                S.dma("sp", wu[:], self.wup_b[l, g], writes=[rwu])
                return wu, rwu

            def outproj_front(i):
                xt, rxt = xts[i % 2]
                oT, roT = onTs[i % 2]
                ss, rss = sss[i % 2]
                rs, rrs = rss_[i % 2]
                for sub in range(4):
                    for half in range(2):
                        ps, rps = self.next_ps()
                        for c in range(8):
                            S.op("pe", lambda: nc.tensor.matmul(ps[:, :], oT[:, c, sub * 128:(sub + 1) * 128], wo[:, c, half * 512:(half + 1) * 512],
                                                                start=(c == 0), stop=(c == 7)),
                                 reads=[roT, rwo], writes=[rps], signal=(c == 7))
                        S.op("dve", lambda: nc.vector.tensor_tensor(out=xt[:, sub, half * 512:(half + 1) * 512], in0=ps[:, :],
                                                                    in1=xt[:, sub, half * 512:(half + 1) * 512], op=ALU.add),
                             reads=[rps, rxt[sub]], writes=[rxt[sub]])
                self.norm_front(xt, rxt, xn, rxn, ss, rss, rs, rrs)

            load(0)
            for i, (si, s0, t0) in enumerate(tl):
                ti = t0 // 512
                xt, rxt = xts[i % 2]
                ss, rss = sss[i % 2]
                rs, rrs = rss_[i % 2]
                wu_next = load_wu(0)
                outproj_front(i)
                if i + 1 < len(tl):
                    load(i + 1)
                self.norm_tpose(xn, rxn, hT, rhT, pc[:, PC_NMLP:PC_NMLP + 8], rpc)
                for g in range(8):
                    wu, rwu = wu_next
                    if g + 1 < 8:
                        wu_next = load_wu(g + 1)
                    for f in range(4):
                        fc = g * 4 + f
                        ps, rps = self.next_ps()
                        for c in range(8):
                            S.op("pe", lambda: nc.tensor.matmul(ps[:, :], wu[:, c, f * 128:(f + 1) * 128], hT[:, c, :],
                                                                start=(c == 0), stop=(c == 7)),
                                 reads=[rwu, rhT[c]], writes=[rps], signal=(c == 7))
                        rl, rrl = rls[fc % 3]
                        S.op("act", lambda: nc.scalar.activation(out=rl[:, :], in_=ps[:, :], func=AF.Relu), reads=[rps], writes=[rrl])
                        S.op("dve", lambda: nc.vector.tensor_tensor(out=uT[:, fc, :], in0=rl[:, :], in1=rl[:, :], op=ALU.mult),
                             reads=[rrl], writes=[ruT[fc]])
                for sub in range(4):
                    for half in range(2):
                        ps, rps = self.next_ps()
                        for fc in range(32):
                            S.op("pe", lambda: nc.tensor.matmul(ps[:, :], uT[:, fc, sub * 128:(sub + 1) * 128], wd[:, fc, half * 512:(half + 1) * 512],
                                                                start=(fc == 0), stop=(fc == 31)),
                                 reads=[ruT[fc], rwd], writes=[rps], signal=(fc == 31))
                        S.op("dve", lambda: nc.vector.tensor_tensor(out=xt[:, sub, half * 512:(half + 1) * 512], in0=ps[:, :],
                                                                    in1=xt[:, sub, half * 512:(half + 1) * 512], op=ALU.add),
                             reads=[rps, rxt[sub]], writes=[rxt[sub]])
                if not last:
                    S.dma("sp", self.xres[t0:t0 + 512, :].rearrange("(s p) d -> p s d", p=128), xt[:],
                          reads=rxt, writes=[self.r_x[ti]])
                else:
                    for sub in range(4):
                        S.op("act", lambda: nc.scalar.activation(out=uT[:, 2 * sub:2 * sub + 2, :], in_=xt[:, sub, :].rearrange("p (a b) -> p a b", b=512),
                                                                 func=AF.Square, scale=1.0 / 32.0, accum_out=ss[:, 4 + sub:5 + sub]),
                             reads=[rxt[sub]], writes=[rss, ruT[2 * sub], ruT[2 * sub + 1]])
                    self.rstd_from_acc(ss, rss, rs, rrs, 4, 4)
                    for sub in range(4):
                        S.op("dve", lambda: nc.vector.scalar_tensor_tensor(out=xt[:, sub, :], in0=xt[:, sub, :], scalar=rs[:, 4 + sub:5 + sub],
                                                                           in1=nf[:, :], op0=ALU.mult, op1=ALU.mult),
                             reads=[rxt[sub], rrs, rnf], writes=[rxt[sub]])
                    S.dma("sp", self.y_out[t0:t0 + 512, :].rearrange("(s p) d -> p s d", p=128), xt[:], reads=rxt)
            S.barrier()


_PROG_CACHE = {}


def get_prog(seqs, depth):
    key = (tuple(seqs), depth)
    if key not in _PROG_CACHE:
        p = Prog(seqs, depth)
        p.build()
        _PROG_CACHE[key] = p
    return _PROG_CACHE[key]


def run_cores(x_list, shared, seqs, depth, core_ids):
    prog = get_prog(seqs, depth)
    shared = dict(shared)
    shared["rope"] = _rope_tables(max(seqs))
    in_maps = []
    for x in x_list:
        m = dict(shared)
        m["x"] = np.ascontiguousarray(x, dtype=np.float32)
        in_maps.append(m)
    res = run_bass_kernel_spmd(prog.nc, in_maps, core_ids=core_ids)
    return [r["y"] for r in res.results]


def kernel(**inputs):
    xp = np.asarray(inputs["x_prompt"], np.float32)
    xs = np.asarray(inputs["x_sample"], np.float32)
    shared = host_prep(inputs, DEPTH)
    seqs = [2048, 2048, 16384]
    x_list = []
    zeros = np.zeros_like(xs[0])
    for c in range(N_CORES):
        samp = xs[c // 4] if c % 4 == 0 else zeros
        x_list.append(np.concatenate([xp[2 * c], xp[2 * c + 1], samp], axis=0))
    ys = run_cores(x_list, shared, seqs, DEPTH, list(range(N_CORES)))
    y_prompt = np.empty_like(xp)
    y_sample = np.empty_like(xs)
    for c in range(N_CORES):
        y_prompt[2 * c] = ys[c][0:2048]
        y_prompt[2 * c + 1] = ys[c][2048:4096]
    y_sample[0] = ys[0][4096:]
    y_sample[1] = ys[4][4096:]
    return (y_prompt, y_sample)
```

```python
import contextlib
import numpy as np
import concourse.bass as bass
import concourse.mybir as mybir
from concourse.bass_utils import run_bass_kernel_spmd

F32 = mybir.dt.float32
BF16 = mybir.dt.bfloat16
AF = mybir.ActivationFunctionType
ALU = mybir.AluOpType

D_MODEL = 1024
DEPTH = 4
D_FF = 4096
EPS = 1e-5
N_CORES = 8
FM_COLS = 1408
TM0 = 1408
KR0 = 2432
WIN_COLS = 2624
NPCOL = 40
PC_NATT, PC_NMLP, PC_GN, PC_QN, PC_KVN, PC_SINK = 0, 8, 16, 24, 26, 27
MA_LEN, MB_LEN, MD_ARR = 2944, 1152, 1408
MD_LEN = MD_ARR + 12 * 512
MASK_MAX = MD_LEN
ROW_AQ, ROW_AK, ROW_BQ, ROW_BK, ROW_DQ, ROW_DK = 0, 256, 512, 768, 896, 1152
NVH = 14
import os as _os
DUMMY_COLS = int(_os.environ.get("KDUM", "0"))
POOL_MASK = int(_os.environ.get("KPOOL", "0"))


class _Stop(Exception):
    pass


class Res:
    __slots__ = ("w", "r")

    def __init__(self):
        self.w = {}
        self.r = {}


class Sched:
    def __init__(self, nc, stack, n_dma_sems=12):
        self.nc = nc
        self.eng = {"pe": nc.tensor, "act": nc.scalar, "dve": nc.vector, "pool": nc.gpsimd, "sp": nc.sync}
        self.names = list(self.eng.keys())
        self.sems = {}
        for n in self.names:
            self.sems[n] = stack.enter_context(nc.semaphore("s_" + n))
        self.cnt = {n: 0 for n in self.names}
        self.dq = {}
        for q in ("sp", "pool"):
            lst = []
            for i in range(n_dma_sems):
                key = "d_%s_%d" % (q, i)
                self.sems[key] = stack.enter_context(nc.semaphore(key))
                self.cnt[key] = 0
                lst.append(key)
            self.dq[q] = [lst, 0]
        self.waited = {n: {} for n in self.names}
        self.pending = {n: [] for n in self.names}
        self.nops = 0

    def _needs(self, X, reads, writes):
        need = {}
        for r in reads:
            for k, v in r.w.items():
                if need.get(k, 0) < v:
                    need[k] = v
        for w in writes:
            for k, v in w.w.items():
                if need.get(k, 0) < v:
                    need[k] = v
            for k, v in w.r.items():
                if need.get(k, 0) < v:
                    need[k] = v
        return need

    def _emit_waits(self, X, need):
        wd = self.waited[X]
        e = self.eng[X]
        for k, v in need.items():
            if k == X and X == "pe":
                continue
            if wd.get(k, 0) < v:
                e.wait_ge(self.sems[k], v)
                wd[k] = v

    def _commit(self, ev_key, ev_val, reads, writes):
        for r in reads:
            if r.r.get(ev_key, 0) < ev_val:
                r.r[ev_key] = ev_val
        for w in writes:
            w.w = {ev_key: ev_val}
            w.r = {}

    def op(self, X, fn, reads=(), writes=(), signal=True):
        need = self._needs(X, reads, writes)
        self._emit_waits(X, need)
        inst = fn()
        self.nops += 1
        if signal:
            self.cnt[X] += 1
            inst.then_inc(self.sems[X], 1)
            c = self.cnt[X]
            for (rr, ww) in self.pending[X]:
                self._commit(X, c, rr, ww)
            self.pending[X] = []
            self._commit(X, c, reads, writes)
        else:
            self.pending[X].append((reads, writes))
        return inst

    def dma(self, q, out, in_, reads=(), writes=(), **kw):
        assert not self.pending[q]
        need = self._needs(q, reads, writes)
        lst, idx = self.dq[q]
        key = lst[idx % len(lst)]
        self.dq[q][1] = idx + 1
        if self.cnt[key] > 0:
            need[key] = max(need.get(key, 0), self.cnt[key])
        self._emit_waits(q, need)
        inst = self.eng[q].dma_start(out=out, in_=in_, **kw)
        self.nops += 1
        self.cnt[key] += 16
        inst.then_inc(self.sems[key], 16)
        self._commit(key, self.cnt[key], reads, writes)
        return inst

    def barrier(self):
        for n in self.names:
            assert not self.pending[n], n
        allev = {k: v for k, v in self.cnt.items() if v > 0}
        for n in self.names:
            need = {k: v for k, v in allev.items() if k != n}
            wd = self.waited[n]
            for k, v in need.items():
                if wd.get(k, 0) < v:
                    self.eng[n].wait_ge(self.sems[k], v)
                    wd[k] = v

    def final_wait(self):
        self.barrier()


def _alibi_slopes():
    s = np.exp2(-8.0 * np.arange(1, 9, dtype=np.float64) / 8.0)
    return s[1::2], s[0::2]


def _mask_a():
    sa, _ = _alibi_slopes()
    d = np.arange(-4096, 4097)
    ad = np.abs(d)
    m = ((ad <= 64).astype(np.float64) + ((d % 4 == 0) & (ad <= 256)) + ((d % 16 == 0) & (ad <= 1024)))
    out = np.zeros((4, 128, MA_LEN), np.float32)
    i = np.arange(128)[:, None]
    n = np.arange(MA_LEN)[None, :]
    dd = i + 1408 - n
    for h in range(4):
        f = m * np.exp(-sa[h] * ad)
        out[h] = f[dd + 4096]
    return out


def _mask_b():
    _, sb = _alibi_slopes()
    out = np.zeros((4, 128, MB_LEN), np.float32)
    i = np.arange(128)[:, None]
    n = np.arange(MB_LEN)[None, :]
    dd = i + 512 - n
    ad = np.abs(dd)
    for h in range(4):
        out[h] = np.where(ad <= 128, np.exp(-sb[h] * ad), 0.0)
    return out


def _d_tables():
    i = np.arange(128)
    rk_l = (i // 64)[:, None, None]
    ck = (i % 64)[:, None, None]
    n = np.arange(22)[None, :, None]
    cq = np.arange(64)[None, None, :]
    delta = rk_l + 10 - n + 0 * cq
    dr_idx = np.clip(delta, -7, 7) + 7
    dc_idx = np.clip(ck - cq, -15, 15) + 15 + 0 * n
    cs = np.clip(cq - 8, 0, 64 - 16)
    colmask = ((ck >= cs) & (ck < cs + 16)) & (n >= 0)
    cw = (colmask & (delta >= -4) & (delta <= 3)).astype(np.float32).reshape(128, MD_ARR)
    cf = (colmask & (delta >= -7) & (delta <= 7)).astype(np.float32).reshape(128, MD_ARR)
    vb = np.zeros((12, 128, 512), np.float32)
    rkl = (i // 64)[:, None]
    rql = (np.arange(512) // 64)[None, :]
    for kt in range(6):
        rk = 2 * kt + rkl
        rs = np.maximum(rql - 4, 0)
        vb[kt] = ((rk >= rs) & (rk < rs + 8)).astype(np.float32)
        rk2 = -4 + 2 * kt + rkl
        rs2 = np.minimum(rql - 4, 0)
        vb[6 + kt] = ((rk2 >= rs2) & (rk2 < rs2 + 8)).astype(np.float32)
    return dr_idx.reshape(128, MD_ARR), dc_idx.reshape(128, MD_ARR), cw, cf, vb


def _rope_tables(smax):
    inv_freq = (1.0 / (np.float32(10000.0) ** (np.arange(0, 32, 2, dtype=np.float32) / np.float32(32)))).astype(np.float32)
    ang = (np.arange(smax, dtype=np.float32)[:, None] * inv_freq[None, :]).astype(np.float32)
    c = np.cos(ang.astype(np.float64)).astype(np.float32).T
    s = np.sin(ang.astype(np.float64)).astype(np.float32).T
    out = np.zeros((2, 32, smax), np.float32)
    out[0, :16] = c
    out[0, 16:] = c
    out[1, :16] = -s
    out[1, 16:] = s
    return out


def host_prep(inp, depth):
    L = depth
    w_in = np.asarray(inp["w_in"], np.float32)[:L]
    sec = np.cumsum([0, 256, 256, 256, 256, 128, 128, 256, 128, 32, 256, 256, 256])
    a_q, a_k, a_v, b_q, b_k, b_v, c_q, c_kv, c_kr, d_q, d_k, d_v = [w_in[:, :, sec[i]:sec[i + 1]] for i in range(12)]
    z64 = np.zeros((L, 1024, 64), np.float32)
    c_kr_sw = np.concatenate([c_kr[:, :, 16:32], c_kr[:, :, 0:16]], axis=-1)
    w_in_p = np.concatenate([a_q, a_k, b_q, b_k, d_q, d_k, a_v, b_v, d_v, c_q, c_kv, z64, c_kr, z64, c_kr_sw], axis=-1)
    assert w_in_p.shape[-1] == WIN_COLS
    wq = np.asarray(inp["w_q_up"], np.float32)[:L].reshape(L, 256, 4, 96)
    z = np.zeros((L, 256, 4, 64), np.float32)
    pe = wq[..., 64:96]
    pe_sw = np.concatenate([pe[..., 16:32], pe[..., 0:16]], axis=-1)
    wq_p = np.concatenate([wq, z, pe_sw], axis=-1).reshape(L, 256, 768)
    wkv = np.asarray(inp["w_kv_up"], np.float32)[:L].reshape(L, 128, 4, 128)
    wkv_p = np.concatenate([wkv[..., :64].reshape(L, 128, 256), wkv[..., 64:].reshape(L, 128, 256)], axis=-1)
    pcol = np.zeros((L, 128, NPCOL), np.float32)
    pcol[:, :, PC_NATT:PC_NATT + 8] = np.asarray(inp["norm_attn"], np.float32)[:L].reshape(L, 8, 128).transpose(0, 2, 1)
    pcol[:, :, PC_NMLP:PC_NMLP + 8] = np.asarray(inp["norm_mlp"], np.float32)[:L].reshape(L, 8, 128).transpose(0, 2, 1)
    pcol[:, :, PC_GN:PC_GN + 8] = np.asarray(inp["group_norm"], np.float32)[:L].reshape(L, 8, 128).transpose(0, 2, 1)
    pcol[:, :, PC_QN:PC_QN + 2] = np.asarray(inp["mla_q_norm"], np.float32)[:L].reshape(L, 2, 128).transpose(0, 2, 1)
    pcol[:, :, PC_KVN] = np.asarray(inp["mla_kv_norm"], np.float32)[:L]
    pcol[:, :, PC_SINK:PC_SINK + 4] = np.asarray(inp["sink_logits"], np.float32)[:L][:, None, :]
    nfin = np.broadcast_to(np.asarray(inp["norm_final"], np.float32)[None, :], (128, 1024)).copy()
    dr_idx, dc_idx, cw, cf, vb = _d_tables()
    rpb = np.asarray(inp["na_rpb"], np.float32)[:L]
    biasf = rpb[:, :, dr_idx, dc_idx]
    return {
        "w_in_p": np.ascontiguousarray(w_in_p),
        "w_q_p": np.ascontiguousarray(wq_p),
        "w_kv_p": np.ascontiguousarray(wkv_p),
        "w_out": np.ascontiguousarray(np.asarray(inp["w_out"], np.float32)[:L]),
        "w_up": np.ascontiguousarray(np.asarray(inp["w_mlp_up"], np.float32)[:L]),
        "w_down": np.ascontiguousarray(np.asarray(inp["w_mlp_down"], np.float32)[:L]),
        "pcol": pcol,
        "nfin": nfin,
        "ident": np.eye(128, dtype=np.float32),
        "mask_a": _mask_a(),
        "mask_b": _mask_b(),
        "biasf": np.ascontiguousarray(biasf),
        "dconst": np.ascontiguousarray(np.concatenate([cw, cf, vb.transpose(1, 0, 2).reshape(128, 12 * 512)], axis=1)),
    }


class Prog:
    def __init__(self, seqs, depth):
        self.seqs = list(seqs)
        self.L = depth
        self.T = sum(seqs)
        self.smax = max(seqs)
        self.starts = [int(x) for x in np.cumsum([0] + self.seqs[:-1])]

    def sb(self, st, name, shape, dt, n=1):
        out = []
        for i in range(n):
            t = st.enter_context(self.nc.sbuf_tensor("%s_%d_%d" % (name, self.uid, i), shape, dt))
            out.append((t, Res()))
        self.uid += 1
        return out

    def build(self):
        nc = bass.Bass("TRN2", target_bir_lowering=False)
        self.nc = nc
        self.uid = 0
        L, T = self.L, self.T

        def din(name, shape):
            return nc.dram_tensor(name, shape, F32, kind="ExternalInput").ap()

        def dscr(name, shape, dt):
            return nc.dram_tensor(name, shape, dt, kind="Internal").ap()

        self.x_in = din("x", [T, 1024])
        self.w_in_p = din("w_in_p", [L, 1024, WIN_COLS])
        self.w_q_p = din("w_q_p", [L, 256, 768])
        self.w_kv_p = din("w_kv_p", [L, 128, 512])
        self.w_out = din("w_out", [L, 1024, 1024])
        self.w_up = din("w_up", [L, 1024, 4096])
        self.w_down = din("w_down", [L, 4096, 1024])
        self.pcol_d = din("pcol", [L, 128, NPCOL])
        self.nfin_d = din("nfin", [128, 1024])
        self.ident_d = din("ident", [128, 128])
        self.mask_a_d = din("mask_a", [4, 128, MA_LEN])
        self.mask_b_d = din("mask_b", [4, 128, MB_LEN])
        self.biasf_d = din("biasf", [L, 4, 128, MD_ARR])
        self.dconst_d = din("dconst", [128, 2 * MD_ARR + 12 * 512])
        self.rope_d = din("rope", [2, 32, self.smax])
        self.y_out = nc.dram_tensor("y", [T, 1024], F32, kind="ExternalOutput").ap()

        self.xres = dscr("xres", [T, 1024], F32)
        self.win_b = dscr("win_b", [L, 128, 8, WIN_COLS], BF16)
        self.wq_b = dscr("wq_b", [L, 128, 2, 768], BF16)
        self.wkv_b = dscr("wkv_b", [L, 128, 512], BF16)
        self.wout_b = dscr("wout_b", [L, 128, 8, 1024], BF16)
        self.wup_b = dscr("wup_b", [L, 8, 128, 8, 512], BF16)
        self.wdown_b = dscr("wdown_b", [L, 128, 32, 1024], BF16)
        self.maska_b = dscr("maska_b", [4, 128, MA_LEN], BF16)
        self.maskb_b = dscr("maskb_b", [4, 128, MB_LEN], BF16)
        self.maskd_b = dscr("maskd_b", [4, 128, MD_LEN], BF16)
        self.qkT = dscr("qkT", [FM_COLS, T], BF16)
        self.qTc = dscr("qTc", [4, 96, T], BF16)
        self.kTc = dscr("kTc", [4, 96, T], BF16)
        self.vsc = dscr("vsc", [T, NVH, 65], BF16)
        self.onT = dscr("onT", [1024, T], BF16)
        nt = T // 512
        self.r_x = [Res() for _ in range(nt)]
        self.r_qk = [Res() for _ in range(nt)]
        self.r_on = [Res() for _ in range(nt)]
        self.r_qc = [Res() for _ in range(nt)]
        self.r_kc = [Res() for _ in range(nt)]
        self.r_v = [Res() for _ in range(nt)]
        self.r_w = Res()
        self.r_maskd = Res()

        with contextlib.ExitStack() as top:
            self.S = Sched(nc, top)
            cst = self.sb(top, "identf", [128, 128], F32)[0]
            self.identf, self.r_identf = cst
            cst = self.sb(top, "identb", [128, 128], BF16)[0]
            self.identb, self.r_identb = cst
            cst = self.sb(top, "neghalf", [128, 8], F32)[0]
            self.neghalf, self.r_neghalf = cst
            S = self.S
            S.dma("sp", self.identf[:], self.ident_d[:, :], writes=[self.r_identf])
            S.op("dve", lambda: nc.vector.tensor_copy(out=self.identb[:], in_=self.identf[:]),
                 reads=[self.r_identf], writes=[self.r_identb])
            S.op("dve", lambda: nc.vector.memset(self.neghalf[:], -0.5), writes=[self.r_neghalf])
            import os
            kstop = int(os.environ.get("KSTOP", "99"))
            if kstop >= 1:
                self.prologue()
            for l in range(L):
                if kstop >= 2:
                    self.phase_dmask(l)
                if kstop >= 3:
                    self.phase1(l)
                if kstop >= 4:
                    self.phase2(l)
                if kstop >= 5:
                    self.phase3(l)
            S.final_wait()
        return nc

    def alloc_psum(self, st, nps, npb):
        nc = self.nc
        self.ps = [(st.enter_context(nc.psum_tensor("ps%d_%d" % (i, self.uid), [128, 512], F32)), Res()) for i in range(nps)]
        self.pb = [(st.enter_context(nc.psum_tensor("pb%d_%d" % (i, self.uid), [128, 1024], BF16)), Res()) for i in range(npb)]
        self.uid += 1
        self.ps_i = 0
        self.pb_i = 0

    def next_ps(self):
        p = self.ps[self.ps_i % len(self.ps)]
        self.ps_i += 1
        return p

    def next_pb(self):
        p = self.pb[self.pb_i % len(self.pb)]
        self.pb_i += 1
        return p

    def prologue(self):
        nc, S, L = self.nc, self.S, self.L
        with contextlib.ExitStack() as st:
            stg = self.sb(st, "cv_f", [128, 4096], F32, 3)
            stb = self.sb(st, "cv_b", [128, 4096], BF16, 3)
            k = [0]

            def conv(src, dst, shape):
                n = int(np.prod(shape))
                assert n <= 4096
                (f, rf), (b, rb) = stg[k[0] % 3], stb[k[0] % 3]
                if len(shape) == 1:
                    fv, bv = f[:, 0:n], b[:, 0:n]
                else:
                    fv = f[:, 0:n].rearrange("p (a b) -> p a b", b=shape[1])
                    bv = b[:, 0:n].rearrange("p (a b) -> p a b", b=shape[1])
                S.dma("sp", fv, src, writes=[rf])
                if k[0] % 2 == 0:
                    S.op("dve", lambda: nc.vector.tensor_copy(out=b[:, 0:n], in_=f[:, 0:n]), reads=[rf], writes=[rb])
                else:
                    S.op("act", lambda: nc.scalar.copy(out=b[:, 0:n], in_=f[:, 0:n]), reads=[rf], writes=[rb])
                S.dma("pool", dst, bv, reads=[rb])
                k[0] += 1

            for l in range(L):
                wv = self.w_in_p[l].rearrange("(c p) n -> p c n", p=128)
                for c in range(8):
                    conv(wv[:, c, :], self.win_b[l, :, c, :], [WIN_COLS])
                conv(self.w_q_p[l].rearrange("(c p) n -> p c n", p=128), self.wq_b[l], [2, 768])
                conv(self.w_kv_p[l], self.wkv_b[l], [512])
                wv = self.w_out[l].rearrange("(c p) n -> p c n", p=128)
                for c0 in range(0, 8, 4):
                    conv(wv[:, c0:c0 + 4, :], self.wout_b[l, :, c0:c0 + 4, :], [4, 1024])
                wv = self.w_up[l].rearrange("(c p) n -> p c n", p=128)
                for g in range(8):
                    conv(wv[:, :, g * 512:(g + 1) * 512], self.wup_b[l, g], [8, 512])
                wv = self.w_down[l].rearrange("(f p) n -> p f n", p=128)
                for f0 in range(0, 32, 4):
                    conv(wv[:, f0:f0 + 4, :], self.wdown_b[l, :, f0:f0 + 4, :], [4, 1024])
            for h in range(4):
                conv(self.mask_a_d[h], self.maska_b[h], [MA_LEN])
                conv(self.mask_b_d[h], self.maskb_b[h], [MB_LEN])
            S.barrier()

    def phase_dmask(self, l):
        nc, S = self.nc, self.S
        with contextlib.ExitStack() as st:
            (dc, rdc) = self.sb(st, "dm_c", [128, 2 * MD_ARR + 12 * 512], F32)[0]
            bf = self.sb(st, "dm_bf", [128, MD_ARR], F32, 2)
            ex = self.sb(st, "dm_ex", [128, MD_ARR], F32, 2)
            full = self.sb(st, "dm_full", [128, MD_ARR], F32, 2)
            mo = self.sb(st, "dm_out", [128, MD_LEN], BF16, 2)
            S.dma("sp", dc[:], self.dconst_d[:, :], writes=[rdc])
            for h in range(4):
                (b, rb), (e, re), (fu, rfu), (m, rm) = bf[h % 2], ex[h % 2], full[h % 2], mo[h % 2]
                S.dma("sp", b[:], self.biasf_d[l, h], writes=[rb])
                S.op("act", lambda: nc.scalar.activation(out=e[:], in_=b[:], func=AF.Exp), reads=[rb], writes=[re])
                S.op("dve", lambda: nc.vector.tensor_tensor(out=m[:, 0:MD_ARR], in0=e[:], in1=dc[:, 0:MD_ARR], op=ALU.mult),
                     reads=[re, rdc], writes=[rm])
                S.op("dve", lambda: nc.vector.tensor_tensor(out=fu[:], in0=e[:], in1=dc[:, MD_ARR:2 * MD_ARR], op=ALU.mult),
                     reads=[re, rdc], writes=[rfu])
                for kt in range(6):
                    for (which, dr) in ((0, 2 * kt), (1, -4 + 2 * kt)):
                        o0 = MD_ARR + (which * 6 + kt) * 512
                        a0 = (10 - dr) * 64
                        v0 = 2 * MD_ARR + (which * 6 + kt) * 512
                        S.op("dve", lambda o0=o0, a0=a0, v0=v0: nc.vector.tensor_tensor(
                            out=m[:, o0:o0 + 512], in0=fu[:, a0:a0 + 512], in1=dc[:, v0:v0 + 512], op=ALU.mult),
                            reads=[rfu, rdc], writes=[rm])
                S.dma("pool", self.maskd_b[h], m[:], reads=[rm])
            S.barrier()

    def sbm(self, st, name, shape, dt, n, nres):
        out = []
        for i in range(n):
            t = st.enter_context(self.nc.sbuf_tensor("%s_%d_%d" % (name, self.uid, i), shape, dt))
            out.append((t, [Res() for _ in range(nres)]))
        self.uid += 1
        return out

    def rstd_from_acc(self, acc, racc, out, rout, c0, n):
        nc, S = self.nc, self.S
        S.op("dve", lambda: nc.vector.tensor_scalar(out=out[:, c0:c0 + n], in0=acc[:, c0:c0 + n], scalar1=EPS, scalar2=None, op0=ALU.add),
             reads=[racc], writes=[rout])
        S.op("pool", lambda: nc.gpsimd.tensor_tensor(out=out[:, c0:c0 + n], in0=out[:, c0:c0 + n], in1=self.neghalf[:, 0:n], op=ALU.pow),
             reads=[rout, self.r_neghalf], writes=[rout])

    def evac(self, i, out, in_, reads, writes, scale=None):
        nc, S = self.nc, self.S
        import os
        kev = os.environ.get("KEV")
        if kev == "0":
            return
        if kev == "1":
            i = 1
        if kev == "2":
            i = 0
        if kev == "3" and scale is None:
            i = 1
        if kev == "4":
            i = 1 if scale is not None else 0
        if i % 2 == 0:
            if scale is None:
                S.op("act", lambda: nc.scalar.activation(out=out, in_=in_, func=AF.Copy), reads=reads, writes=writes)
            else:
                S.op("act", lambda: nc.scalar.activation(out=out, in_=in_, func=AF.Copy, scale=scale), reads=reads, writes=writes)
        else:
            if scale is None:
                S.op("dve", lambda: nc.vector.tensor_copy(out=out, in_=in_), reads=reads, writes=writes)
            else:
                S.op("dve", lambda: nc.vector.tensor_scalar(out=out, in0=in_, scalar1=scale, scalar2=None, op0=ALU.mult),
                     reads=reads, writes=writes)

    def norm_front(self, xt, rxt, xn, rxn, ss, rss, rs, rrs, D=1024):
        nc, S = self.nc, self.S
        sc = float(D) ** -0.5
        for sub in range(4):
            S.op("act", lambda: nc.scalar.activation(out=xn[:, sub, :], in_=xt[:, sub, :], func=AF.Square, scale=sc,
                                                     accum_out=ss[:, sub:sub + 1]),
                 reads=[rxt[sub]], writes=[rss, rxn[sub]])
        self.rstd_from_acc(ss, rss, rs, rrs, 0, 4)
        for sub in range(4):
            S.op("dve", lambda: nc.vector.tensor_scalar(out=xn[:, sub, :], in0=xt[:, sub, :], scalar1=rs[:, sub:sub + 1],
                                                        scalar2=None, op0=ALU.mult),
                 reads=[rxt[sub], rrs], writes=[rxn[sub]])

    def norm_tpose(self, xn, rxn, hT, rhT, gain, rgain, D=1024):
        nc, S = self.nc, self.S
        for c in range(D // 128):
            pb, rpb = self.next_pb()
            for sub in range(4):
                S.op("pe", lambda: nc.tensor.transpose(out=pb[:, sub * 128:(sub + 1) * 128], in_=xn[:, sub, c * 128:(c + 1) * 128],
                                                       identity=self.identb[:]),
                     reads=[rxn[sub], self.r_identb], writes=[rpb], signal=(sub == 3))
            self.evac(c, hT[:, c, :], pb[:, 0:512], [rpb, rgain], [rhT[c]], scale=gain[:, c:c + 1])

    def chk(self, n):
        import os
        if int(os.environ.get('KSUB', '99')) < n:
            raise _Stop()

    def tiles(self):
        out = []
        for si, (s0, sl) in enumerate(zip(self.starts, self.seqs)):
            for t0 in range(s0, s0 + sl, 512):
                out.append((si, s0, t0))
        return out

    def phase1(self, l):
        nc, S = self.nc, self.S
        x_src = self.x_in if l == 0 else self.xres
        with contextlib.ExitStack() as st:
            self.alloc_psum(st, 6, 2)
            (win, rwin) = self.sb(st, "p1_win", [128, 8, WIN_COLS], BF16)[0]
            (wq, rwq) = self.sb(st, "p1_wq", [128, 2, 768], BF16)[0]
            (wkv, rwkv) = self.sb(st, "p1_wkv", [128, 512], BF16)[0]
            (pc, rpc) = self.sb(st, "p1_pc", [128, NPCOL], F32)[0]
            xts = self.sbm(st, "p1_xt", [128, 4, 1024], F32, 2, 4)
            xns = self.sbm(st, "p1_xn", [128, 4, 1024], BF16, 2, 4)
            hTs = self.sbm(st, "p1_hT", [128, 8, 512], BF16, 2, 8)
            junk = None
            (junk2, _) = self.sb(st, "p1_junk2", [128, 256], BF16)[0]
            sss = self.sb(st, "p1_ss", [128, 8], F32, 2)
            rss_ = self.sb(st, "p1_rs", [128, 8], F32, 2)
            qks = self.sbm(st, "p1_qk", [128, 11, 512], BF16, 2, 11)
            vss = self.sbm(st, "p1_vs", [128, 4, NVH, 65], BF16, 2, 12)
            cns = self.sbm(st, "p1_cn", [128, 4, 384], BF16, 2, 8)
            cqTs = self.sbm(st, "p1_cqT", [128, 3, 512], BF16, 2, 3)
            ssc = self.sbm(st, "p1_ssc", [128, 8], F32, 2, 4)
            rsc = self.sbm(st, "p1_rsc", [128, 8], F32, 2, 4)
            rts = self.sb(st, "p1_rt", [128, 2, 512], F32, 2)
            tmps = self.sb(st, "p1_tmp", [128, 512], F32, 2)
            kpes = self.sb(st, "p1_kpe", [128, 512], F32, 2)
            qTs = self.sbm(st, "p1_qT", [128, 4, 512], BF16, 2, 8)
            kTs = self.sbm(st, "p1_kT", [128, 4, 512], BF16, 2, 8)

            S.dma("sp", win[:], self.win_b[l], writes=[rwin])
            S.dma("sp", wq[:], self.wq_b[l], writes=[rwq])
            S.dma("sp", wkv[:], self.wkv_b[l], writes=[rwkv])
            S.dma("sp", pc[:], self.pcol_d[l], writes=[rpc])
            for (v, rv) in vss:
                S.op("dve", lambda: nc.vector.memset(v[:], 1.0), writes=rv)

            tl = self.tiles()

            def load_x(i):
                si, s0, t0 = tl[i]
                xt, rxt = xts[i % 2]
                S.dma("sp", xt[:], x_src[t0:t0 + 512, :].rearrange("(s p) d -> p s d", p=128),
                      reads=[self.r_x[t0 // 512]], writes=rxt)

            def load_rt(i):
                si, s0, t0 = tl[i]
                rt, rrt = rts[i % 2]
                p0 = t0 - s0
                S.dma("sp", rt[64:96, :, :], self.rope_d[:, :, p0:p0 + 512].rearrange("a r t -> r a t"), writes=[rrt])

            def front(i):
                xt, rxt = xts[i % 2]
                xn, rxn = xns[i % 2]
                ss, rss = sss[i % 2]
                rs, rrs = rss_[i % 2]
                self.norm_front(xt, rxt, xn, rxn, ss, rss, rs, rrs)

            load_x(0)
            load_rt(0)
            if len(tl) > 1:
                load_x(1)
            front(0)
            tk = 0
            for i, (si, s0, t0) in enumerate(tl):
                if i + 1 < len(tl):
                    load_rt(i + 1)
                b = i % 2
                xt, rxt = xts[b]
                xn, rxn = xns[b]
                hT, rhT = hTs[b]
                ss, rss = sss[b]
                rs, rrs = rss_[b]
                qk, rqk = qks[b]
                vs, rvs = vss[b]
                cn, rcn = cns[b]
                cqT, rcqT = cqTs[b]
                sc_, rsc_ = ssc[b]
                rc_, rrc_ = rsc[b]
                rt, rrt = rts[b]
                qT, rqT = qTs[b]
                kT, rkT = kTs[b]
                ti = t0 // 512
                try:
                  self.phase1_tile(locals())
                except _Stop:
                  break
            S.barrier()

    def phase1_tile(self, L_):
                nc, S = self.nc, self.S
                globals_ = L_
                (xt, rxt, xn, rxn, ss, rss, rs, rrs, junk, hT, rhT, pc, rpc, win, rwin, qk, rqk, t0, ti, vs, rvs, junk2, sc_, rsc_, rc_, rrc_, cn, rcn, cqT, rcqT, tmps, kpes, rt, rrt, kT, rkT, qT, rqT, wq, rwq, wkv, rwkv) = [L_[k] for k in 'xt rxt xn rxn ss rss rs rrs junk hT rhT pc rpc win rwin qk rqk t0 ti vs rvs junk2 sc_ rsc_ rc_ rrc_ cn rcn cqT rcqT tmps kpes rt rrt kT rkT qT rqT wq rwq wkv rwkv'.split()]
                tk = 0
                self.chk(1)
                self.norm_tpose(xn, rxn, hT, rhT, pc[:, PC_NATT:PC_NATT + 8], rpc)
                if L_["i"] + 1 < len(L_["tl"]):
                    L_["front"](L_["i"] + 1)
                if L_["i"] + 2 < len(L_["tl"]):
                    L_["load_x"](L_["i"] + 2)
                self.chk(2)
                for j in range(11):
                    ps, rps = self.next_ps()
                    for kc in range(8):
                        S.op("pe", lambda: nc.tensor.matmul(ps[:, :], win[:, kc, j * 128:(j + 1) * 128], hT[:, kc, :],
                                                            start=(kc == 0), stop=(kc == 7)),
                             reads=[rwin, rhT[kc]], writes=[rps], signal=(kc == 7))
                    self.evac(j, qk[:, j, :], ps[:, :], [rps], [rqk[j]])
                import os
                if not os.environ.get("KNOSTORE"):
                    S.dma("sp", self.qkT[:, t0:t0 + 512].rearrange("(j p) t -> p j t", p=128), qk[:, :, :],
                          reads=rqk, writes=[self.r_qk[ti]])
                self.chk(3)
                for sub in range(4):
                    psA, rpsA = self.next_ps()
                    for kc in range(8):
                        S.op("pe", lambda: nc.tensor.matmul(psA[:, :], hT[:, kc, sub * 128:(sub + 1) * 128], win[:, kc, TM0:TM0 + 512],
                                                            start=(kc == 0), stop=(kc == 7)),
                             reads=[rwin, rhT[kc]], writes=[rpsA], signal=(kc == 7))
                    self.evac(0, vs[:, sub, 0:8, 0:64], psA[:, :].rearrange("p (h d) -> p h d", d=64), [rpsA], [rvs[sub * 3]])
                    psB, rpsB = self.next_ps()
                    for kc in range(8):
                        S.op("pe", lambda: nc.tensor.matmul(psB[:, :], hT[:, kc, sub * 128:(sub + 1) * 128], win[:, kc, TM0 + 512:TM0 + 1024],
                                                            start=(kc == 0), stop=(kc == 7)),
                             reads=[rwin, rhT[kc]], writes=[rpsB], signal=(kc == 7))
                    self.evac(0, vs[:, sub, 8:10, 0:64], psB[:, 0:128].rearrange("p (h d) -> p h d", d=64), [rpsB], [rvs[sub * 3 + 1]])
                    S.op("act", lambda: nc.scalar.activation(out=junk2[:, 0:256], in_=psB[:, 128:384], func=AF.Square, scale=1.0 / 16.0,
                                                             accum_out=sc_[:, 2 * sub:2 * sub + 1]),
                         reads=[rpsB], writes=[rsc_[sub]])
                    S.op("act", lambda: nc.scalar.activation(out=junk2[:, 0:128], in_=psB[:, 384:512], func=AF.Square, scale=128.0 ** -0.5,
                                                             accum_out=sc_[:, 2 * sub + 1:2 * sub + 2]),
                         reads=[rpsB], writes=[rsc_[sub]])
                    self.rstd_from_acc(sc_, rsc_[sub], rc_, rrc_[sub], 2 * sub, 2)
                    S.op("dve", lambda: nc.vector.tensor_scalar(out=cn[:, sub, 0:256], in0=psB[:, 128:384], scalar1=rc_[:, 2 * sub:2 * sub + 1],
                                                                scalar2=None, op0=ALU.mult),
                         reads=[rpsB, rrc_[sub]], writes=[rcn[2 * sub]])
                    S.op("dve", lambda: nc.vector.tensor_scalar(out=cn[:, sub, 256:384], in0=psB[:, 384:512], scalar1=rc_[:, 2 * sub + 1:2 * sub + 2],
                                                                scalar2=None, op0=ALU.mult),
                         reads=[rpsB, rrc_[sub]], writes=[rcn[2 * sub + 1]])
                self.chk(4)
                for c in range(3):
                    pb, rpb = self.next_pb()
                    for sub in range(4):
                        S.op("pe", lambda: nc.tensor.transpose(out=pb[:, sub * 128:(sub + 1) * 128], in_=cn[:, sub, c * 128:(c + 1) * 128],
                                                               identity=self.identb[:]),
                             reads=[rcn[2 * sub + (1 if c == 2 else 0)], self.r_identb], writes=[rpb], signal=(sub == 3))
                    self.evac(c, cqT[:, c, :], pb[:, 0:512], [rpb, rpc], [rcqT[c]], scale=pc[:, PC_QN + c:PC_QN + c + 1])
                self.chk(5)
                tmp, rtmp = tmps[tk % 2]
                kpe, rkpe = kpes[tk % 2]
                tk += 1
                ps1, rps1 = self.next_ps()
                for kc in range(8):
                    S.op("pe", lambda: nc.tensor.matmul(ps1[0:96, :], win[:, kc, KR0:KR0 + 96], hT[:, kc, :], start=(kc == 0), stop=(kc == 7)),
                         reads=[rwin, rhT[kc]], writes=[rps1], signal=(kc == 7))
                ps2, rps2 = self.next_ps()
                for kc in range(8):
                    S.op("pe", lambda: nc.tensor.matmul(ps2[0:96, :], win[:, kc, KR0 + 96:KR0 + 192], hT[:, kc, :], start=(kc == 0), stop=(kc == 7)),
                         reads=[rwin, rhT[kc]], writes=[rps2], signal=(kc == 7))
                S.op("dve", lambda: nc.vector.tensor_tensor(out=tmp[64:96, :], in0=ps2[64:96, :], in1=rt[64:96, 1, :], op=ALU.mult),
                     reads=[rps2, rrt], writes=[rtmp])
                S.op("dve", lambda: nc.vector.tensor_tensor(out=kpe[64:96, :], in0=ps1[64:96, :], in1=rt[64:96, 0, :], op=ALU.mult),
                     reads=[rps1, rrt], writes=[rkpe])
                for h in range(4):
                    S.op("pool", lambda: nc.gpsimd.tensor_tensor(out=kT[64:96, h, :], in0=kpe[64:96, :], in1=tmp[64:96, :], op=ALU.add),
                         reads=[rkpe, rtmp], writes=[rkT[2 * h + 1]])
                self.chk(6)
                for h in range(4):
                    tmp, rtmp = tmps[tk % 2]
                    kpe, rkpe = kpes[tk % 2]
                    tk += 1
                    pq1, rpq1 = self.next_ps()
                    for kc in range(2):
                        S.op("pe", lambda: nc.tensor.matmul(pq1[0:96, :], wq[:, kc, h * 192:h * 192 + 96], cqT[:, kc, :], start=(kc == 0), stop=(kc == 1)),
                             reads=[rwq, rcqT[kc]], writes=[rpq1], signal=(kc == 1))
                    pq2, rpq2 = self.next_ps()
                    for kc in range(2):
                        S.op("pe", lambda: nc.tensor.matmul(pq2[0:96, :], wq[:, kc, h * 192 + 96:h * 192 + 192], cqT[:, kc, :], start=(kc == 0), stop=(kc == 1)),
                             reads=[rwq, rcqT[kc]], writes=[rpq2], signal=(kc == 1))
                    S.op("act", lambda: nc.scalar.copy(out=qT[0:64, h, :], in_=pq1[0:64, :]), reads=[rpq1], writes=[rqT[2 * h]])
                    S.op("dve", lambda: nc.vector.tensor_tensor(out=tmp[64:96, :], in0=pq2[64:96, :], in1=rt[64:96, 1, :], op=ALU.mult),
                         reads=[rpq2, rrt], writes=[rtmp])
                    S.op("dve", lambda: nc.vector.tensor_tensor(out=kpe[64:96, :], in0=pq1[64:96, :], in1=rt[64:96, 0, :], op=ALU.mult),
                         reads=[rpq1, rrt], writes=[rkpe])
                    S.op("pool", lambda: nc.gpsimd.tensor_tensor(out=qT[64:96, h, :], in0=kpe[64:96, :], in1=tmp[64:96, :], op=ALU.add),
                         reads=[rkpe, rtmp], writes=[rqT[2 * h + 1]])
                    pk, rpk = self.next_ps()
                    S.op("pe", lambda: nc.tensor.matmul(pk[0:64, :], wkv[:, h * 64:(h + 1) * 64], cqT[:, 2, :], start=True, stop=True),
                         reads=[rwkv, rcqT[2]], writes=[rpk])
                    self.evac(h, kT[0:64, h, :], pk[0:64, :], [rpk], [rkT[2 * h]])
                for sub in range(4):
                    pv, rpv = self.next_ps()
                    S.op("pe", lambda: nc.tensor.matmul(pv[:, 0:256], cqT[:, 2, sub * 128:(sub + 1) * 128], wkv[:, 256:512], start=True, stop=True),
                         reads=[rwkv, rcqT[2]], writes=[rpv])
                    self.evac(0, vs[:, sub, 10:14, 0:64], pv[:, 0:256].rearrange("p (h d) -> p h d", d=64), [rpv], [rvs[sub * 3 + 2]])
                self.chk(7)
                S.dma("sp", self.qTc[:, :, t0:t0 + 512].rearrange("h r t -> r h t"), qT[0:96, :, :], reads=rqT, writes=[self.r_qc[ti]])
                S.dma("sp", self.kTc[:, :, t0:t0 + 512].rearrange("h r t -> r h t"), kT[0:96, :, :], reads=rkT, writes=[self.r_kc[ti]])
                S.dma("sp", self.vsc[t0:t0 + 512, :, :].rearrange("(s p) h e -> p s h e", p=128), vs[:, :, :, :],
                      reads=rvs, writes=[self.r_v[ti]])


    def phase2(self, l):
        nc, S = self.nc, self.S
        SPAN = 2048
        SMAXK = self.smax
        LOOK, EP_DEFER, GN_DEFER = int(_os.environ.get('KLOOK', '3')), 2, 3
        NP = LOOK + 1
        with contextlib.ExitStack() as st:
            scs = [(st.enter_context(nc.psum_tensor("p2sc%d_%d" % (i, self.uid), [128, 1024], F32)), Res()) for i in range(2)]
            self.alloc_psum(st, 3, 1)
            o_banks = self.ps[0:2]
            tp_bank = self.ps[2]
            Ks = self.sb(st, "p2_K", [128, SMAXK], BF16, 2)
            Vs = self.sb(st, "p2_V", [128, SMAXK // 128, 65], BF16, 2)
            Ms = self.sb(st, "p2_M", [128, MASK_MAX], BF16, 2)
            Qs = self.sb(st, "p2_Q", [96, 512], BF16, 3)
            QLs = self.sb(st, "p2_QL", [128, 512], BF16, 3)
            QHs = self.sb(st, "p2_QH", [128, 512], BF16, 3)
            Ps = self.sb(st, "p2_P", [128, 1024], BF16, NP)
            P2s = self.sbm(st, "p2_P2", [128, 1024], BF16, NP, 2)
            osbs = self.sb(st, "p2_osb", [65, 512], F32, 3)
            ogrps = self.sbm(st, "p2_ogrp", [128, SPAN // 128, 256], F32, 2, SPAN // 512)
            (pc, rpc) = self.sb(st, "p2_pc", [128, NPCOL], F32)[0]
            (esink, resink) = self.sb(st, "p2_esink", [128, 4], F32)[0]
            (junk, _) = self.sb(st, "p2_junk", [128, 256], BF16)[0]
            gss = self.sb(st, "p2_gss", [128, 4], F32, 2)
            grs = self.sb(st, "p2_grs", [128, 4], F32, 2)
            recs = self.sb(st, "p2_rec", [128, 4], F32, 3)
            ons = self.sbm(st, "p2_on", [128, 4, 256], BF16, 2, 4)
            onTs = self.sbm(st, "p2_onT", [128, 2, 512], BF16, 2, 2)
            S.dma("sp", pc[:], self.pcol_d[l], writes=[rpc])
            S.op("act", lambda: nc.scalar.activation(out=esink[:], in_=pc[:, PC_SINK:PC_SINK + 4], func=AF.Exp),
                 reads=[rpc], writes=[resink])
            for (qb, rqb) in QLs + QHs:
                S.op("pool", lambda: nc.gpsimd.memset(qb[:], 0.0), writes=[rqb])
            ctr = {"q": 0, "ql": 0, "qh": 0, "p": 0, "sc": 0, "ob": 0, "gn": 0, "ep": 0}

            mixers = [
                ("A", 64, 0.125, ROW_AQ, ROW_AK, 0, 1),
                ("B", 64, 0.125, ROW_BQ, ROW_BK, 4, 2),
                ("C", 96, 96.0 ** -0.5, None, None, 10, 1),
                ("D", 64, 0.125, ROW_DQ, ROW_DK, 6, 1),
            ]
            halo = {"A": (1024, 1024), "B": (128, 128), "D": (256, 256)}

            for si, (s0, sl) in enumerate(zip(self.starts, self.seqs)):
                for sp0 in range(0, sl, SPAN):
                    span = min(SPAN, sl - sp0)
                    nq = span // 512
                    ctxs = []
                    units = []
                    for mi, (mname, dqk, scale, qrow, krow, vbase, grp) in enumerate(mixers):
                        for h in range(4):
                            if mname == "C":
                                wlo, whi = 0, sl
                            else:
                                wlo, whi = max(sp0 - halo[mname][0], 0), min(sp0 + span + halo[mname][1], sl)
                            c = dict(mi=mi, mname=mname, dqk=dqk, scale=scale, qrow=qrow, krow=krow, vbase=vbase, h=h, kvh=h // grp,
                                     wlo=wlo, whi=whi, kt_base=wlo // 128, start=len(units))
                            ci = len(ctxs)
                            ctxs.append(c)
                            for qi in range(nq):
                                q0 = sp0 + qi * 512
                                if mname == "A":
                                    lo, hi = q0 - 1024, q0 + 512 + 1024
                                elif mname == "B":
                                    lo, hi = q0 - 128, q0 + 512 + 128
                                elif mname == "C":
                                    lo, hi = 0, sl
                                else:
                                    lo, hi = q0 - 256, q0 + 768
                                lo, hi = max(lo, 0), min(hi, sl)
                                kts = list(range(lo // 128, hi // 128))
                                for a in range(0, len(kts), 2):
                                    units.append((ci, qi, q0, kts[a:a + 2], a == 0, a + 2 >= len(kts)))
                    state = {}
                    dq = []

                    def load_ctx(ci):
                        c = ctxs[ci]
                        K, rK = Ks[ci % 2]
                        V, rV = Vs[ci % 2]
                        wlo, whi = c["wlo"], c["whi"]
                        wlen = whi - wlo
                        tiles_rng = range((s0 + wlo) // 512, (s0 + whi + 511) // 512)
                        if c["krow"] is not None:
                            r0 = c["krow"] + (c["kvh"] // 2) * 128
                            ksrc = self.qkT[r0:r0 + 128, s0 + wlo:s0 + whi]
                            rdep = [self.r_qk[t] for t in tiles_rng]
                            S.dma("sp", K[0:128, 0:wlen], ksrc, reads=rdep, writes=[rK])
                        else:
                            ksrc = self.kTc[c["h"], :, s0 + wlo:s0 + whi]
                            rdep = [self.r_kc[t] for t in tiles_rng]
                            S.dma("sp", K[0:c["dqk"], 0:wlen], ksrc, reads=rdep, writes=[rK])
                        S.dma("sp", V[:, 0:wlen // 128, :],
                              self.vsc[s0 + wlo:s0 + whi, c["vbase"] + c["kvh"], :].rearrange("(k p) e -> p k e", p=128),
                              reads=[self.r_v[t] for t in tiles_rng], writes=[rV])
                        M = rM = None
                        if c["mname"] != "C":
                            M, rM = Ms[ci % 2]
                            if c["mname"] == "A":
                                S.dma("sp", M[:, 0:MA_LEN], self.maska_b[c["h"]], writes=[rM])
                            elif c["mname"] == "B":
                                S.dma("sp", M[:, 0:MB_LEN], self.maskb_b[c["h"]], writes=[rM])
                            else:
                                S.dma("sp", M[:, 0:MD_LEN], self.maskd_b[c["h"]], writes=[rM])
                        c.update(K=K, rK=rK, V=V, rV=rV, M=M, rM=rM)

                    def mask_ap(c, q0, kt):
                        dlt = kt * 128 - q0
                        if c["mname"] == "A":
                            off = 1408 - dlt
                        elif c["mname"] == "B":
                            off = 512 - dlt
                        else:
                            rows = sl // 64
                            R0 = q0 // 64
                            if R0 == 0:
                                off = MD_ARR + (dlt // 128) * 512
                            elif R0 == rows - 8:
                                off = MD_ARR + (6 + (dlt + 256) // 128) * 512
                            else:
                                off = 640 - dlt
                        return c["M"][:, off:off + 512]

                    def emit_S(i):
                        ci, qi, q0, kts, first, last = units[i]
                        c = ctxs[ci]
                        nk = len(kts)
                        dqk, K, rK, kb = c["dqk"], c["K"], c["rK"], c["kt_base"]
                        if c["qrow"] is not None:
                            dqk = 128
                        if first:
                            t0 = s0 + q0
                            if c["qrow"] is not None:
                                half = c["kvh"] % 2
                                if half == 0:
                                    Q, rQ = QLs[ctr["ql"] % 3]
                                    ctr["ql"] += 1
                                else:
                                    Q, rQ = QHs[ctr["qh"] % 3]
                                    ctr["qh"] += 1
                                r0 = c["qrow"] + c["h"] * 64
                                qsrc = self.qkT[r0:r0 + 64, t0:t0 + 512]
                                rd = [self.r_qk[t0 // 512]]
                                S.dma("sp", Q[half * 64:half * 64 + 64, :], qsrc, reads=rd, writes=[rQ])
                            else:
                                Q, rQ = Qs[ctr["q"] % 3]
                                ctr["q"] += 1
                                qsrc = self.qTc[c["h"], :, t0:t0 + 512]
                                rd = [self.r_qc[t0 // 512]]
                                S.dma("sp", Q[0:dqk, :], qsrc, reads=rd, writes=[rQ])
                            state[("q", ci, qi)] = (Q, rQ)
                        Q, rQ = state[("q", ci, qi)]
                        sc, rsc = scs[ctr["sc"] % 2]
                        ctr["sc"] += 1
                        for a, kt in enumerate(kts):
                            S.op("pe", lambda: nc.tensor.matmul(sc[:, a * 512:(a + 1) * 512], K[0:dqk, (kt - kb) * 128:(kt - kb + 1) * 128], Q[0:dqk, :],
                                                                start=True, stop=True),
                                 reads=[rK, rQ], writes=[rsc], signal=(a == nk - 1))
                        P, rP = Ps[ctr["p"] % NP]
                        S.op("act", lambda: nc.scalar.activation(out=P[:, 0:nk * 512], in_=sc[:, 0:nk * 512], func=AF.Exp, scale=c["scale"]),
                             reads=[rsc], writes=[rP])
                        if c["M"] is not None:
                            P2, (rP2, rP2b) = P2s[ctr["p"] % NP]
                            for a, kt in enumerate(kts):
                                mk = mask_ap(c, q0, kt)
                                if a == 1 and POOL_MASK:
                                    S.op("pool", lambda: nc.gpsimd.tensor_tensor(out=P2[:, a * 512:(a + 1) * 512], in0=P[:, a * 512:(a + 1) * 512],
                                                                                 in1=mk, op=ALU.mult),
                                         reads=[rP, c["rM"]], writes=[rP2b])
                                else:
                                    S.op("dve", lambda: nc.vector.tensor_tensor(out=P2[:, a * 512:(a + 1) * 512], in0=P[:, a * 512:(a + 1) * 512],
                                                                                in1=mk, op=ALU.mult),
                                         reads=[rP, c["rM"]], writes=[rP2])
                            state[("p", i)] = (P2, [rP2, rP2b])
                        else:
                            state[("p", i)] = (P, [rP, rP])
                        ctr["p"] += 1

                    def emit_PV(j):
                        ci, qi, q0, kts, first, last = units[j]
                        c = ctxs[ci]
                        nk = len(kts)
                        V, rV, kb = c["V"], c["rV"], c["kt_base"]
                        P, rP = state.pop(("p", j))
                        if first:
                            ob, rob = o_banks[ctr["ob"] % 2]
                            ctr["ob"] += 1
                            state[("o", ci, qi)] = (ob, rob)
                        ob, rob = state[("o", ci, qi)]
                        for a, kt in enumerate(kts):
                            S.op("pe", lambda: nc.tensor.matmul(ob[0:65, :], V[:, kt - kb, :], P[:, a * 512:(a + 1) * 512],
                                                                start=(first and a == 0), stop=(last and a == nk - 1)),
                                 reads=[rV, rP[a]], writes=[rob], signal=(a == nk - 1))
                        if last:
                            e = ctr["ep"] % 3
                            ctr["ep"] += 1
                            osb, rosb = osbs[e]
                            S.op("dve", lambda: nc.vector.tensor_copy(out=osb[:, :], in_=ob[0:65, :]), reads=[rob], writes=[rosb])
                            dq.append([EP_DEFER, lambda: emit_epilogue(ci, qi, e)])

                    def emit_epilogue(ci, qi, e):
                        c = ctxs[ci]
                        h, mi = c["h"], c["mi"]
                        osb, rosb = osbs[e]
                        rec, rrec = recs[e]
                        ogrp, rogrp = ogrps[mi % 2]
                        tp, rtp = tp_bank
                        for sub in range(4):
                            S.op("pe", lambda: nc.tensor.transpose(out=tp[:, sub * 65:(sub + 1) * 65], in_=osb[0:65, sub * 128:(sub + 1) * 128],
                                                                   identity=self.identf[0:65, 0:65]),
                                 reads=[rosb, self.r_identf], writes=[rtp], signal=(sub == 3))
                        den = tp[:, 0:260].rearrange("p (s e) -> p s e", e=65)[:, :, 64]
                        if c["mname"] == "B":
                            S.op("dve", lambda: nc.vector.tensor_scalar(out=rec[:, 0:4], in0=den, scalar1=esink[:, h:h + 1], scalar2=None, op0=ALU.add),
                                 reads=[rtp, resink], writes=[rrec])
                            S.op("dve", lambda: nc.vector.reciprocal(out=rec[:, 0:4], in_=rec[:, 0:4]), reads=[rrec], writes=[rrec])
                        else:
                            S.op("dve", lambda: nc.vector.reciprocal(out=rec[:, 0:4], in_=den), reads=[rtp], writes=[rrec])
                        for sub in range(4):
                            S.op("dve", lambda: nc.vector.tensor_scalar(out=ogrp[:, qi * 4 + sub, h * 64:(h + 1) * 64],
                                                                        in0=tp[:, sub * 65:sub * 65 + 64], scalar1=rec[:, sub:sub + 1],
                                                                        scalar2=None, op0=ALU.mult),
                                 reads=[rtp, rrec], writes=[rogrp[qi]])
                        if h == 3:
                            emit_gn_a(mi, qi)

                    def emit_gn_a(mi, qi):
                        g = ctr["gn"] % 2
                        ctr["gn"] += 1
                        ogrp, rogrp = ogrps[mi % 2]
                        gs, rgs = gss[g]
                        gr, rgr = grs[g]
                        on, ron = ons[g]
                        for sub in range(4):
                            S.op("act", lambda: nc.scalar.activation(out=junk[:, :], in_=ogrp[:, qi * 4 + sub, :], func=AF.Square, scale=1.0 / 16.0,
                                                                     accum_out=gs[:, sub:sub + 1]),
                                 reads=[rogrp[qi]], writes=[rgs])
                        self.rstd_from_acc(gs, rgs, gr, rgr, 0, 4)
                        for sub in range(4):
                            S.op("dve", lambda: nc.vector.tensor_scalar(out=on[:, sub, :], in0=ogrp[:, qi * 4 + sub, :], scalar1=gr[:, sub:sub + 1],
                                                                        scalar2=None, op0=ALU.mult),
                                 reads=[rogrp[qi], rgr], writes=[ron[sub]])
                        dq.append([GN_DEFER, lambda: emit_gn_b(mi, qi, g)])

                    def emit_gn_b(mi, qi, g):
                        t0 = s0 + sp0 + qi * 512
                        on, ron = ons[g]
                        onT, ronT = onTs[g]
                        for cc in range(2):
                            pb, rpb = self.next_pb()
                            for sub in range(4):
                                S.op("pe", lambda: nc.tensor.transpose(out=pb[:, sub * 128:(sub + 1) * 128], in_=on[:, sub, cc * 128:(cc + 1) * 128],
                                                                       identity=self.identb[:]),
                                     reads=[ron[sub], self.r_identb], writes=[rpb], signal=(sub == 3))
                            gc = PC_GN + mi * 2 + cc
                            S.op("dve", lambda: nc.vector.tensor_scalar(out=onT[:, cc, :], in0=pb[:, 0:512], scalar1=pc[:, gc:gc + 1], scalar2=None, op0=ALU.mult),
                                 reads=[rpb, rpc], writes=[ronT[cc]])
                        S.dma("pool", self.onT[mi * 256:(mi + 1) * 256, t0:t0 + 512].rearrange("(c p) t -> p c t", p=128), onT[:, :, :],
                              reads=ronT, writes=[self.r_on[t0 // 512]])

                    def tick():
                        ready = [f for (cnt, f) in dq if cnt <= 0]
                        rest = [[cnt - 1, f] for (cnt, f) in dq if cnt > 0]
                        dq[:] = rest
                        for f in ready:
                            f()

                    n = len(units)
                    load_ctx(0)
                    for i in range(n + LOOK):
                        if i < n:
                            ci = units[i][0]
                            if i == ctxs[ci]["start"] + LOOK and ci + 1 < len(ctxs):
                                load_ctx(ci + 1)
                            emit_S(i)
                        tick()
                        if i - LOOK >= 0:
                            emit_PV(i - LOOK)
                    while dq:
                        tick()
            S.barrier()

    def phase3(self, l):
        nc, S = self.nc, self.S
        last = (l == self.L - 1)
        x_src = self.x_in if l == 0 else self.xres
        with contextlib.ExitStack() as st:
            self.alloc_psum(st, 6, 2)
            (wo, rwo) = self.sb(st, "p3_wo", [128, 8, 1024], BF16)[0]
            (wd, rwd) = self.sb(st, "p3_wd", [128, 32, 1024], BF16)[0]
            wus = self.sb(st, "p3_wu", [128, 8, 512], BF16, 2)
            (pc, rpc) = self.sb(st, "p3_pc", [128, NPCOL], F32)[0]
            (nf, rnf) = self.sb(st, "p3_nf", [128, 1024], F32)[0]
            xts = self.sbm(st, "p3_xt", [128, 4, 1024], F32, 2, 4)
            (xn, rxn) = self.sbm(st, "p3_xn", [128, 4, 1024], BF16, 1, 4)[0]
            onTs = self.sb(st, "p3_onT", [128, 8, 512], BF16, 2)
            (hT, rhT) = self.sbm(st, "p3_hT", [128, 8, 512], BF16, 1, 8)[0]
            (uT, ruT) = self.sbm(st, "p3_uT", [128, 32, 512], BF16, 1, 32)[0]
            rls = self.sb(st, "p3_rl", [128, 512], BF16, 3)
            junk = None
            sss = self.sb(st, "p3_ss", [128, 8], F32, 2)
            rss_ = self.sb(st, "p3_rs", [128, 8], F32, 2)
            S.dma("sp", wo[:], self.wout_b[l], writes=[rwo])
            S.dma("sp", pc[:], self.pcol_d[l], writes=[rpc])
            S.dma("sp", nf[:], self.nfin_d[:, :], writes=[rnf])
            S.dma("sp", wd[:], self.wdown_b[l], writes=[rwd])
            tl = self.tiles()
            wu_ctr = [0]

            def load(i):
                si, s0, t0 = tl[i]
                xt, rxt = xts[i % 2]
                oT, roT = onTs[i % 2]
                S.dma("sp", xt[:], x_src[t0:t0 + 512, :].rearrange("(s p) d -> p s d", p=128),
                      reads=[self.r_x[t0 // 512]], writes=rxt)
                S.dma("sp", oT[:], self.onT[:, t0:t0 + 512].rearrange("(c p) t -> p c t", p=128),
                      reads=[self.r_on[t0 // 512]], writes=[roT])

            def load_wu(g):
                wu, rwu = wus[wu_ctr[0] % 2]
                wu_ctr[0] += 1
                S.dma("sp", wu[:], self.wup_b[l, g], writes=[rwu])
                return wu, rwu

            def outproj_front(i):
                xt, rxt = xts[i % 2]
                oT, roT = onTs[i % 2]
                ss, rss = sss[i % 2]
                rs, rrs = rss_[i % 2]
                for sub in range(4):
                    for half in range(2):
                        ps, rps = self.next_ps()
                        for c in range(8):
                            S.op("pe", lambda: nc.tensor.matmul(ps[:, :], oT[:, c, sub * 128:(sub + 1) * 128], wo[:, c, half * 512:(half + 1) * 512],
                                                                start=(c == 0), stop=(c == 7)),
                                 reads=[roT, rwo], writes=[rps], signal=(c == 7))
                        S.op("dve", lambda: nc.vector.tensor_tensor(out=xt[:, sub, half * 512:(half + 1) * 512], in0=ps[:, :],
                                                                    in1=xt[:, sub, half * 512:(half + 1) * 512], op=ALU.add),
                             reads=[rps, rxt[sub]], writes=[rxt[sub]])
                self.norm_front(xt, rxt, xn, rxn, ss, rss, rs, rrs)

            load(0)
            for i, (si, s0, t0) in enumerate(tl):
                ti = t0 // 512
                xt, rxt = xts[i % 2]
                ss, rss = sss[i % 2]
                rs, rrs = rss_[i % 2]
                wu_next = load_wu(0)
                outproj_front(i)
                if i + 1 < len(tl):
                    load(i + 1)
                self.norm_tpose(xn, rxn, hT, rhT, pc[:, PC_NMLP:PC_NMLP + 8], rpc)
                for g in range(8):
                    wu, rwu = wu_next
                    if g + 1 < 8:
                        wu_next = load_wu(g + 1)
                    for f in range(4):
                        fc = g * 4 + f
                        ps, rps = self.next_ps()
                        for c in range(8):
                            S.op("pe", lambda: nc.tensor.matmul(ps[:, :], wu[:, c, f * 128:(f + 1) * 128], hT[:, c, :],
                                                                start=(c == 0), stop=(c == 7)),
                                 reads=[rwu, rhT[c]], writes=[rps], signal=(c == 7))
                        rl, rrl = rls[fc % 3]
                        S.op("act", lambda: nc.scalar.activation(out=rl[:, :], in_=ps[:, :], func=AF.Relu), reads=[rps], writes=[rrl])
                        S.op("dve", lambda: nc.vector.tensor_tensor(out=uT[:, fc, :], in0=rl[:, :], in1=rl[:, :], op=ALU.mult),
                             reads=[rrl], writes=[ruT[fc]])
                for sub in range(4):
                    for half in range(2):
                        ps, rps = self.next_ps()
                        for fc in range(32):
                            S.op("pe", lambda: nc.tensor.matmul(ps[:, :], uT[:, fc, sub * 128:(sub + 1) * 128], wd[:, fc, half * 512:(half + 1) * 512],
                                                                start=(fc == 0), stop=(fc == 31)),
                                 reads=[ruT[fc], rwd], writes=[rps], signal=(fc == 31))
                        S.op("dve", lambda: nc.vector.tensor_tensor(out=xt[:, sub, half * 512:(half + 1) * 512], in0=ps[:, :],
                                                                    in1=xt[:, sub, half * 512:(half + 1) * 512], op=ALU.add),
                             reads=[rps, rxt[sub]], writes=[rxt[sub]])
                if not last:
                    S.dma("sp", self.xres[t0:t0 + 512, :].rearrange("(s p) d -> p s d", p=128), xt[:],
                          reads=rxt, writes=[self.r_x[ti]])
                else:
                    for sub in range(4):
                        S.op("act", lambda: nc.scalar.activation(out=uT[:, 2 * sub:2 * sub + 2, :], in_=xt[:, sub, :].rearrange("p (a b) -> p a b", b=512),
                                                                 func=AF.Square, scale=1.0 / 32.0, accum_out=ss[:, 4 + sub:5 + sub]),
                             reads=[rxt[sub]], writes=[rss, ruT[2 * sub], ruT[2 * sub + 1]])
                    self.rstd_from_acc(ss, rss, rs, rrs, 4, 4)
                    for sub in range(4):
                        S.op("dve", lambda: nc.vector.scalar_tensor_tensor(out=xt[:, sub, :], in0=xt[:, sub, :], scalar=rs[:, 4 + sub:5 + sub],
                                                                           in1=nf[:, :], op0=ALU.mult, op1=ALU.mult),
                             reads=[rxt[sub], rrs, rnf], writes=[rxt[sub]])
                    S.dma("sp", self.y_out[t0:t0 + 512, :].rearrange("(s p) d -> p s d", p=128), xt[:], reads=rxt)
            S.barrier()


_PROG_CACHE = {}


def get_prog(seqs, depth):
    key = (tuple(seqs), depth)
    if key not in _PROG_CACHE:
        p = Prog(seqs, depth)
        p.build()
        _PROG_CACHE[key] = p
    return _PROG_CACHE[key]


def run_cores(x_list, shared, seqs, depth, core_ids):
    prog = get_prog(seqs, depth)
    shared = dict(shared)
    shared["rope"] = _rope_tables(max(seqs))
    in_maps = []
    for x in x_list:
        m = dict(shared)
        m["x"] = np.ascontiguousarray(x, dtype=np.float32)
        in_maps.append(m)
    res = run_bass_kernel_spmd(prog.nc, in_maps, core_ids=core_ids)
    return [r["y"] for r in res.results]


def kernel(**inputs):
    xp = np.asarray(inputs["x_prompt"], np.float32)
    xs = np.asarray(inputs["x_sample"], np.float32)
    shared = host_prep(inputs, DEPTH)
    seqs = [2048, 2048, 16384]
    x_list = []
    zeros = np.zeros_like(xs[0])
    for c in range(N_CORES):
        samp = xs[c // 4] if c % 4 == 0 else zeros
        x_list.append(np.concatenate([xp[2 * c], xp[2 * c + 1], samp], axis=0))
    ys = run_cores(x_list, shared, seqs, DEPTH, list(range(N_CORES)))
    y_prompt = np.empty_like(xp)
    y_sample = np.empty_like(xs)
    for c in range(N_CORES):
        y_prompt[2 * c] = ys[c][0:2048]
        y_prompt[2 * c + 1] = ys[c][2048:4096]
    y_sample[0] = ys[0][4096:]
    y_sample[1] = ys[4][4096:]
    return (y_prompt, y_sample)
```

```python
import contextlib
import numpy as np
import concourse.bass as bass
import concourse.mybir as mybir
from concourse.bass_utils import run_bass_kernel_spmd

F32 = mybir.dt.float32
BF16 = mybir.dt.bfloat16
AF = mybir.ActivationFunctionType
ALU = mybir.AluOpType

D_MODEL = 1024
DEPTH = 4
D_FF = 4096
EPS = 1e-5
N_CORES = 8
FM_COLS = 1408
TM0 = 1408
KR0 = 2432
WIN_COLS = 2624
NPCOL = 40
PC_NATT, PC_NMLP, PC_GN, PC_QN, PC_KVN, PC_SINK = 0, 8, 16, 24, 26, 27
MA_LEN, MB_LEN, MD_ARR = 2944, 1152, 1408
MD_LEN = MD_ARR + 12 * 512
MASK_MAX = MD_LEN
ROW_AQ, ROW_AK, ROW_BQ, ROW_BK, ROW_DQ, ROW_DK = 0, 256, 512, 768, 896, 1152
NVH = 14
import os as _os
DUMMY_COLS = int(_os.environ.get("KDUM", "0"))
POOL_MASK = int(_os.environ.get("KPOOL", "0"))


class _Stop(Exception):
    pass


class Res:
    __slots__ = ("w", "r")

    def __init__(self):
        self.w = {}
        self.r = {}


class Sched:
    def __init__(self, nc, stack, n_dma_sems=12):
        self.nc = nc
        self.eng = {"pe": nc.tensor, "act": nc.scalar, "dve": nc.vector, "pool": nc.gpsimd, "sp": nc.sync}
        self.names = list(self.eng.keys())
        self.sems = {}
        for n in self.names:
            self.sems[n] = stack.enter_context(nc.semaphore("s_" + n))
        self.cnt = {n: 0 for n in self.names}
        self.dq = {}
        for q in ("sp", "pool"):
            lst = []
            for i in range(n_dma_sems):
                key = "d_%s_%d" % (q, i)
                self.sems[key] = stack.enter_context(nc.semaphore(key))
                self.cnt[key] = 0
                lst.append(key)
            self.dq[q] = [lst, 0]
        self.waited = {n: {} for n in self.names}
        self.pending = {n: [] for n in self.names}
        self.nops = 0

    def _needs(self, X, reads, writes):
        need = {}
        for r in reads:
            for k, v in r.w.items():
                if need.get(k, 0) < v:
                    need[k] = v
        for w in writes:
            for k, v in w.w.items():
                if need.get(k, 0) < v:
                    need[k] = v
            for k, v in w.r.items():
                if need.get(k, 0) < v:
                    need[k] = v
        return need

    def _emit_waits(self, X, need):
        wd = self.waited[X]
        e = self.eng[X]
        for k, v in need.items():
            if k == X and X == "pe":
                continue
            if wd.get(k, 0) < v:
                e.wait_ge(self.sems[k], v)
                wd[k] = v

    def _commit(self, ev_key, ev_val, reads, writes):
        for r in reads:
            if r.r.get(ev_key, 0) < ev_val:
                r.r[ev_key] = ev_val
        for w in writes:
            w.w = {ev_key: ev_val}
            w.r = {}

    def op(self, X, fn, reads=(), writes=(), signal=True):
        need = self._needs(X, reads, writes)
        self._emit_waits(X, need)
        inst = fn()
        self.nops += 1
        if signal:
            self.cnt[X] += 1
            inst.then_inc(self.sems[X], 1)
            c = self.cnt[X]
            for (rr, ww) in self.pending[X]:
                self._commit(X, c, rr, ww)
            self.pending[X] = []
            self._commit(X, c, reads, writes)
        else:
            self.pending[X].append((reads, writes))
        return inst

    def dma(self, q, out, in_, reads=(), writes=(), **kw):
        assert not self.pending[q]
        need = self._needs(q, reads, writes)
        lst, idx = self.dq[q]
        key = lst[idx % len(lst)]
        self.dq[q][1] = idx + 1
        if self.cnt[key] > 0:
            need[key] = max(need.get(key, 0), self.cnt[key])
        self._emit_waits(q, need)
        inst = self.eng[q].dma_start(out=out, in_=in_, **kw)
        self.nops += 1
        self.cnt[key] += 16
        inst.then_inc(self.sems[key], 16)
        self._commit(key, self.cnt[key], reads, writes)
        return inst

    def barrier(self):
        for n in self.names:
            assert not self.pending[n], n
        allev = {k: v for k, v in self.cnt.items() if v > 0}
        for n in self.names:
            need = {k: v for k, v in allev.items() if k != n}
            wd = self.waited[n]
            for k, v in need.items():
                if wd.get(k, 0) < v:
                    self.eng[n].wait_ge(self.sems[k], v)
                    wd[k] = v

    def final_wait(self):
        self.barrier()


def _alibi_slopes():
    s = np.exp2(-8.0 * np.arange(1, 9, dtype=np.float64) / 8.0)
    return s[1::2], s[0::2]


def _mask_a():
    sa, _ = _alibi_slopes()
    d = np.arange(-4096, 4097)
    ad = np.abs(d)
    m = ((ad <= 64).astype(np.float64) + ((d % 4 == 0) & (ad <= 256)) + ((d % 16 == 0) & (ad <= 1024)))
    out = np.zeros((4, 128, MA_LEN), np.float32)
    i = np.arange(128)[:, None]
    n = np.arange(MA_LEN)[None, :]
    dd = i + 1408 - n
    for h in range(4):
        f = m * np.exp(-sa[h] * ad)
        out[h] = f[dd + 4096]
    return out


def _mask_b():
    _, sb = _alibi_slopes()
    out = np.zeros((4, 128, MB_LEN), np.float32)
    i = np.arange(128)[:, None]
    n = np.arange(MB_LEN)[None, :]
    dd = i + 512 - n
    ad = np.abs(dd)
    for h in range(4):
        out[h] = np.where(ad <= 128, np.exp(-sb[h] * ad), 0.0)
    return out


def _d_tables():
    i = np.arange(128)
    rk_l = (i // 64)[:, None, None]
    ck = (i % 64)[:, None, None]
    n = np.arange(22)[None, :, None]
    cq = np.arange(64)[None, None, :]
    delta = rk_l + 10 - n + 0 * cq
    dr_idx = np.clip(delta, -7, 7) + 7
    dc_idx = np.clip(ck - cq, -15, 15) + 15 + 0 * n
    cs = np.clip(cq - 8, 0, 64 - 16)
    colmask = ((ck >= cs) & (ck < cs + 16)) & (n >= 0)
    cw = (colmask & (delta >= -4) & (delta <= 3)).astype(np.float32).reshape(128, MD_ARR)
    cf = (colmask & (delta >= -7) & (delta <= 7)).astype(np.float32).reshape(128, MD_ARR)
    vb = np.zeros((12, 128, 512), np.float32)
    rkl = (i // 64)[:, None]
    rql = (np.arange(512) // 64)[None, :]
    for kt in range(6):
        rk = 2 * kt + rkl
        rs = np.maximum(rql - 4, 0)
        vb[kt] = ((rk >= rs) & (rk < rs + 8)).astype(np.float32)
        rk2 = -4 + 2 * kt + rkl
        rs2 = np.minimum(rql - 4, 0)
        vb[6 + kt] = ((rk2 >= rs2) & (rk2 < rs2 + 8)).astype(np.float32)
    return dr_idx.reshape(128, MD_ARR), dc_idx.reshape(128, MD_ARR), cw, cf, vb


def _rope_tables(smax):
    inv_freq = (1.0 / (np.float32(10000.0) ** (np.arange(0, 32, 2, dtype=np.float32) / np.float32(32)))).astype(np.float32)
    ang = (np.arange(smax, dtype=np.float32)[:, None] * inv_freq[None, :]).astype(np.float32)
    c = np.cos(ang.astype(np.float64)).astype(np.float32).T
    s = np.sin(ang.astype(np.float64)).astype(np.float32).T
    out = np.zeros((2, 32, smax), np.float32)
    out[0, :16] = c
    out[0, 16:] = c
    out[1, :16] = -s
    out[1, 16:] = s
    return out


def host_prep(inp, depth):
    L = depth
    w_in = np.asarray(inp["w_in"], np.float32)[:L]
    sec = np.cumsum([0, 256, 256, 256, 256, 128, 128, 256, 128, 32, 256, 256, 256])
    a_q, a_k, a_v, b_q, b_k, b_v, c_q, c_kv, c_kr, d_q, d_k, d_v = [w_in[:, :, sec[i]:sec[i + 1]] for i in range(12)]
    z64 = np.zeros((L, 1024, 64), np.float32)
    c_kr_sw = np.concatenate([c_kr[:, :, 16:32], c_kr[:, :, 0:16]], axis=-1)
    w_in_p = np.concatenate([a_q, a_k, b_q, b_k, d_q, d_k, a_v, b_v, d_v, c_q, c_kv, z64, c_kr, z64, c_kr_sw], axis=-1)
    assert w_in_p.shape[-1] == WIN_COLS
    wq = np.asarray(inp["w_q_up"], np.float32)[:L].reshape(L, 256, 4, 96)
    z = np.zeros((L, 256, 4, 64), np.float32)
    pe = wq[..., 64:96]
    pe_sw = np.concatenate([pe[..., 16:32], pe[..., 0:16]], axis=-1)
    wq_p = np.concatenate([wq, z, pe_sw], axis=-1).reshape(L, 256, 768)
    wkv = np.asarray(inp["w_kv_up"], np.float32)[:L].reshape(L, 128, 4, 128)
    wkv_p = np.concatenate([wkv[..., :64].reshape(L, 128, 256), wkv[..., 64:].reshape(L, 128, 256)], axis=-1)
    pcol = np.zeros((L, 128, NPCOL), np.float32)
    pcol[:, :, PC_NATT:PC_NATT + 8] = np.asarray(inp["norm_attn"], np.float32)[:L].reshape(L, 8, 128).transpose(0, 2, 1)
    pcol[:, :, PC_NMLP:PC_NMLP + 8] = np.asarray(inp["norm_mlp"], np.float32)[:L].reshape(L, 8, 128).transpose(0, 2, 1)
    pcol[:, :, PC_GN:PC_GN + 8] = np.asarray(inp["group_norm"], np.float32)[:L].reshape(L, 8, 128).transpose(0, 2, 1)
    pcol[:, :, PC_QN:PC_QN + 2] = np.asarray(inp["mla_q_norm"], np.float32)[:L].reshape(L, 2, 128).transpose(0, 2, 1)
    pcol[:, :, PC_KVN] = np.asarray(inp["mla_kv_norm"], np.float32)[:L]
    pcol[:, :, PC_SINK:PC_SINK + 4] = np.asarray(inp["sink_logits"], np.float32)[:L][:, None, :]
    nfin = np.broadcast_to(np.asarray(inp["norm_final"], np.float32)[None, :], (128, 1024)).copy()
    dr_idx, dc_idx, cw, cf, vb = _d_tables()
    rpb = np.asarray(inp["na_rpb"], np.float32)[:L]
    biasf = rpb[:, :, dr_idx, dc_idx]
    return {
        "w_in_p": np.ascontiguousarray(w_in_p),
        "w_q_p": np.ascontiguousarray(wq_p),
        "w_kv_p": np.ascontiguousarray(wkv_p),
        "w_out": np.ascontiguousarray(np.asarray(inp["w_out"], np.float32)[:L]),
        "w_up": np.ascontiguousarray(np.asarray(inp["w_mlp_up"], np.float32)[:L]),
        "w_down": np.ascontiguousarray(np.asarray(inp["w_mlp_down"], np.float32)[:L]),
        "pcol": pcol,
        "nfin": nfin,
        "ident": np.eye(128, dtype=np.float32),
        "mask_a": _mask_a(),
        "mask_b": _mask_b(),
        "biasf": np.ascontiguousarray(biasf),
        "dconst": np.ascontiguousarray(np.concatenate([cw, cf, vb.transpose(1, 0, 2).reshape(128, 12 * 512)], axis=1)),
    }


class Prog:
    def __init__(self, seqs, depth):
        self.seqs = list(seqs)
        self.L = depth
        self.T = sum(seqs)
        self.smax = max(seqs)
        self.starts = [int(x) for x in np.cumsum([0] + self.seqs[:-1])]

    def sb(self, st, name, shape, dt, n=1):
        out = []
        for i in range(n):
            t = st.enter_context(self.nc.sbuf_tensor("%s_%d_%d" % (name, self.uid, i), shape, dt))
            out.append((t, Res()))
        self.uid += 1
        return out

    def build(self):
        nc = bass.Bass("TRN2", target_bir_lowering=False)
        self.nc = nc
        self.uid = 0
        L, T = self.L, self.T

        def din(name, shape):
            return nc.dram_tensor(name, shape, F32, kind="ExternalInput").ap()

        def dscr(name, shape, dt):
            return nc.dram_tensor(name, shape, dt, kind="Internal").ap()

        self.x_in = din("x", [T, 1024])
        self.w_in_p = din("w_in_p", [L, 1024, WIN_COLS])
        self.w_q_p = din("w_q_p", [L, 256, 768])
        self.w_kv_p = din("w_kv_p", [L, 128, 512])
        self.w_out = din("w_out", [L, 1024, 1024])
        self.w_up = din("w_up", [L, 1024, 4096])
        self.w_down = din("w_down", [L, 4096, 1024])
        self.pcol_d = din("pcol", [L, 128, NPCOL])
        self.nfin_d = din("nfin", [128, 1024])
        self.ident_d = din("ident", [128, 128])
        self.mask_a_d = din("mask_a", [4, 128, MA_LEN])
        self.mask_b_d = din("mask_b", [4, 128, MB_LEN])
        self.biasf_d = din("biasf", [L, 4, 128, MD_ARR])
        self.dconst_d = din("dconst", [128, 2 * MD_ARR + 12 * 512])
        self.rope_d = din("rope", [2, 32, self.smax])
        self.y_out = nc.dram_tensor("y", [T, 1024], F32, kind="ExternalOutput").ap()

        self.xres = dscr("xres", [T, 1024], F32)
        self.win_b = dscr("win_b", [L, 128, 8, WIN_COLS], BF16)
        self.wq_b = dscr("wq_b", [L, 128, 2, 768], BF16)
        self.wkv_b = dscr("wkv_b", [L, 128, 512], BF16)
        self.wout_b = dscr("wout_b", [L, 128, 8, 1024], BF16)
        self.wup_b = dscr("wup_b", [L, 8, 128, 8, 512], BF16)
        self.wdown_b = dscr("wdown_b", [L, 128, 32, 1024], BF16)
        self.maska_b = dscr("maska_b", [4, 128, MA_LEN], BF16)
        self.maskb_b = dscr("maskb_b", [4, 128, MB_LEN], BF16)
        self.maskd_b = dscr("maskd_b", [4, 128, MD_LEN], BF16)
        self.qkT = dscr("qkT", [FM_COLS, T], BF16)
        self.qTc = dscr("qTc", [4, 96, T], BF16)
        self.kTc = dscr("kTc", [4, 96, T], BF16)
        self.vsc = dscr("vsc", [T, NVH, 65], BF16)
        self.onT = dscr("onT", [1024, T], BF16)
        nt = T // 512
        self.r_x = [Res() for _ in range(nt)]
        self.r_qk = [Res() for _ in range(nt)]
        self.r_on = [Res() for _ in range(nt)]
        self.r_qc = [Res() for _ in range(nt)]
        self.r_kc = [Res() for _ in range(nt)]
        self.r_v = [Res() for _ in range(nt)]
        self.r_w = Res()
        self.r_maskd = Res()

        with contextlib.ExitStack() as top:
            self.S = Sched(nc, top)
            cst = self.sb(top, "identf", [128, 128], F32)[0]
            self.identf, self.r_identf = cst
            cst = self.sb(top, "identb", [128, 128], BF16)[0]
            self.identb, self.r_identb = cst
            cst = self.sb(top, "neghalf", [128, 8], F32)[0]
            self.neghalf, self.r_neghalf = cst
            S = self.S
            S.dma("sp", self.identf[:], self.ident_d[:, :], writes=[self.r_identf])
            S.op("dve", lambda: nc.vector.tensor_copy(out=self.identb[:], in_=self.identf[:]),
                 reads=[self.r_identf], writes=[self.r_identb])
            S.op("dve", lambda: nc.vector.memset(self.neghalf[:], -0.5), writes=[self.r_neghalf])
            import os
            kstop = int(os.environ.get("KSTOP", "99"))
            if kstop >= 1:
                self.prologue()
            for l in range(L):
                if kstop >= 2:
                    self.phase_dmask(l)
                if kstop >= 3:
                    self.phase1(l)
                if kstop >= 4:
                    self.phase2(l)
                if kstop >= 5:
                    self.phase3(l)
            S.final_wait()
        return nc

    def alloc_psum(self, st, nps, npb):
        nc = self.nc
        self.ps = [(st.enter_context(nc.psum_tensor("ps%d_%d" % (i, self.uid), [128, 512], F32)), Res()) for i in range(nps)]
        self.pb = [(st.enter_context(nc.psum_tensor("pb%d_%d" % (i, self.uid), [128, 1024], BF16)), Res()) for i in range(npb)]
        self.uid += 1
        self.ps_i = 0
        self.pb_i = 0

    def next_ps(self):
        p = self.ps[self.ps_i % len(self.ps)]
        self.ps_i += 1
        return p

    def next_pb(self):
        p = self.pb[self.pb_i % len(self.pb)]
        self.pb_i += 1
        return p

    def prologue(self):
        nc, S, L = self.nc, self.S, self.L
        with contextlib.ExitStack() as st:
            stg = self.sb(st, "cv_f", [128, 4096], F32, 3)
            stb = self.sb(st, "cv_b", [128, 4096], BF16, 3)
            k = [0]

            def conv(src, dst, shape):
                n = int(np.prod(shape))
                assert n <= 4096
                (f, rf), (b, rb) = stg[k[0] % 3], stb[k[0] % 3]
                if len(shape) == 1:
                    fv, bv = f[:, 0:n], b[:, 0:n]
                else:
                    fv = f[:, 0:n].rearrange("p (a b) -> p a b", b=shape[1])
                    bv = b[:, 0:n].rearrange("p (a b) -> p a b", b=shape[1])
                S.dma("sp", fv, src, writes=[rf])
                if k[0] % 2 == 0:
                    S.op("dve", lambda: nc.vector.tensor_copy(out=b[:, 0:n], in_=f[:, 0:n]), reads=[rf], writes=[rb])
                else:
                    S.op("act", lambda: nc.scalar.copy(out=b[:, 0:n], in_=f[:, 0:n]), reads=[rf], writes=[rb])
                S.dma("pool", dst, bv, reads=[rb])
                k[0] += 1

            for l in range(L):
                wv = self.w_in_p[l].rearrange("(c p) n -> p c n", p=128)
                for c in range(8):
                    conv(wv[:, c, :], self.win_b[l, :, c, :], [WIN_COLS])
                conv(self.w_q_p[l].rearrange("(c p) n -> p c n", p=128), self.wq_b[l], [2, 768])
                conv(self.w_kv_p[l], self.wkv_b[l], [512])
                wv = self.w_out[l].rearrange("(c p) n -> p c n", p=128)
                for c0 in range(0, 8, 4):
                    conv(wv[:, c0:c0 + 4, :], self.wout_b[l, :, c0:c0 + 4, :], [4, 1024])
                wv = self.w_up[l].rearrange("(c p) n -> p c n", p=128)
                for g in range(8):
                    conv(wv[:, :, g * 512:(g + 1) * 512], self.wup_b[l, g], [8, 512])
                wv = self.w_down[l].rearrange("(f p) n -> p f n", p=128)
                for f0 in range(0, 32, 4):
                    conv(wv[:, f0:f0 + 4, :], self.wdown_b[l, :, f0:f0 + 4, :], [4, 1024])
            for h in range(4):
                conv(self.mask_a_d[h], self.maska_b[h], [MA_LEN])
                conv(self.mask_b_d[h], self.maskb_b[h], [MB_LEN])
            S.barrier()

    def phase_dmask(self, l):
        nc, S = self.nc, self.S
        with contextlib.ExitStack() as st:
            (dc, rdc) = self.sb(st, "dm_c", [128, 2 * MD_ARR + 12 * 512], F32)[0]
            bf = self.sb(st, "dm_bf", [128, MD_ARR], F32, 2)
            ex = self.sb(st, "dm_ex", [128, MD_ARR], F32, 2)
            full = self.sb(st, "dm_full", [128, MD_ARR], F32, 2)
            mo = self.sb(st, "dm_out", [128, MD_LEN], BF16, 2)
            S.dma("sp", dc[:], self.dconst_d[:, :], writes=[rdc])
            for h in range(4):
                (b, rb), (e, re), (fu, rfu), (m, rm) = bf[h % 2], ex[h % 2], full[h % 2], mo[h % 2]
                S.dma("sp", b[:], self.biasf_d[l, h], writes=[rb])
                S.op("act", lambda: nc.scalar.activation(out=e[:], in_=b[:], func=AF.Exp), reads=[rb], writes=[re])
                S.op("dve", lambda: nc.vector.tensor_tensor(out=m[:, 0:MD_ARR], in0=e[:], in1=dc[:, 0:MD_ARR], op=ALU.mult),
                     reads=[re, rdc], writes=[rm])
                S.op("dve", lambda: nc.vector.tensor_tensor(out=fu[:], in0=e[:], in1=dc[:, MD_ARR:2 * MD_ARR], op=ALU.mult),
                     reads=[re, rdc], writes=[rfu])
                for kt in range(6):
                    for (which, dr) in ((0, 2 * kt), (1, -4 + 2 * kt)):
                        o0 = MD_ARR + (which * 6 + kt) * 512
                        a0 = (10 - dr) * 64
                        v0 = 2 * MD_ARR + (which * 6 + kt) * 512
                        S.op("dve", lambda o0=o0, a0=a0, v0=v0: nc.vector.tensor_tensor(
                            out=m[:, o0:o0 + 512], in0=fu[:, a0:a0 + 512], in1=dc[:, v0:v0 + 512], op=ALU.mult),
                            reads=[rfu, rdc], writes=[rm])
                S.dma("pool", self.maskd_b[h], m[:], reads=[rm])
            S.barrier()

    def sbm(self, st, name, shape, dt, n, nres):
        out = []
        for i in range(n):
            t = st.enter_context(self.nc.sbuf_tensor("%s_%d_%d" % (name, self.uid, i), shape, dt))
            out.append((t, [Res() for _ in range(nres)]))
        self.uid += 1
        return out

    def rstd_from_acc(self, acc, racc, out, rout, c0, n):
        nc, S = self.nc, self.S
        S.op("dve", lambda: nc.vector.tensor_scalar(out=out[:, c0:c0 + n], in0=acc[:, c0:c0 + n], scalar1=EPS, scalar2=None, op0=ALU.add),
             reads=[racc], writes=[rout])
        S.op("pool", lambda: nc.gpsimd.tensor_tensor(out=out[:, c0:c0 + n], in0=out[:, c0:c0 + n], in1=self.neghalf[:, 0:n], op=ALU.pow),
             reads=[rout, self.r_neghalf], writes=[rout])

    def evac(self, i, out, in_, reads, writes, scale=None):
        nc, S = self.nc, self.S
        import os
        kev = os.environ.get("KEV")
        if kev == "0":
            return
        if kev == "1":
            i = 1
        if kev == "2":
            i = 0
        if kev == "3" and scale is None:
            i = 1
        if kev == "4":
            i = 1 if scale is not None else 0
        if i % 2 == 0:
            if scale is None:
                S.op("act", lambda: nc.scalar.activation(out=out, in_=in_, func=AF.Copy), reads=reads, writes=writes)
            else:
                S.op("act", lambda: nc.scalar.activation(out=out, in_=in_, func=AF.Copy, scale=scale), reads=reads, writes=writes)
        else:
            if scale is None:
                S.op("dve", lambda: nc.vector.tensor_copy(out=out, in_=in_), reads=reads, writes=writes)
            else:
                S.op("dve", lambda: nc.vector.tensor_scalar(out=out, in0=in_, scalar1=scale, scalar2=None, op0=ALU.mult),
                     reads=reads, writes=writes)

    def norm_front(self, xt, rxt, xn, rxn, ss, rss, rs, rrs, D=1024):
        nc, S = self.nc, self.S
        sc = float(D) ** -0.5
        for sub in range(4):
            S.op("act", lambda: nc.scalar.activation(out=xn[:, sub, :], in_=xt[:, sub, :], func=AF.Square, scale=sc,
                                                     accum_out=ss[:, sub:sub + 1]),
                 reads=[rxt[sub]], writes=[rss, rxn[sub]])
        self.rstd_from_acc(ss, rss, rs, rrs, 0, 4)
        for sub in range(4):
            S.op("dve", lambda: nc.vector.tensor_scalar(out=xn[:, sub, :], in0=xt[:, sub, :], scalar1=rs[:, sub:sub + 1],
                                                        scalar2=None, op0=ALU.mult),
                 reads=[rxt[sub], rrs], writes=[rxn[sub]])

    def norm_tpose(self, xn, rxn, hT, rhT, gain, rgain, D=1024):
        nc, S = self.nc, self.S
        for c in range(D // 128):
            pb, rpb = self.next_pb()
            for sub in range(4):
                S.op("pe", lambda: nc.tensor.transpose(out=pb[:, sub * 128:(sub + 1) * 128], in_=xn[:, sub, c * 128:(c + 1) * 128],
                                                       identity=self.identb[:]),
                     reads=[rxn[sub], self.r_identb], writes=[rpb], signal=(sub == 3))
            self.evac(c, hT[:, c, :], pb[:, 0:512], [rpb, rgain], [rhT[c]], scale=gain[:, c:c + 1])

    def chk(self, n):
        import os
        if int(os.environ.get('KSUB', '99')) < n:
            raise _Stop()

    def tiles(self):
        out = []
        for si, (s0, sl) in enumerate(zip(self.starts, self.seqs)):
            for t0 in range(s0, s0 + sl, 512):
                out.append((si, s0, t0))
        return out

    def phase1(self, l):
        nc, S = self.nc, self.S
        x_src = self.x_in if l == 0 else self.xres
        with contextlib.ExitStack() as st:
            self.alloc_psum(st, 6, 2)
            (win, rwin) = self.sb(st, "p1_win", [128, 8, WIN_COLS], BF16)[0]
            (wq, rwq) = self.sb(st, "p1_wq", [128, 2, 768], BF16)[0]
            (wkv, rwkv) = self.sb(st, "p1_wkv", [128, 512], BF16)[0]
            (pc, rpc) = self.sb(st, "p1_pc", [128, NPCOL], F32)[0]
            xts = self.sbm(st, "p1_xt", [128, 4, 1024], F32, 2, 4)
            xns = self.sbm(st, "p1_xn", [128, 4, 1024], BF16, 2, 4)
            hTs = self.sbm(st, "p1_hT", [128, 8, 512], BF16, 2, 8)
            junk = None
            (junk2, _) = self.sb(st, "p1_junk2", [128, 256], BF16)[0]
            sss = self.sb(st, "p1_ss", [128, 8], F32, 2)
            rss_ = self.sb(st, "p1_rs", [128, 8], F32, 2)
            qks = self.sbm(st, "p1_qk", [128, 11, 512], BF16, 2, 11)
            vss = self.sbm(st, "p1_vs", [128, 4, NVH, 65], BF16, 2, 12)
            cns = self.sbm(st, "p1_cn", [128, 4, 384], BF16, 2, 8)
            cqTs = self.sbm(st, "p1_cqT", [128, 3, 512], BF16, 2, 3)
            ssc = self.sbm(st, "p1_ssc", [128, 8], F32, 2, 4)
            rsc = self.sbm(st, "p1_rsc", [128, 8], F32, 2, 4)
            rts = self.sb(st, "p1_rt", [128, 2, 512], F32, 2)
            tmps = self.sb(st, "p1_tmp", [128, 512], F32, 2)
            kpes = self.sb(st, "p1_kpe", [128, 512], F32, 2)
            qTs = self.sbm(st, "p1_qT", [128, 4, 512], BF16, 2, 8)
            kTs = self.sbm(st, "p1_kT", [128, 4, 512], BF16, 2, 8)

            S.dma("sp", win[:], self.win_b[l], writes=[rwin])
            S.dma("sp", wq[:], self.wq_b[l], writes=[rwq])
            S.dma("sp", wkv[:], self.wkv_b[l], writes=[rwkv])
            S.dma("sp", pc[:], self.pcol_d[l], writes=[rpc])
            for (v, rv) in vss:
                S.op("dve", lambda: nc.vector.memset(v[:], 1.0), writes=rv)

            tl = self.tiles()

            def load_x(i):
                si, s0, t0 = tl[i]
                xt, rxt = xts[i % 2]
                S.dma("sp", xt[:], x_src[t0:t0 + 512, :].rearrange("(s p) d -> p s d", p=128),
                      reads=[self.r_x[t0 // 512]], writes=rxt)

            def load_rt(i):
                si, s0, t0 = tl[i]
                rt, rrt = rts[i % 2]
                p0 = t0 - s0
                S.dma("sp", rt[64:96, :, :], self.rope_d[:, :, p0:p0 + 512].rearrange("a r t -> r a t"), writes=[rrt])

            def front(i):
                xt, rxt = xts[i % 2]
                xn, rxn = xns[i % 2]
                ss, rss = sss[i % 2]
                rs, rrs = rss_[i % 2]
                self.norm_front(xt, rxt, xn, rxn, ss, rss, rs, rrs)

            load_x(0)
            load_rt(0)
            if len(tl) > 1:
                load_x(1)
            front(0)
            tk = 0
            for i, (si, s0, t0) in enumerate(tl):
                if i + 1 < len(tl):
                    load_rt(i + 1)
                b = i % 2
                xt, rxt = xts[b]
                xn, rxn = xns[b]
                hT, rhT = hTs[b]
                ss, rss = sss[b]
                rs, rrs = rss_[b]
                qk, rqk = qks[b]
                vs, rvs = vss[b]
                cn, rcn = cns[b]
                cqT, rcqT = cqTs[b]
                sc_, rsc_ = ssc[b]
                rc_, rrc_ = rsc[b]
                rt, rrt = rts[b]
                qT, rqT = qTs[b]
                kT, rkT = kTs[b]
                ti = t0 // 512
                try:
                  self.phase1_tile(locals())
                except _Stop:
                  break
            S.barrier()

    def phase1_tile(self, L_):
                nc, S = self.nc, self.S
                globals_ = L_
                (xt, rxt, xn, rxn, ss, rss, rs, rrs, junk, hT, rhT, pc, rpc, win, rwin, qk, rqk, t0, ti, vs, rvs, junk2, sc_, rsc_, rc_, rrc_, cn, rcn, cqT, rcqT, tmps, kpes, rt, rrt, kT, rkT, qT, rqT, wq, rwq, wkv, rwkv) = [L_[k] for k in 'xt rxt xn rxn ss rss rs rrs junk hT rhT pc rpc win rwin qk rqk t0 ti vs rvs junk2 sc_ rsc_ rc_ rrc_ cn rcn cqT rcqT tmps kpes rt rrt kT rkT qT rqT wq rwq wkv rwkv'.split()]
                tk = 0
                self.chk(1)
                self.norm_tpose(xn, rxn, hT, rhT, pc[:, PC_NATT:PC_NATT + 8], rpc)
                if L_["i"] + 1 < len(L_["tl"]):
                    L_["front"](L_["i"] + 1)
                if L_["i"] + 2 < len(L_["tl"]):
                    L_["load_x"](L_["i"] + 2)
                self.chk(2)
                for j in range(11):
                    ps, rps = self.next_ps()
                    for kc in range(8):
                        S.op("pe", lambda: nc.tensor.matmul(ps[:, :], win[:, kc, j * 128:(j + 1) * 128], hT[:, kc, :],
                                                            start=(kc == 0), stop=(kc == 7)),
                             reads=[rwin, rhT[kc]], writes=[rps], signal=(kc == 7))
                    self.evac(j, qk[:, j, :], ps[:, :], [rps], [rqk[j]])
                import os
                if not os.environ.get("KNOSTORE"):
                    S.dma("sp", self.qkT[:, t0:t0 + 512].rearrange("(j p) t -> p j t", p=128), qk[:, :, :],
                          reads=rqk, writes=[self.r_qk[ti]])
                self.chk(3)
                for sub in range(4):
                    psA, rpsA = self.next_ps()
                    for kc in range(8):
                        S.op("pe", lambda: nc.tensor.matmul(psA[:, :], hT[:, kc, sub * 128:(sub + 1) * 128], win[:, kc, TM0:TM0 + 512],
                                                            start=(kc == 0), stop=(kc == 7)),
                             reads=[rwin, rhT[kc]], writes=[rpsA], signal=(kc == 7))
                    self.evac(0, vs[:, sub, 0:8, 0:64], psA[:, :].rearrange("p (h d) -> p h d", d=64), [rpsA], [rvs[sub * 3]])
                    psB, rpsB = self.next_ps()
                    for kc in range(8):
                        S.op("pe", lambda: nc.tensor.matmul(psB[:, :], hT[:, kc, sub * 128:(sub + 1) * 128], win[:, kc, TM0 + 512:TM0 + 1024],
                                                            start=(kc == 0), stop=(kc == 7)),
                             reads=[rwin, rhT[kc]], writes=[rpsB], signal=(kc == 7))
                    self.evac(0, vs[:, sub, 8:10, 0:64], psB[:, 0:128].rearrange("p (h d) -> p h d", d=64), [rpsB], [rvs[sub * 3 + 1]])
                    S.op("act", lambda: nc.scalar.activation(out=junk2[:, 0:256], in_=psB[:, 128:384], func=AF.Square, scale=1.0 / 16.0,
                                                             accum_out=sc_[:, 2 * sub:2 * sub + 1]),
                         reads=[rpsB], writes=[rsc_[sub]])
                    S.op("act", lambda: nc.scalar.activation(out=junk2[:, 0:128], in_=psB[:, 384:512], func=AF.Square, scale=128.0 ** -0.5,
                                                             accum_out=sc_[:, 2 * sub + 1:2 * sub + 2]),
                         reads=[rpsB], writes=[rsc_[sub]])
                    self.rstd_from_acc(sc_, rsc_[sub], rc_, rrc_[sub], 2 * sub, 2)
                    S.op("dve", lambda: nc.vector.tensor_scalar(out=cn[:, sub, 0:256], in0=psB[:, 128:384], scalar1=rc_[:, 2 * sub:2 * sub + 1],
                                                                scalar2=None, op0=ALU.mult),
                         reads=[rpsB, rrc_[sub]], writes=[rcn[2 * sub]])
                    S.op("dve", lambda: nc.vector.tensor_scalar(out=cn[:, sub, 256:384], in0=psB[:, 384:512], scalar1=rc_[:, 2 * sub + 1:2 * sub + 2],
                                                                scalar2=None, op0=ALU.mult),
                         reads=[rpsB, rrc_[sub]], writes=[rcn[2 * sub + 1]])
                self.chk(4)
                for c in range(3):
                    pb, rpb = self.next_pb()
                    for sub in range(4):
                        S.op("pe", lambda: nc.tensor.transpose(out=pb[:, sub * 128:(sub + 1) * 128], in_=cn[:, sub, c * 128:(c + 1) * 128],
                                                               identity=self.identb[:]),
                             reads=[rcn[2 * sub + (1 if c == 2 else 0)], self.r_identb], writes=[rpb], signal=(sub == 3))
                    self.evac(c, cqT[:, c, :], pb[:, 0:512], [rpb, rpc], [rcqT[c]], scale=pc[:, PC_QN + c:PC_QN + c + 1])
                self.chk(5)
                tmp, rtmp = tmps[tk % 2]
                kpe, rkpe = kpes[tk % 2]
                tk += 1
                ps1, rps1 = self.next_ps()
                for kc in range(8):
                    S.op("pe", lambda: nc.tensor.matmul(ps1[0:96, :], win[:, kc, KR0:KR0 + 96], hT[:, kc, :], start=(kc == 0), stop=(kc == 7)),
                         reads=[rwin, rhT[kc]], writes=[rps1], signal=(kc == 7))
                ps2, rps2 = self.next_ps()
                for kc in range(8):
                    S.op("pe", lambda: nc.tensor.matmul(ps2[0:96, :], win[:, kc, KR0 + 96:KR0 + 192], hT[:, kc, :], start=(kc == 0), stop=(kc == 7)),
                         reads=[rwin, rhT[kc]], writes=[rps2], signal=(kc == 7))
                S.op("dve", lambda: nc.vector.tensor_tensor(out=tmp[64:96, :], in0=ps2[64:96, :], in1=rt[64:96, 1, :], op=ALU.mult),
                     reads=[rps2, rrt], writes=[rtmp])
                S.op("dve", lambda: nc.vector.tensor_tensor(out=kpe[64:96, :], in0=ps1[64:96, :], in1=rt[64:96, 0, :], op=ALU.mult),
                     reads=[rps1, rrt], writes=[rkpe])
                for h in range(4):
                    S.op("pool", lambda: nc.gpsimd.tensor_tensor(out=kT[64:96, h, :], in0=kpe[64:96, :], in1=tmp[64:96, :], op=ALU.add),
                         reads=[rkpe, rtmp], writes=[rkT[2 * h + 1]])
                self.chk(6)
                for h in range(4):
                    tmp, rtmp = tmps[tk % 2]
                    kpe, rkpe = kpes[tk % 2]
                    tk += 1
                    pq1, rpq1 = self.next_ps()
                    for kc in range(2):
                        S.op("pe", lambda: nc.tensor.matmul(pq1[0:96, :], wq[:, kc, h * 192:h * 192 + 96], cqT[:, kc, :], start=(kc == 0), stop=(kc == 1)),
                             reads=[rwq, rcqT[kc]], writes=[rpq1], signal=(kc == 1))
                    pq2, rpq2 = self.next_ps()
                    for kc in range(2):
                        S.op("pe", lambda: nc.tensor.matmul(pq2[0:96, :], wq[:, kc, h * 192 + 96:h * 192 + 192], cqT[:, kc, :], start=(kc == 0), stop=(kc == 1)),
                             reads=[rwq, rcqT[kc]], writes=[rpq2], signal=(kc == 1))
                    S.op("act", lambda: nc.scalar.copy(out=qT[0:64, h, :], in_=pq1[0:64, :]), reads=[rpq1], writes=[rqT[2 * h]])
                    S.op("dve", lambda: nc.vector.tensor_tensor(out=tmp[64:96, :], in0=pq2[64:96, :], in1=rt[64:96, 1, :], op=ALU.mult),
                         reads=[rpq2, rrt], writes=[rtmp])
                    S.op("dve", lambda: nc.vector.tensor_tensor(out=kpe[64:96, :], in0=pq1[64:96, :], in1=rt[64:96, 0, :], op=ALU.mult),
                         reads=[rpq1, rrt], writes=[rkpe])
                    S.op("pool", lambda: nc.gpsimd.tensor_tensor(out=qT[64:96, h, :], in0=kpe[64:96, :], in1=tmp[64:96, :], op=ALU.add),
                         reads=[rkpe, rtmp], writes=[rqT[2 * h + 1]])
                    pk, rpk = self.next_ps()
                    S.op("pe", lambda: nc.tensor.matmul(pk[0:64, :], wkv[:, h * 64:(h + 1) * 64], cqT[:, 2, :], start=True, stop=True),
                         reads=[rwkv, rcqT[2]], writes=[rpk])
                    self.evac(h, kT[0:64, h, :], pk[0:64, :], [rpk], [rkT[2 * h]])
                for sub in range(4):
                    pv, rpv = self.next_ps()
                    S.op("pe", lambda: nc.tensor.matmul(pv[:, 0:256], cqT[:, 2, sub * 128:(sub + 1) * 128], wkv[:, 256:512], start=True, stop=True),
                         reads=[rwkv, rcqT[2]], writes=[rpv])
                    self.evac(0, vs[:, sub, 10:14, 0:64], pv[:, 0:256].rearrange("p (h d) -> p h d", d=64), [rpv], [rvs[sub * 3 + 2]])
                self.chk(7)
                S.dma("sp", self.qTc[:, :, t0:t0 + 512].rearrange("h r t -> r h t"), qT[0:96, :, :], reads=rqT, writes=[self.r_qc[ti]])
                S.dma("sp", self.kTc[:, :, t0:t0 + 512].rearrange("h r t -> r h t"), kT[0:96, :, :], reads=rkT, writes=[self.r_kc[ti]])
                S.dma("sp", self.vsc[t0:t0 + 512, :, :].rearrange("(s p) h e -> p s h e", p=128), vs[:, :, :, :],
                      reads=rvs, writes=[self.r_v[ti]])


    def phase2(self, l):
        nc, S = self.nc, self.S
        SPAN = 2048
        SMAXK = self.smax
        LOOK, EP_DEFER, GN_DEFER = int(_os.environ.get('KLOOK', '3')), 2, 3
        NP = LOOK + 1
        with contextlib.ExitStack() as st:
            scs = [(st.enter_context(nc.psum_tensor("p2sc%d_%d" % (i, self.uid), [128, 1024], F32)), Res()) for i in range(2)]
            self.alloc_psum(st, 3, 1)
            o_banks = self.ps[0:2]
            tp_bank = self.ps[2]
            Ks = self.sb(st, "p2_K", [128, SMAXK], BF16, 2)
            Vs = self.sb(st, "p2_V", [128, SMAXK // 128, 65], BF16, 2)
            Ms = self.sb(st, "p2_M", [128, MASK_MAX], BF16, 2)
            Qs = self.sb(st, "p2_Q", [96, 512], BF16, 3)
            QLs = self.sb(st, "p2_QL", [128, 512], BF16, 3)
            QHs = self.sb(st, "p2_QH", [128, 512], BF16, 3)
            Ps = self.sb(st, "p2_P", [128, 1024], BF16, NP)
            P2s = self.sbm(st, "p2_P2", [128, 1024], BF16, NP, 2)
            osbs = self.sb(st, "p2_osb", [65, 512], F32, 3)
            ogrps = self.sbm(st, "p2_ogrp", [128, SPAN // 128, 256], F32, 2, SPAN // 512)
            (pc, rpc) = self.sb(st, "p2_pc", [128, NPCOL], F32)[0]
            (esink, resink) = self.sb(st, "p2_esink", [128, 4], F32)[0]
            (junk, _) = self.sb(st, "p2_junk", [128, 256], BF16)[0]
            gss = self.sb(st, "p2_gss", [128, 4], F32, 2)
            grs = self.sb(st, "p2_grs", [128, 4], F32, 2)
            recs = self.sb(st, "p2_rec", [128, 4], F32, 3)
            ons = self.sbm(st, "p2_on", [128, 4, 256], BF16, 2, 4)
            onTs = self.sbm(st, "p2_onT", [128, 2, 512], BF16, 2, 2)
            S.dma("sp", pc[:], self.pcol_d[l], writes=[rpc])
            S.op("act", lambda: nc.scalar.activation(out=esink[:], in_=pc[:, PC_SINK:PC_SINK + 4], func=AF.Exp),
                 reads=[rpc], writes=[resink])
            for (qb, rqb) in QLs + QHs:
                S.op("pool", lambda: nc.gpsimd.memset(qb[:], 0.0), writes=[rqb])
            ctr = {"q": 0, "ql": 0, "qh": 0, "p": 0, "sc": 0, "ob": 0, "gn": 0, "ep": 0}

            mixers = [
                ("A", 64, 0.125, ROW_AQ, ROW_AK, 0, 1),
                ("B", 64, 0.125, ROW_BQ, ROW_BK, 4, 2),
                ("C", 96, 96.0 ** -0.5, None, None, 10, 1),
                ("D", 64, 0.125, ROW_DQ, ROW_DK, 6, 1),
            ]
            halo = {"A": (1024, 1024), "B": (128, 128), "D": (256, 256)}

            for si, (s0, sl) in enumerate(zip(self.starts, self.seqs)):
                for sp0 in range(0, sl, SPAN):
                    span = min(SPAN, sl - sp0)
                    nq = span // 512
                    ctxs = []
                    units = []
                    for mi, (mname, dqk, scale, qrow, krow, vbase, grp) in enumerate(mixers):
                        for h in range(4):
                            if mname == "C":
                                wlo, whi = 0, sl
                            else:
                                wlo, whi = max(sp0 - halo[mname][0], 0), min(sp0 + span + halo[mname][1], sl)
                            c = dict(mi=mi, mname=mname, dqk=dqk, scale=scale, qrow=qrow, krow=krow, vbase=vbase, h=h, kvh=h // grp,
                                     wlo=wlo, whi=whi, kt_base=wlo // 128, start=len(units))
                            ci = len(ctxs)
                            ctxs.append(c)
                            for qi in range(nq):
                                q0 = sp0 + qi * 512
                                if mname == "A":
                                    lo, hi = q0 - 1024, q0 + 512 + 1024
                                elif mname == "B":
                                    lo, hi = q0 - 128, q0 + 512 + 128
                                elif mname == "C":
                                    lo, hi = 0, sl
                                else:
                                    lo, hi = q0 - 256, q0 + 768
                                lo, hi = max(lo, 0), min(hi, sl)
                                kts = list(range(lo // 128, hi // 128))
                                for a in range(0, len(kts), 2):
                                    units.append((ci, qi, q0, kts[a:a + 2], a == 0, a + 2 >= len(kts)))
                    state = {}
                    dq = []

                    def load_ctx(ci):
                        c = ctxs[ci]
                        K, rK = Ks[ci % 2]
                        V, rV = Vs[ci % 2]
                        wlo, whi = c["wlo"], c["whi"]
                        wlen = whi - wlo
                        tiles_rng = range((s0 + wlo) // 512, (s0 + whi + 511) // 512)
                        if c["krow"] is not None:
                            r0 = c["krow"] + (c["kvh"] // 2) * 128
                            ksrc = self.qkT[r0:r0 + 128, s0 + wlo:s0 + whi]
                            rdep = [self.r_qk[t] for t in tiles_rng]
                            S.dma("sp", K[0:128, 0:wlen], ksrc, reads=rdep, writes=[rK])
                        else:
                            ksrc = self.kTc[c["h"], :, s0 + wlo:s0 + whi]
                            rdep = [self.r_kc[t] for t in tiles_rng]
                            S.dma("sp", K[0:c["dqk"], 0:wlen], ksrc, reads=rdep, writes=[rK])
                        S.dma("sp", V[:, 0:wlen // 128, :],
                              self.vsc[s0 + wlo:s0 + whi, c["vbase"] + c["kvh"], :].rearrange("(k p) e -> p k e", p=128),
                              reads=[self.r_v[t] for t in tiles_rng], writes=[rV])
                        M = rM = None
                        if c["mname"] != "C":
                            M, rM = Ms[ci % 2]
                            if c["mname"] == "A":
                                S.dma("sp", M[:, 0:MA_LEN], self.maska_b[c["h"]], writes=[rM])
                            elif c["mname"] == "B":
                                S.dma("sp", M[:, 0:MB_LEN], self.maskb_b[c["h"]], writes=[rM])
                            else:
                                S.dma("sp", M[:, 0:MD_LEN], self.maskd_b[c["h"]], writes=[rM])
                        c.update(K=K, rK=rK, V=V, rV=rV, M=M, rM=rM)

                    def load_q(ui):
                        ci_, qi_, q0_, _, _, _ = units[ui]
                        c = ctxs[ci_]
                        t0 = s0 + q0_
                        if c["qrow"] is not None:
                            half = c["kvh"] % 2
                            if half == 0:
                                Q, rQ = QLs[ctr["ql"] % 3]
                                ctr["ql"] += 1
                            else:
                                Q, rQ = QHs[ctr["qh"] % 3]
                                ctr["qh"] += 1
                            r0 = c["qrow"] + c["h"] * 64
                            qsrc = self.qkT[r0:r0 + 64, t0:t0 + 512]
                            rd = [self.r_qk[t0 // 512]]
                            S.dma("sp", Q[half * 64:half * 64 + 64, :], qsrc, reads=rd, writes=[rQ])
                        else:
                            Q, rQ = Qs[ctr["q"] % 3]
                            ctr["q"] += 1
                            qsrc = self.qTc[c["h"], :, t0:t0 + 512]
                            rd = [self.r_qc[t0 // 512]]
                            S.dma("sp", Q[0:c["dqk"], :], qsrc, reads=rd, writes=[rQ])
                        state[("q", ci_, qi_)] = (Q, rQ)

                    def mask_ap(c, q0, kt):
                        dlt = kt * 128 - q0
                        if c["mname"] == "A":
                            off = 1408 - dlt
                        elif c["mname"] == "B":
                            off = 512 - dlt
                        else:
                            rows = sl // 64
                            R0 = q0 // 64
                            if R0 == 0:
                                off = MD_ARR + (dlt // 128) * 512
                            elif R0 == rows - 8:
                                off = MD_ARR + (6 + (dlt + 256) // 128) * 512
                            else:
                                off = 640 - dlt
                        return c["M"][:, off:off + 512]

                    def emit_S(i):
                        ci, qi, q0, kts, first, last = units[i]
                        c = ctxs[ci]
                        nk = len(kts)
                        dqk, K, rK, kb = c["dqk"], c["K"], c["rK"], c["kt_base"]
                        if c["qrow"] is not None:
                            dqk = 128
                        if first:
                            k = firsts.index(i)
                            if k + 2 < len(firsts):
                                load_q(firsts[k + 2])
                        Q, rQ = state[("q", ci, qi)]
                        sc, rsc = scs[ctr["sc"] % 2]
                        ctr["sc"] += 1
                        for a, kt in enumerate(kts):
                            S.op("pe", lambda: nc.tensor.matmul(sc[:, a * 512:(a + 1) * 512], K[0:dqk, (kt - kb) * 128:(kt - kb + 1) * 128], Q[0:dqk, :],
                                                                start=True, stop=True),
                                 reads=[rK, rQ], writes=[rsc], signal=(a == nk - 1))
                        P, rP = Ps[ctr["p"] % NP]
                        S.op("act", lambda: nc.scalar.activation(out=P[:, 0:nk * 512], in_=sc[:, 0:nk * 512], func=AF.Exp, scale=c["scale"]),
                             reads=[rsc], writes=[rP])
                        if c["M"] is not None:
                            P2, (rP2, rP2b) = P2s[ctr["p"] % NP]
                            for a, kt in enumerate(kts):
                                mk = mask_ap(c, q0, kt)
                                if a == 1 and POOL_MASK:
                                    S.op("pool", lambda: nc.gpsimd.tensor_tensor(out=P2[:, a * 512:(a + 1) * 512], in0=P[:, a * 512:(a + 1) * 512],
                                                                                 in1=mk, op=ALU.mult),
                                         reads=[rP, c["rM"]], writes=[rP2b])
                                else:
                                    S.op("dve", lambda: nc.vector.tensor_tensor(out=P2[:, a * 512:(a + 1) * 512], in0=P[:, a * 512:(a + 1) * 512],
                                                                                in1=mk, op=ALU.mult),
                                         reads=[rP, c["rM"]], writes=[rP2])
                            state[("p", i)] = (P2, [rP2, rP2b])
                        else:
                            state[("p", i)] = (P, [rP, rP])
                        ctr["p"] += 1

                    def emit_PV(j):
                        ci, qi, q0, kts, first, last = units[j]
                        c = ctxs[ci]
                        nk = len(kts)
                        V, rV, kb = c["V"], c["rV"], c["kt_base"]
                        P, rP = state.pop(("p", j))
                        if first:
                            ob, rob = o_banks[ctr["ob"] % 2]
                            ctr["ob"] += 1
                            state[("o", ci, qi)] = (ob, rob)
                        ob, rob = state[("o", ci, qi)]
                        for a, kt in enumerate(kts):
                            S.op("pe", lambda: nc.tensor.matmul(ob[0:65, :], V[:, kt - kb, :], P[:, a * 512:(a + 1) * 512],
                                                                start=(first and a == 0), stop=(last and a == nk - 1)),
                                 reads=[rV, rP[a]], writes=[rob], signal=(a == nk - 1))
                        if last:
                            e = ctr["ep"] % 3
                            ctr["ep"] += 1
                            osb, rosb = osbs[e]
                            S.op("dve", lambda: nc.vector.tensor_copy(out=osb[:, :], in_=ob[0:65, :]), reads=[rob], writes=[rosb])
                            dq.append([EP_DEFER, lambda: emit_epilogue(ci, qi, e)])

                    def emit_epilogue(ci, qi, e):
                        c = ctxs[ci]
                        h, mi = c["h"], c["mi"]
                        osb, rosb = osbs[e]
                        rec, rrec = recs[e]
                        ogrp, rogrp = ogrps[mi % 2]
                        tp, rtp = tp_bank
                        for sub in range(4):
                            S.op("pe", lambda: nc.tensor.transpose(out=tp[:, sub * 65:(sub + 1) * 65], in_=osb[0:65, sub * 128:(sub + 1) * 128],
                                                                   identity=self.identf[0:65, 0:65]),
                                 reads=[rosb, self.r_identf], writes=[rtp], signal=(sub == 3))
                        den = tp[:, 0:260].rearrange("p (s e) -> p s e", e=65)[:, :, 64]
                        if c["mname"] == "B":
                            S.op("dve", lambda: nc.vector.tensor_scalar(out=rec[:, 0:4], in0=den, scalar1=esink[:, h:h + 1], scalar2=None, op0=ALU.add),
                                 reads=[rtp, resink], writes=[rrec])
                            S.op("dve", lambda: nc.vector.reciprocal(out=rec[:, 0:4], in_=rec[:, 0:4]), reads=[rrec], writes=[rrec])
                        else:
                            S.op("dve", lambda: nc.vector.reciprocal(out=rec[:, 0:4], in_=den), reads=[rtp], writes=[rrec])
                        for sub in range(4):
                            S.op("dve", lambda: nc.vector.tensor_scalar(out=ogrp[:, qi * 4 + sub, h * 64:(h + 1) * 64],
                                                                        in0=tp[:, sub * 65:sub * 65 + 64], scalar1=rec[:, sub:sub + 1],
                                                                        scalar2=None, op0=ALU.mult),
                                 reads=[rtp, rrec], writes=[rogrp[qi]])
                        if h == 3:
                            emit_gn_a(mi, qi)

                    def emit_gn_a(mi, qi):
                        g = ctr["gn"] % 2
                        ctr["gn"] += 1
                        ogrp, rogrp = ogrps[mi % 2]
                        gs, rgs = gss[g]
                        gr, rgr = grs[g]
                        on, ron = ons[g]
                        for sub in range(4):
                            S.op("act", lambda: nc.scalar.activation(out=junk[:, :], in_=ogrp[:, qi * 4 + sub, :], func=AF.Square, scale=1.0 / 16.0,
                                                                     accum_out=gs[:, sub:sub + 1]),
                                 reads=[rogrp[qi]], writes=[rgs])
                        self.rstd_from_acc(gs, rgs, gr, rgr, 0, 4)
                        for sub in range(4):
                            S.op("dve", lambda: nc.vector.tensor_scalar(out=on[:, sub, :], in0=ogrp[:, qi * 4 + sub, :], scalar1=gr[:, sub:sub + 1],
                                                                        scalar2=None, op0=ALU.mult),
                                 reads=[rogrp[qi], rgr], writes=[ron[sub]])
                        dq.append([GN_DEFER, lambda: emit_gn_b(mi, qi, g)])

                    def emit_gn_b(mi, qi, g):
                        t0 = s0 + sp0 + qi * 512
                        on, ron = ons[g]
                        onT, ronT = onTs[g]
                        for cc in range(2):
                            pb, rpb = self.next_pb()
                            for sub in range(4):
                                S.op("pe", lambda: nc.tensor.transpose(out=pb[:, sub * 128:(sub + 1) * 128], in_=on[:, sub, cc * 128:(cc + 1) * 128],
                                                                       identity=self.identb[:]),
                                     reads=[ron[sub], self.r_identb], writes=[rpb], signal=(sub == 3))
                            gc = PC_GN + mi * 2 + cc
                            S.op("dve", lambda: nc.vector.tensor_scalar(out=onT[:, cc, :], in0=pb[:, 0:512], scalar1=pc[:, gc:gc + 1], scalar2=None, op0=ALU.mult),
                                 reads=[rpb, rpc], writes=[ronT[cc]])
                        S.dma("pool", self.onT[mi * 256:(mi + 1) * 256, t0:t0 + 512].rearrange("(c p) t -> p c t", p=128), onT[:, :, :],
                              reads=ronT, writes=[self.r_on[t0 // 512]])

                    def tick():
                        ready = [f for (cnt, f) in dq if cnt <= 0]
                        rest = [[cnt - 1, f] for (cnt, f) in dq if cnt > 0]
                        dq[:] = rest
                        for f in ready:
                            f()

                    n = len(units)
                    firsts = [ui for ui, u in enumerate(units) if u[4]]
                    load_ctx(0)
                    for ui in firsts[0:2]:
                        load_q(ui)
                    for i in range(n + LOOK):
                        if i < n:
                            ci = units[i][0]
                            if i == ctxs[ci]["start"] + LOOK and ci + 1 < len(ctxs):
                                load_ctx(ci + 1)
                            emit_S(i)
                        tick()
                        if i - LOOK >= 0:
                            emit_PV(i - LOOK)
                    while dq:
                        tick()
            S.barrier()

    def phase3(self, l):
        nc, S = self.nc, self.S
        last = (l == self.L - 1)
        x_src = self.x_in if l == 0 else self.xres
        with contextlib.ExitStack() as st:
            self.alloc_psum(st, 6, 2)
            (wo, rwo) = self.sb(st, "p3_wo", [128, 8, 1024], BF16)[0]
            (wd, rwd) = self.sb(st, "p3_wd", [128, 32, 1024], BF16)[0]
            wus = self.sb(st, "p3_wu", [128, 8, 512], BF16, 2)
            (pc, rpc) = self.sb(st, "p3_pc", [128, NPCOL], F32)[0]
            (nf, rnf) = self.sb(st, "p3_nf", [128, 1024], F32)[0]
            xts = self.sbm(st, "p3_xt", [128, 4, 1024], F32, 2, 4)
            (xn, rxn) = self.sbm(st, "p3_xn", [128, 4, 1024], BF16, 1, 4)[0]
            onTs = self.sb(st, "p3_onT", [128, 8, 512], BF16, 2)
            (hT, rhT) = self.sbm(st, "p3_hT", [128, 8, 512], BF16, 1, 8)[0]
            (uT, ruT) = self.sbm(st, "p3_uT", [128, 32, 512], BF16, 1, 32)[0]
            rls = self.sb(st, "p3_rl", [128, 512], BF16, 3)
            junk = None
            sss = self.sb(st, "p3_ss", [128, 8], F32, 2)
            rss_ = self.sb(st, "p3_rs", [128, 8], F32, 2)
            S.dma("sp", wo[:], self.wout_b[l], writes=[rwo])
            S.dma("sp", pc[:], self.pcol_d[l], writes=[rpc])
            S.dma("sp", nf[:], self.nfin_d[:, :], writes=[rnf])
            S.dma("sp", wd[:], self.wdown_b[l], writes=[rwd])
            tl = self.tiles()
            wu_ctr = [0]

            def load(i):
                si, s0, t0 = tl[i]
                xt, rxt = xts[i % 2]
                oT, roT = onTs[i % 2]
                S.dma("sp", xt[:], x_src[t0:t0 + 512, :].rearrange("(s p) d -> p s d", p=128),
                      reads=[self.r_x[t0 // 512]], writes=rxt)
                S.dma("sp", oT[:], self.onT[:, t0:t0 + 512].rearrange("(c p) t -> p c t", p=128),
                      reads=[self.r_on[t0 // 512]], writes=[roT])

            def load_wu(g):
                wu, rwu = wus[wu_ctr[0] % 2]
                wu_ctr[0] += 1
                S.dma("sp", wu[:], self.wup_b[l, g], writes=[rwu])
                return wu, rwu

            def outproj_front(i):
                xt, rxt = xts[i % 2]
                oT, roT = onTs[i % 2]
                ss, rss = sss[i % 2]
                rs, rrs = rss_[i % 2]
                for sub in range(4):
                    for half in range(2):
                        ps, rps = self.next_ps()
                        for c in range(8):
                            S.op("pe", lambda: nc.tensor.matmul(ps[:, :], oT[:, c, sub * 128:(sub + 1) * 128], wo[:, c, half * 512:(half + 1) * 512],
                                                                start=(c == 0), stop=(c == 7)),
                                 reads=[roT, rwo], writes=[rps], signal=(c == 7))
                        S.op("dve", lambda: nc.vector.tensor_tensor(out=xt[:, sub, half * 512:(half + 1) * 512], in0=ps[:, :],
                                                                    in1=xt[:, sub, half * 512:(half + 1) * 512], op=ALU.add),
                             reads=[rps, rxt[sub]], writes=[rxt[sub]])
                self.norm_front(xt, rxt, xn, rxn, ss, rss, rs, rrs)

            load(0)
            for i, (si, s0, t0) in enumerate(tl):
                ti = t0 // 512
                xt, rxt = xts[i % 2]
                ss, rss = sss[i % 2]
                rs, rrs = rss_[i % 2]
                wu_next = load_wu(0)
                outproj_front(i)
                if i + 1 < len(tl):
                    load(i + 1)
                self.norm_tpose(xn, rxn, hT, rhT, pc[:, PC_NMLP:PC_NMLP + 8], rpc)
                for g in range(8):
                    wu, rwu = wu_next
                    if g + 1 < 8:
                        wu_next = load_wu(g + 1)
                    for f in range(4):
                        fc = g * 4 + f
                        ps, rps = self.next_ps()
                        for c in range(8):
                            S.op("pe", lambda: nc.tensor.matmul(ps[:, :], wu[:, c, f * 128:(f + 1) * 128], hT[:, c, :],
                                                                start=(c == 0), stop=(c == 7)),
                                 reads=[rwu, rhT[c]], writes=[rps], signal=(c == 7))
                        rl, rrl = rls[fc % 3]
                        S.op("act", lambda: nc.scalar.activation(out=rl[:, :], in_=ps[:, :], func=AF.Relu), reads=[rps], writes=[rrl])
                        S.op("dve", lambda: nc.vector.tensor_tensor(out=uT[:, fc, :], in0=rl[:, :], in1=rl[:, :], op=ALU.mult),
                             reads=[rrl], writes=[ruT[fc]])
                for sub in range(4):
                    for half in range(2):
                        ps, rps = self.next_ps()
                        for fc in range(32):
                            S.op("pe", lambda: nc.tensor.matmul(ps[:, :], uT[:, fc, sub * 128:(sub + 1) * 128], wd[:, fc, half * 512:(half + 1) * 512],
                                                                start=(fc == 0), stop=(fc == 31)),
                                 reads=[ruT[fc], rwd], writes=[rps], signal=(fc == 31))
                        S.op("dve", lambda: nc.vector.tensor_tensor(out=xt[:, sub, half * 512:(half + 1) * 512], in0=ps[:, :],
                                                                    in1=xt[:, sub, half * 512:(half + 1) * 512], op=ALU.add),
                             reads=[rps, rxt[sub]], writes=[rxt[sub]])
                if not last:
                    S.dma("sp", self.xres[t0:t0 + 512, :].rearrange("(s p) d -> p s d", p=128), xt[:],
                          reads=rxt, writes=[self.r_x[ti]])
                else:
                    for sub in range(4):
                        S.op("act", lambda: nc.scalar.activation(out=uT[:, 2 * sub:2 * sub + 2, :], in_=xt[:, sub, :].rearrange("p (a b) -> p a b", b=512),
                                                                 func=AF.Square, scale=1.0 / 32.0, accum_out=ss[:, 4 + sub:5 + sub]),
                             reads=[rxt[sub]], writes=[rss, ruT[2 * sub], ruT[2 * sub + 1]])
                    self.rstd_from_acc(ss, rss, rs, rrs, 4, 4)
                    for sub in range(4):
                        S.op("dve", lambda: nc.vector.scalar_tensor_tensor(out=xt[:, sub, :], in0=xt[:, sub, :], scalar=rs[:, 4 + sub:5 + sub],
                                                                           in1=nf[:, :], op0=ALU.mult, op1=ALU.mult),
                             reads=[rxt[sub], rrs, rnf], writes=[rxt[sub]])
                    S.dma("sp", self.y_out[t0:t0 + 512, :].rearrange("(s p) d -> p s d", p=128), xt[:], reads=rxt)
            S.barrier()


_PROG_CACHE = {}


def get_prog(seqs, depth):
    key = (tuple(seqs), depth)
    if key not in _PROG_CACHE:
        p = Prog(seqs, depth)
        p.build()
        _PROG_CACHE[key] = p
    return _PROG_CACHE[key]


def run_cores(x_list, shared, seqs, depth, core_ids):
    prog = get_prog(seqs, depth)
    shared = dict(shared)
    shared["rope"] = _rope_tables(max(seqs))
    in_maps = []
    for x in x_list:
        m = dict(shared)
        m["x"] = np.ascontiguousarray(x, dtype=np.float32)
        in_maps.append(m)
    res = run_bass_kernel_spmd(prog.nc, in_maps, core_ids=core_ids)
    return [r["y"] for r in res.results]


def kernel(**inputs):
    xp = np.asarray(inputs["x_prompt"], np.float32)
    xs = np.asarray(inputs["x_sample"], np.float32)
    shared = host_prep(inputs, DEPTH)
    seqs = [2048, 2048, 16384]
    x_list = []
    zeros = np.zeros_like(xs[0])
    for c in range(N_CORES):
        samp = xs[c // 4] if c % 4 == 0 else zeros
        x_list.append(np.concatenate([xp[2 * c], xp[2 * c + 1], samp], axis=0))
    ys = run_cores(x_list, shared, seqs, DEPTH, list(range(N_CORES)))
    y_prompt = np.empty_like(xp)
    y_sample = np.empty_like(xs)
    for c in range(N_CORES):
        y_prompt[2 * c] = ys[c][0:2048]
        y_prompt[2 * c + 1] = ys[c][2048:4096]
    y_sample[0] = ys[0][4096:]
    y_sample[1] = ys[4][4096:]
    return (y_prompt, y_sample)
```
